# Optimizing a Trainium2 kernel written in Bass

```python
import math
import jax, jax.numpy as jnp
from jax import lax
import numpy as np


D_MODEL = 1024
BATCH = 2
SEQ = 8192
DEPTH = 2
DEC_BATCH = 128
DEC_SEQ = 1
PAST_LEN = 2048
PAGE_SIZE = 128

D_A = 512
CONV_A = 31
N_HEADS_B = 8
HEAD_DIM = 64
D_B = N_HEADS_B * HEAD_DIM
MOBA_BLOCK = 256
MOBA_TOPK = 3
Q_CHUNK = 128
NUM_BUCKETS = 32
MAX_DISTANCE = 128
D_C = 1024
CONV_C = 3
D_FF = 2816
CONV_F = 3
N_EVEN = (DEPTH + 1) // 2
N_ODD = DEPTH // 2
EPS = 1e-6

kernel_name = 'hybrid_conformer_moba_shortconv_decode_step'


def rmsnorm(x, g):
    xf = x.astype(jnp.float32)
    y = xf * lax.rsqrt(jnp.mean(xf * xf, axis=-1, keepdims=True) + EPS)
    return (y * g.astype(jnp.float32)).astype(x.dtype)


def layernorm(x, g, b):
    xf = x.astype(jnp.float32)
    mu = jnp.mean(xf, axis=-1, keepdims=True)
    var = jnp.mean(jnp.square(xf - mu), axis=-1, keepdims=True)
    y = (xf - mu) * lax.rsqrt(var + EPS)
    return (y * g.astype(jnp.float32) + b.astype(jnp.float32)).astype(x.dtype)


def causal_dwconv(x_ext, w):
    c = x_ext.shape[-1]
    return lax.conv_general_dilated(x_ext, w.astype(x_ext.dtype)[:, None, :], window_strides=(1,), padding='VALID',
                                    dimension_numbers=('NWC', 'WIO', 'NWC'), feature_group_count=c)


def t5_bucket(dist):
    n = jnp.maximum(dist, 0)
    max_exact = NUM_BUCKETS // 2
    nf = jnp.maximum(n, 1).astype(jnp.float32)
    large = max_exact + (jnp.log(nf / max_exact) / math.log(MAX_DISTANCE / max_exact)
                         * (NUM_BUCKETS - max_exact)).astype(jnp.int32)
    large = jnp.minimum(large, NUM_BUCKETS - 1)
    return jnp.where(n < max_exact, n, large)


def moba_select_attend(q, q_pos, kb, vb, k_mean, rel_bias):
    t = q.shape[1]
    nb = kb.shape[2]
    tk = min(MOBA_TOPK, nb)
    own = q_pos // MOBA_BLOCK
    gate = jnp.einsum('bthd,bhnd->bhtn', q.astype(jnp.float32), k_mean)
    fully_past = jnp.arange(nb, dtype=jnp.int32)[None, :] < own[:, None]
    gate = jnp.where(fully_past, gate, -jnp.inf)
    _, top = lax.top_k(gate, tk)
    own_b = jnp.broadcast_to(own[None, None, :, None], top.shape[:3] + (1,))
    sel = jnp.concatenate([top.astype(jnp.int32), own_b.astype(jnp.int32)], axis=-1)
    slot_ok = jnp.concatenate([jnp.arange(tk, dtype=jnp.int32)[None, :] < own[:, None],
                               jnp.ones((t, 1), dtype=bool)], axis=-1)
    gather = jax.vmap(jax.vmap(lambda blocks, idx: blocks[idx]))
    k_sel = gather(kb, sel)
    v_sel = gather(vb, sel)
    key_pos = sel[..., None] * MOBA_BLOCK + jnp.arange(MOBA_BLOCK, dtype=jnp.int32)
    dist = q_pos[:, None, None] - key_pos
    mask = slot_ok[:, :, None] & (dist >= 0)
    bias = rel_bias.T.astype(jnp.float32)[jnp.arange(N_HEADS_B)[:, None, None, None], t5_bucket(dist)]
    logits = jnp.einsum('bthd,bhtsjd->bhtsj', q, k_sel, preferred_element_type=jnp.float32) * (HEAD_DIM ** -0.5) + bias
    logits = jnp.where(mask, logits, -jnp.inf)
    b, h, _, s, blk = logits.shape
    probs = jax.nn.softmax(logits.reshape(b, h, t, s * blk), axis=-1).reshape(b, h, t, s, blk)
    return jnp.einsum('bhtsj,bhtsjd->bthd', probs.astype(v_sel.dtype), v_sel)


def moba_attention(q, k_new, v_new, past_k, past_v, rel_bias):
    b, t = q.shape[:2]
    p = past_k.shape[1]
    l = p + t
    l_pad = -(-l // MOBA_BLOCK) * MOBA_BLOCK
    pad = ((0, 0), (0, l_pad - l), (0, 0), (0, 0))
    k_all = jnp.pad(jnp.concatenate([past_k, k_new], axis=1), pad)
    v_all = jnp.pad(jnp.concatenate([past_v, v_new], axis=1), pad)
    nb = l_pad // MOBA_BLOCK
    kb = k_all.reshape(b, nb, MOBA_BLOCK, N_HEADS_B, HEAD_DIM).transpose(0, 3, 1, 2, 4)
    vb = v_all.reshape(b, nb, MOBA_BLOCK, N_HEADS_B, HEAD_DIM).transpose(0, 3, 1, 2, 4)
    k_mean = jnp.mean(kb.astype(jnp.float32), axis=3)
    q_pos = p + jnp.arange(t, dtype=jnp.int32)
    if t > Q_CHUNK and t % Q_CHUNK == 0:
        nc = t // Q_CHUNK
        qc = q.reshape(b, nc, Q_CHUNK, N_HEADS_B, HEAD_DIM).transpose(1, 0, 2, 3, 4)
        pc = q_pos.reshape(nc, Q_CHUNK)
        out = lax.map(lambda qp: moba_select_attend(qp[0], qp[1], kb, vb, k_mean, rel_bias), (qc, pc))
        return out.transpose(1, 0, 2, 3, 4).reshape(b, t, N_HEADS_B, HEAD_DIM)
    return moba_select_attend(q, q_pos, kb, vb, k_mean, rel_bias)


def even_mixer(h, prefix_a, past_k, past_v, rel_bias, w_in, conv_w, conv_b, ln_g, ln_b, q_g, k_g, w_out):
    b, t, _ = h.shape
    z = h @ w_in
    a_val, a_gate, q, k, v = jnp.split(z, [D_A, 2 * D_A, 2 * D_A + D_B, 2 * D_A + 2 * D_B], axis=-1)
    a = a_val * jax.nn.sigmoid(a_gate)
    a_ext = jnp.concatenate([prefix_a, a], axis=1)
    a_out = jax.nn.silu(layernorm(causal_dwconv(a_ext, conv_w) + conv_b, ln_g, ln_b))
    q = rmsnorm(q.reshape(b, t, N_HEADS_B, HEAD_DIM), q_g)
    k = rmsnorm(k.reshape(b, t, N_HEADS_B, HEAD_DIM), k_g)
    v = v.reshape(b, t, N_HEADS_B, HEAD_DIM)
    o = moba_attention(q, k, v, past_k, past_v, rel_bias)
    y = jnp.concatenate([a_out, o.reshape(b, t, D_B)], axis=-1) @ w_out
    return y, a_ext[:, -(CONV_A - 1):], k, v


def odd_mixer(h, prefix_c, w_in, conv_w, w_out):
    g_b, g_c, u = jnp.split(h @ w_in, 3, axis=-1)
    ext = jnp.concatenate([prefix_c, g_c * u], axis=1)
    y = (g_b * causal_dwconv(ext, conv_w)) @ w_out
    return y, ext[:, -(CONV_C - 1):]


def conv_ffn(h, prefix_f, w_up, conv_w, conv_b, w_down):
    ext = jnp.concatenate([prefix_f, h @ w_up], axis=1)
    g, u = jnp.split(causal_dwconv(ext, conv_w) + conv_b, 2, axis=-1)
    return (jax.nn.silu(g) * u) @ w_down, ext[:, -(CONV_F - 1):]


def run_trunk(x, past_k, past_v, pre_a, pre_c, pre_f, rel_bias, w_even, w_odd, w_ffn):
    norm_mix_e, w_in_e, conv_a_w, conv_a_b, ln_a_g, ln_a_b, q_norm_g, k_norm_g, w_out_e = w_even
    norm_mix_o, w_in_o, conv_c_w, w_out_o = w_odd
    norm_ffn, w_up, conv_f_w, conv_f_b, w_down = w_ffn
    new_k, new_v, new_a, new_c, new_f = [], [], [], [], []
    for layer in range(DEPTH):
        i = layer // 2
        if layer % 2 == 0:
            y, sa, k, v = even_mixer(rmsnorm(x, norm_mix_e[i]), pre_a[i], past_k[i], past_v[i], rel_bias,
                                     w_in_e[i], conv_a_w[i], conv_a_b[i], ln_a_g[i], ln_a_b[i],
                                     q_norm_g[i], k_norm_g[i], w_out_e[i])
            new_a.append(sa)
            new_k.append(k)
            new_v.append(v)
        else:
            y, sc = odd_mixer(rmsnorm(x, norm_mix_o[i]), pre_c[i], w_in_o[i], conv_c_w[i], w_out_o[i])
            new_c.append(sc)
        x = x + y
        y, sf = conv_ffn(rmsnorm(x, norm_ffn[layer]), pre_f[layer], w_up[layer], conv_f_w[layer],
                         conv_f_b[layer], w_down[layer])
        new_f.append(sf)
        x = x + y
    return x, jnp.stack(new_k), jnp.stack(new_v), jnp.stack(new_a), jnp.stack(new_c), jnp.stack(new_f)


def setup_inputs(seed: int = 0) -> dict:
    key = jax.random.key(seed)
    ks = jax.random.split(key, 32)
    f32 = jnp.float32
    n_pages = PAST_LEN // PAGE_SIZE
    n_used = DEC_BATCH * n_pages
    n_pool = (n_used * 5) // 4

    def nrm(k, shape, scale):
        return scale * jax.random.normal(k, shape, f32)

    def gain(k, shape):
        return 1.0 + 0.01 * jax.random.normal(k, shape, f32)

    page_table = jax.random.permutation(ks[0], n_pool)[:n_used].reshape(DEC_BATCH, n_pages).astype(jnp.int32)
    return {
        'x_prompt': nrm(ks[1], (BATCH, SEQ, D_MODEL), 1.0),
        'x_sample': nrm(ks[2], (DEC_BATCH, DEC_SEQ, D_MODEL), 1.0),
        'cache_k': nrm(ks[3], (N_EVEN, n_pool, PAGE_SIZE, N_HEADS_B, HEAD_DIM), 1.0),
        'cache_v': nrm(ks[4], (N_EVEN, n_pool, PAGE_SIZE, N_HEADS_B, HEAD_DIM), 1.0),
        'state_conv_a': nrm(ks[5], (N_EVEN, DEC_BATCH, CONV_A - 1, D_A), 0.5),
        'state_conv_c': nrm(ks[6], (N_ODD, DEC_BATCH, CONV_C - 1, D_C), 0.5),
        'state_ffn': nrm(ks[7], (DEPTH, DEC_BATCH, CONV_F - 1, 2 * D_FF), 1.0),
        'page_table': page_table,
        'rel_bias': nrm(ks[8], (NUM_BUCKETS, N_HEADS_B), 0.5),
        'norm_mix_e': gain(ks[9], (N_EVEN, D_MODEL)),
        'w_in_e': nrm(ks[10], (N_EVEN, D_MODEL, 2 * D_A + 3 * D_B), D_MODEL ** -0.5),
        'conv_a_w': nrm(ks[11], (N_EVEN, CONV_A, D_A), CONV_A ** -0.5),
        'conv_a_b': nrm(ks[12], (N_EVEN, D_A), 0.01),
        'ln_a_g': gain(ks[13], (N_EVEN, D_A)),
        'ln_a_b': nrm(ks[14], (N_EVEN, D_A), 0.01),
        'q_norm_g': gain(ks[15], (N_EVEN, HEAD_DIM)),
        'k_norm_g': gain(ks[16], (N_EVEN, HEAD_DIM)),
        'w_out_e': nrm(ks[17], (N_EVEN, D_A + D_B, D_MODEL), (D_A + D_B) ** -0.5),
        'norm_mix_o': gain(ks[18], (N_ODD, D_MODEL)),
        'w_in_o': nrm(ks[19], (N_ODD, D_MODEL, 3 * D_C), D_MODEL ** -0.5),
        'conv_c_w': nrm(ks[20], (N_ODD, CONV_C, D_C), CONV_C ** -0.5),
        'w_out_o': nrm(ks[21], (N_ODD, D_C, D_MODEL), D_C ** -0.5),
        'norm_ffn': gain(ks[22], (DEPTH, D_MODEL)),
        'w_up': nrm(ks[23], (DEPTH, D_MODEL, 2 * D_FF), D_MODEL ** -0.5),
        'conv_f_w': nrm(ks[24], (DEPTH, CONV_F, 2 * D_FF), CONV_F ** -0.5),
        'conv_f_b': nrm(ks[25], (DEPTH, 2 * D_FF), 0.01),
        'w_down': nrm(ks[26], (DEPTH, D_FF, D_MODEL), D_FF ** -0.5),
    }


def reference(x_prompt, x_sample, cache_k, cache_v, state_conv_a, state_conv_c, state_ffn, page_table,
              rel_bias, norm_mix_e, w_in_e, conv_a_w, conv_a_b, ln_a_g, ln_a_b, q_norm_g, k_norm_g, w_out_e,
              norm_mix_o, w_in_o, conv_c_w, w_out_o, norm_ffn, w_up, conv_f_w, conv_f_b, w_down):
    w_even = (norm_mix_e, w_in_e, conv_a_w, conv_a_b, ln_a_g, ln_a_b, q_norm_g, k_norm_g, w_out_e)
    w_odd = (norm_mix_o, w_in_o, conv_c_w, w_out_o)
    w_ffn = (norm_ffn, w_up, conv_f_w, conv_f_b, w_down)
    n_ev, _, page, nh, hd = cache_k.shape

    b, t = x_prompt.shape[:2]
    dt = x_prompt.dtype
    empty = jnp.zeros((n_ev, b, 0, nh, hd), dt)
    pre_a = jnp.zeros((n_ev, b, CONV_A - 1, D_A), dt)
    pre_c = jnp.zeros((state_conv_c.shape[0], b, CONV_C - 1, D_C), dt)
    pre_f = jnp.zeros((DEPTH, b, CONV_F - 1, 2 * D_FF), dt)
    y_prompt, k_p, v_p, a_p, c_p, f_p = run_trunk(x_prompt, empty, empty, pre_a, pre_c, pre_f, rel_bias,
                                                  w_even, w_odd, w_ffn)
    k_prompt = k_p.reshape(n_ev, b, t // page, page, nh, hd)
    v_prompt = v_p.reshape(n_ev, b, t // page, page, nh, hd)

    bs, n_pages = page_table.shape
    past_k = cache_k[:, page_table].reshape(n_ev, bs, n_pages * page, nh, hd)
    past_v = cache_v[:, page_table].reshape(n_ev, bs, n_pages * page, nh, hd)
    y_sample, k_s, v_s, a_s, c_s, f_s = run_trunk(x_sample, past_k, past_v, state_conv_a, state_conv_c, state_ffn,
                                                  rel_bias, w_even, w_odd, w_ffn)
    return (y_prompt, y_sample, k_prompt, v_prompt, a_p, c_p, f_p, k_s, v_s, a_s, c_s, f_s)
```

```python
import math
import os
from contextlib import ExitStack
import numpy as np
import concourse.bass as bass
import concourse.mybir as mybir
from concourse.bass_utils import run_bass_kernel_spmd

F32 = mybir.dt.float32
BF16 = mybir.dt.bfloat16
I32 = mybir.dt.int32
AF = mybir.ActivationFunctionType
ALU = mybir.AluOpType
AX = mybir.AxisListType

D = 1024
DA = 512
H = 8
HD = 64
DFF = 2816
NPAIR = 22
EPS = 1e-6
NEG = -30000.0
ENG = ('pe', 'act', 'dve', 'pool', 'sp')


class Sched:
    NDMA = 12

    def __init__(self, nc):
        self.nc = nc
        self.ops = []
        self.eng = {'pe': nc.tensor, 'act': nc.scalar, 'dve': nc.vector, 'pool': nc.gpsimd, 'sp': nc.sync}

    def op(self, engine, fn, reads=(), writes=(), dma=False):
        self.ops.append(dict(e=engine, fn=fn, r=tuple(reads), w=tuple(writes), dma=dma,
                             sig=False, seq=None, waits=[]))

    def emit(self, stack):
        nc = self.nc
        ops = self.ops
        last_w = {}
        readers = {}
        seqc = {e: 0 for e in ENG}
        waited = {e: {y: -1 for y in ENG} for e in ENG}
        waited_dma = {e: set() for e in ENG}
        dma_cnt = {e: 0 for e in ENG}
        for i, o in enumerate(ops):
            e = o['e']
            deps = set()
            for k in o['r']:
                if k in last_w:
                    deps.add(last_w[k])
            for k in o['w']:
                if k in last_w:
                    deps.add(last_w[k])
                for j in readers.get(k, ()):
                    deps.add(j)
            deps.discard(i)
            need = {}
            for j in deps:
                oj = ops[j]
                if oj['dma']:
                    if j not in waited_dma[e]:
                        waited_dma[e].add(j)
                        o['waits'].append(('dma', j))
                else:
                    y = oj['e']
                    if y == 'pe' and e == 'pe':
                        continue
                    if oj['seq'] > waited[e][y]:
                        need[y] = max(need.get(y, -1), oj['seq'])
            for y, s in need.items():
                waited[e][y] = s
                o['waits'].append(('cmp', y, s))
            if o['dma']:
                n = dma_cnt[e]
                dma_cnt[e] += 1
                o['dsem'] = (e, n % self.NDMA)
                o['dval'] = 16 * (n // self.NDMA + 1)
            else:
                o['seq'] = seqc[e]
                seqc[e] += 1
            for k in o['r']:
                readers.setdefault(k, []).append(i)
            for k in o['w']:
                last_w[k] = i
                readers[k] = []
        by_seq = {e: {} for e in ENG}
        for i, o in enumerate(ops):
            if not o['dma']:
                by_seq[o['e']][o['seq']] = i
        for o in ops:
            for w in o['waits']:
                if w[0] == 'cmp':
                    ops[by_seq[w[1]][w[2]]]['sig'] = True
        cnt = {e: 0 for e in ENG}
        for o in ops:
            if not o['dma'] and o['sig']:
                cnt[o['e']] += 1
                o['cnt'] = cnt[o['e']]
        self.max_counts = dict(cnt)
        csem = {e: stack.enter_context(nc.semaphore('c_' + e)) for e in ENG}
        dsem = {}
        for e in ENG:
            for n in range(min(self.NDMA, dma_cnt[e])):
                dsem[(e, n)] = stack.enter_context(nc.semaphore('d_%s_%d' % (e, n)))
        for i, o in enumerate(ops):
            e = o['e']
            eng = self.eng[e]
            for w in o['waits']:
                if w[0] == 'dma':
                    oj = ops[w[1]]
                    eng.wait_ge(dsem[oj['dsem']], oj['dval'])
                else:
                    oj = ops[by_seq[w[1]][w[2]]]
                    eng.wait_ge(csem[w[1]], oj['cnt'])
            if o['dma']:
                if o['dval'] > 16:
                    eng.wait_ge(dsem[o['dsem']], o['dval'] - 16)
                ins = o['fn'](eng)
                ins.then_inc(dsem[o['dsem']], 16)
            else:
                ins = o['fn'](eng)
                if o['sig']:
                    ins.then_inc(csem[e], 1)
        last_dma = {}
        for o in ops:
            if o['dma']:
                last_dma[o['dsem']] = o['dval']
        for k, v in last_dma.items():
            nc.sync.wait_ge(dsem[k], v)


def t5_bucket_np(n):
    n = np.maximum(n, 0)
    nf = np.maximum(n, 1).astype(np.float32)
    large = 16 + (np.log(nf / np.float32(16)) / np.float32(math.log(128 / 16)) * np.float32(16)).astype(np.int32)
    large = np.minimum(large, 31)
    return np.where(n < 16, n, large)


class Builder:
    def __init__(self, C, nsamp=16, do_sample=True, npool=2560):
        self.C = C
        self.P = 3 * C
        self.NK = 4 * C
        self.NT = self.NK // 128
        self.NB = self.NK // 256
        self.GS = min(512, C)
        self.nsamp = nsamp
        self.do_sample = do_sample
        self.npool = npool
        self.nc = bass.Bass("TRN2", target_bir_lowering=False)
        self.ukey = 0

    def din(self, name, shape, dt=F32):
        return self.nc.dram_tensor(name, list(shape), dt, kind="ExternalInput").ap()

    def dout(self, name, shape, dt=F32):
        return self.nc.dram_tensor(name, list(shape), dt, kind="ExternalOutput").ap()

    def dscr(self, name, shape, dt):
        return self.nc.dram_tensor(name, list(shape), dt, kind="Internal").ap()

    def sb(self, name, shape, dt):
        return self.st.enter_context(self.nc.sbuf_tensor(name, list(shape), dt))

    def uk(self, p='t'):
        self.ukey += 1
        return (p, self.ukey)

    def mm(self, out, lhsT, rhs, start, stop, r, w):
        self.S.op('pe', lambda e: e.matmul(out, lhsT=lhsT, rhs=rhs, start=start, stop=stop), r, w)

    def tr(self, out, in_, ident, r, w):
        self.S.op('pe', lambda e: e.transpose(out=out, in_=in_, identity=ident), r, w)

    def act(self, out, in_, func, r, w, scale=None, bias=None, accum=None):
        kw = {}
        if scale is not None:
            kw['scale'] = scale
        if bias is not None:
            kw['bias'] = bias
        if accum is not None:
            kw['accum_out'] = accum
        self.S.op('act', lambda e: e.activation(out=out, in_=in_, func=func, **kw), r, w)

    def tt(self, eng, out, in0, in1, op, r, w):
        self.S.op(eng, lambda e: e.tensor_tensor(out=out, in0=in0, in1=in1, op=op), r, w)

    def ts(self, eng, out, in0, s1, s2, op0, op1, r, w):
        if op1 is None:
            self.S.op(eng, lambda e: e.tensor_scalar(out=out, in0=in0, scalar1=s1, scalar2=None, op0=op0), r, w)
        else:
            self.S.op(eng, lambda e: e.tensor_scalar(out=out, in0=in0, scalar1=s1, scalar2=s2, op0=op0, op1=op1), r, w)

    def stt(self, out, in0, scalar, in1, op0, op1, r, w):
        self.S.op('dve', lambda e: e.scalar_tensor_tensor(out=out, in0=in0, scalar=scalar, in1=in1, op0=op0, op1=op1), r, w)

    def cp(self, eng, out, in_, r, w):
        if eng == 'act':
            self.S.op('act', lambda e: e.activation(out=out, in_=in_, func=AF.Copy), r, w)
        else:
            self.S.op(eng, lambda e: e.tensor_copy(out=out, in_=in_), r, w)

    def red(self, out, in_, op, r, w, axis=AX.X):
        self.S.op('dve', lambda e: e.tensor_reduce(out=out, in_=in_, axis=axis, op=op), r, w)

    def recip(self, out, in_, r, w):
        self.S.op('dve', lambda e: e.reciprocal(out=out, in_=in_), r, w)

    def memset(self, eng, ap, val, w):
        self.S.op(eng, lambda e: e.memset(ap, val), (), w)

    def dma(self, q, out, in_, r, w, slow=False):
        if slow:
            self.S.op(q, lambda e: e.dma_start(out=out, in_=in_, allow_slow_non_contiguous=True), r, w, dma=True)
        else:
            self.S.op(q, lambda e: e.dma_start(out=out, in_=in_), r, w, dma=True)

    def gbank(self):
        i = self.gi % 4
        self.gi += 1
        return self.ps[i], ('ps', i)

    def abank(self):
        i = 4 + self.ai % 4
        self.ai += 1
        return self.ps[i], ('ps', i)

    def sbank(self):
        i = 4 + self.si % 2
        self.si += 1
        return self.ps[i], ('ps', i)

    def obank(self):
        i = 6 + self.oi % 2
        self.oi += 1
        return self.ps[i], ('ps', i)

    def wpiece(self, loads):
        i = self.wi % self.NSLOT
        self.wi += 1
        slot = self.wsl[i]
        key = ('w', i)
        for dstf, src in loads:
            self.dma('pool', dstf(slot), src, (), [key, 'wser'])
        return slot, key

    def build(self):
        nc = self.nc
        C, P, NK, NT, NB, GS = self.C, self.P, self.NK, self.NT, self.NB, self.GS
        xk = self.din("xk", [NK, D])
        gmask_d = self.din("gmask", [NB, NB])
        hv_d = self.din("hv", [1, 1])
        oh_d = self.din("oh", [33, 384])
        w_in_e = self.din("w_in_e", [D, 2560])
        w_out_e = self.din("w_out_e", [D, D])
        w_in_o = self.din("w_in_o", [D, 3072])
        w_out_o = self.din("w_out_o", [D, D])
        w_up = self.din("w_up", [2, D, 2 * DFF])
        w_down = self.din("w_down", [2, DFF, D])
        rel_bias = self.din("rel_bias", [32, H])
        norm_mix_e = self.din("norm_mix_e", [D])
        norm_mix_o = self.din("norm_mix_o", [D])
        norm_ffn = self.din("norm_ffn", [2, D])
        conv_a_w = self.din("conv_a_w", [31, DA])
        conv_a_b = self.din("conv_a_b", [DA])
        ln_a_g = self.din("ln_a_g", [DA])
        ln_a_b = self.din("ln_a_b", [DA])
        q_norm_g = self.din("q_norm_g", [HD])
        k_norm_g = self.din("k_norm_g", [HD])
        conv_c_w = self.din("conv_c_w", [3, D])
        conv_f_w = self.din("conv_f_w", [2, 3, 2 * DFF])
        conv_f_b = self.din("conv_f_b", [2, 2 * DFF])
        y_out = self.dout("y_out", [C, D])
        k_out = self.dout("k_out", [C, 512])
        v_out = self.dout("v_out", [C, 512])
        ca_out = self.dout("ca_out", [30, DA])
        cc_out = self.dout("cc_out", [2, D])
        f_out = self.dout("f_out", [2, 2, 2 * DFF])
        if self.do_sample:
            NPOOL = self.npool
            xs_d = self.din("xs", [16, D])
            pt_d = self.din("pt", [1, 256], I32)
            ck_d = self.din("cache_k", [NPOOL * 128, 512])
            cv_d = self.din("cache_v", [NPOOL * 128, 512])
            sta_d = self.din("state_a", [16, 30, DA])
            stc_d = self.din("state_c", [16, 2, D])
            stf_d = self.din("state_f", [2, 16, 2, 2 * DFF])
            ohS_d = self.din("ohS", [33, 128])
            oh16_d = self.din("oh16", [30, 256])
            ys_out = self.dout("ys_out", [16, D])
            ks_out = self.dout("ks_out", [16, 512])
            vs_out = self.dout("vs_out", [16, 512])
            cas_out = self.dout("cas_out", [16, 30, DA])
            ccs_out = self.dout("ccs_out", [16, 2, D])
            fs_out = self.dout("fs_out", [2, 16, 2, 2 * DFF])
        qs_scr = self.dscr("qs_scr", [16, 512], F32)
        kt_scr = self.dscr("kt_scr", [H, 96, NK], BF16)
        va_scr = self.dscr("va_scr", [H, 128, NT, 128], BF16)
        fv_scr = self.dscr("fv_scr", [H, 384], F32)
        if os.environ.get("KDBG"):
            k_out = self.dscr("dbg_scr", [C, 512], F32)

        with ExitStack() as st:
            self.st = st
            self.S = Sched(nc)
            S = self.S
            self.gi = 0
            self.ai = 0
            self.si = 0
            self.oi = 0
            self.wi = 0
            self.NSLOT = 5
            self.ps = [st.enter_context(nc.psum_tensor("ps%d" % i, [128, 512], F32)) for i in range(8)]
            self.wsl = [self.sb("wslot%d" % i, [128, 4096], BF16) for i in range(self.NSLOT)]
            identf = self.sb("identf", [128, 128], F32)
            ident = self.sb("ident", [128, 128], BF16)
            onesf = self.sb("onesf", [128, 128], F32)
            eps_t = self.sb("eps_t", [128, 1], F32)
            self.memset('pool', identf[:], 0.0, ['identf'])
            S.op('pool', lambda e: e.affine_select(out=identf[:], in_=identf[:], pattern=[[-1, 128]], compare_op=ALU.not_equal,
                                                   fill=1.0, base=0, channel_multiplier=1), ['identf'], ['identf'])
            self.cp('dve', ident[:], identf[:], ['identf'], ['ident'])
            self.memset('pool', onesf[:], 1.0, ['onesf'])
            self.memset('pool', eps_t[:], EPS, ['eps_t'])
            self.ident, self.identf, self.onesf, self.eps_t = ident, identf, onesf, eps_t

            def colload(name, src_ap, shape):
                t = self.sb(name, shape, F32)
                self.dma('act', t[:], src_ap, (), [name], slow=True)
                return t
            def colload2(name, shape, parts):
                t = self.sb(name, shape, F32)
                for dst_fn, src in parts:
                    self.dma('act', dst_fn(t), src, (), [name], slow=True)
                return t
            gE = colload("gE", norm_mix_e.rearrange("(c p) -> p c", p=128), [128, 8])
            gO = colload("gO", norm_mix_o.rearrange("(c p) -> p c", p=128), [128, 8])
            gF = colload2("gF", [128, 2, 8], [(lambda t, l=l: t[:, l, :], norm_ffn[l].rearrange("(c p) -> p c", p=128)) for l in range(2)])
            caw = colload2("caw", [128, 4, 31], [(lambda t, c=c: t[:, c, :], conv_a_w[:, c * 128:(c + 1) * 128].rearrange("j p -> p j"))
                                                  for c in range(4)])
            cab = colload("cab", conv_a_b.rearrange("(c p) -> p c", p=128), [128, 4])
            lng = colload("lng", ln_a_g.rearrange("(c p) -> p c", p=128), [128, 4])
            lnb = colload("lnb", ln_a_b.rearrange("(c p) -> p c", p=128), [128, 4])
            ccw = colload2("ccw", [128, 8, 3], [(lambda t, j=j: t[:, :, j], conv_c_w[j].rearrange("(c p) -> p c", p=128)) for j in range(3)])
            cfw = colload2("cfw", [128, 2, 44, 3], [(lambda t, l=l, j=j: t[:, l, :, j], conv_f_w[l, j].rearrange("(c p) -> p c", p=128))
                                                     for l in range(2) for j in range(3)])
            cfb = colload2("cfb", [128, 2, 44], [(lambda t, l=l: t[:, l, :], conv_f_b[l].rearrange("(c p) -> p c", p=128)) for l in range(2)])
            gqB = colload("gqB", q_norm_g.rearrange("(o d) -> o d", o=1).partition_broadcast(128), [128, 1, HD])
            gkB = colload("gkB", k_norm_g.rearrange("(o d) -> o d", o=1).partition_broadcast(128), [128, 1, HD])
            b31B = colload("b31B", rel_bias[31:32, :].partition_broadcast(128), [128, 1, H])
            gmask = self.sb("gmaskt", [128, 1, NB, NB], BF16)
            self.dma('pool', gmask[:], gmask_d.rearrange("(o a) b -> o a b", o=1).partition_broadcast(128), (), ['gmaskt'])
            hv = colload("hvt", hv_d.partition_broadcast(128), [128, 1, 1])
            scB = self.sb("scB", [128, H], F32)
            self.ts('dve', scB[:], b31B[:, 0, :], -NEG, None, ALU.add, None, ['b31B'], ['scB'])

            rb = self.sb("rb", [33, H], F32)
            rb31 = self.sb("rb31", [33, 1, H], F32)
            ohs = self.sb("ohs", [33, 384], F32)
            self.dma('act', rb[0:32, :], rel_bias, (), ['rb'])
            self.dma('act', rb31[:], rel_bias[31:32, :].partition_broadcast(33), (), ['rb31'])
            self.dma('act', ohs[:], oh_d, (), ['ohs'])
            self.tt('dve', rb[0:32, :], rb[0:32, :], rb31[0:32, 0, :], ALU.subtract, ['rb', 'rb31'], ['rb'])
            self.memset('pool', rb[32:33, :], NEG, ['rb'])
            pb, pk = self.gbank()
            self.mm(pb[0:H, 0:384], rb[:], ohs[:], True, True, ['rb', 'ohs'], [pk])
            fv = self.sb("fv", [H, 384], F32)
            self.cp('dve', fv[:], pb[0:H, 0:384], [pk], ['fv'])
            self.dma('sp', fv_scr, fv[:], ['fv'], ['fv_scr'])
            DC = self.sb("DC", [128, H, 2, 128], BF16)

            xres = self.sb("xres", [128, 4, D], F32)
            hT = self.sb("hT", [128, 8, GS], BF16)
            catT = self.sb("catT", [128, 8, GS], BF16)
            big = self.sb("big", [128, 16384], BF16)
            mT = big[:, 0:NPAIR * GS].rearrange("p (c t) -> p c t", c=NPAIR)
            Ast = self.sb("Ast", [128, 4, 30], F32)
            Cst = self.sb("Cst", [128, 8, 2], F32)
            Ust = self.sb("Ust", [128, 2, 44, 2], F32)
            tsum = self.sb("tsum", [64, H, NT], F32)
            kmT = self.sb("kmT", [64, H, NB], F32)
            QTaug = self.sb("QTaug", [96, H, GS], BF16)
            self.memset('pool', Ast[:], 0.0, ['Ast'])
            self.memset('pool', Cst[:], 0.0, ['Cst'])
            self.memset('pool', Ust[:], 0.0, ['Ust'])
            self.memset('pool', tsum[:], 0.0, ['tsum'])
            KTb = [big[0:96, i * 4096:(i + 1) * 4096] for i in range(2)]
            VAb = [big[:, 8192 + i * 4096:8192 + (i + 1) * 4096].rearrange("p (t c) -> p t c", c=128) for i in range(2)]

            def mkeys(c):
                a = ('KTb', 0) if c * GS < 4096 else (('KTb', 1) if c * GS < 8192 else (('VAb', 0) if c * GS < 12288 else ('VAb', 1)))
                b = ('KTb', 0) if (c + 1) * GS - 1 < 4096 else (('KTb', 1) if (c + 1) * GS - 1 < 8192 else (('VAb', 0) if (c + 1) * GS - 1 < 12288 else ('VAb', 1)))
                return list({('mT', c), a, b})
            self.kvi = 0
            def pool(name, n, shape, dt):
                return [self.sb("%s%d" % (name, i), shape, dt) for i in range(n)]
            sqb = pool("sqb", 1, [128, D], BF16)
            hb = pool("hb", 2, [128, D], BF16)
            st1 = pool("st1", 4, [128, 8], F32)
            f512 = pool("f512", 5, [128, 512], F32)
            kaug = pool("kaug", 2, [128, H, 96], BF16)
            nmt = pool("nmt", 2, [128, H, 32], BF16)
            g2p = pool("g2p", 2, [128, H, NB], F32)
            t8p = pool("t8p", 2, [128, H, 8], F32)
            qtf = pool("qtf", 1, [64, H, 128], F32)
            ktsb = pool("ktsb", 2, [96, H, 128], BF16)
            vasb = pool("vasb", 2, [128, H, 128], BF16)
            for i in range(2):
                self.memset('pool', vasb[i][:], 1.0, [('vasb', i)])
            self.cvA = pool("cvA", 4, [128, GS], F32)
            self.ubuf = pool("ubuf", 3, [128, 2 + GS], F32)
            abuf = pool("abuf", 2, [128, 30 + GS], F32)
            ptb = pool("ptb", 3, [128, 512], BF16)
            for i in range(2):
                self.memset('pool', nmt[i][:], 0.0, [('nmt', i)])
            self.rr = {}

            def nxt(name, lst):
                i = self.rr.get(name, 0)
                self.rr[name] = i + 1
                return lst[i % len(lst)], (name, i % len(lst))

            Jf = self.sb("Jf", [128, 128], F32)
            self.memset('pool', Jf[:], 0.0, ['Jf'])
            S.op('pool', lambda e: e.affine_select(out=Jf[:], in_=Jf[:], pattern=[[1, 128]], compare_op=ALU.not_equal,
                                                   fill=1.0, base=-127, channel_multiplier=1), ['Jf'], ['Jf'])
            for h in range(H):
                dct, dctk = nxt('f512', f512)
                for k, off in ((0, 0), (1, 128)):
                    src = bass.AP(tensor=fv_scr.tensor, offset=h * 384 + off, ap=[[1, 128], [1, 128]])
                    self.dma('sp', dct[:, k * 128:(k + 1) * 128], src, ['fv_scr'], [dctk], slow=True)
                pbj, pkj = self.gbank()
                self.mm(pbj[:, 0:256], Jf[:], dct[:, 0:256], True, True, ['Jf', dctk], [pkj])
                self.cp('dve', DC[:, h, :, :], pbj[:, 0:256].rearrange("p (k t) -> p k t", k=2), [pkj], ['DC'])
            def norm_T(xt, xkey, gcol, gkey, ntile_idx):
                sq, sqk = nxt('sqb', sqb)
                s1, s1k = nxt('st1', st1)
                self.act(sq[:], xt, AF.Square, [xkey], [sqk, s1k], accum=s1[:, 0:1])
                self.act(s1[:, 1:2], s1[:, 0:1], AF.Sqrt, [s1k], [s1k], scale=1.0 / D, bias=eps_t[:, 0:1])
                self.recip(s1[:, 2:3], s1[:, 1:2], [s1k], [s1k])
                hh, hk = nxt('hb', hb)
                self.ts('dve', hh[:], xt, s1[:, 2:3], None, ALU.mult, None, [xkey, s1k], [hk])
                pb, pk = self.gbank()
                pbf = pb[:].bitcast(BF16)
                for kc in range(8):
                    self.tr(pbf[:, kc * 128:(kc + 1) * 128], hh[:, kc * 128:(kc + 1) * 128], ident[:], [hk, 'ident'], [pk])
                c0 = ntile_idx * 128
                self.tt('dve', hT[:, :, c0:c0 + 128], pbf.rearrange("p (c t) -> p c t", c=8),
                        gcol.unsqueeze(2).to_broadcast([128, 8, 128]), ALU.mult, [pk, gkey], [('hT', ntile_idx)])

            def head_norm(pb, pk, gB, gBkey, extra_scale):
                sq, sqk = nxt('f512', f512)
                s1, s1k = nxt('st1', st1)
                self.act(sq[:], pb[:], AF.Square, [pk], [sqk])
                self.red(s1[:, 0:8], sq[:].rearrange("p (h d) -> p h d", h=H), ALU.add, [sqk], [s1k])
                s2, s2k = nxt('st1', st1)
                self.act(s2[:], s1[:], AF.Sqrt, [s1k], [s2k], scale=1.0 / HD, bias=eps_t[:, 0:1])
                self.recip(s1[:], s2[:], [s2k], [s1k])
                if extra_scale != 1.0:
                    self.ts('dve', s1[:], s1[:], extra_scale, None, ALU.mult, None, [s1k], [s1k])
                o, ok = nxt('f512', f512)
                o3 = o[:].rearrange("p (h d) -> p h d", h=H)
                self.tt('dve', o3, pb[:].rearrange("p (h d) -> p h d", h=H), s1[:].unsqueeze(2).to_broadcast([128, H, HD]),
                        ALU.mult, [pk, s1k], [ok])
                self.tt('dve', o3, o3, gB[:].to_broadcast([128, H, HD]), ALU.mult, [ok, gBkey], [ok])
                return o, ok

            def tok_mm(slot, skey, ntile_idx):
                pb, pk = self.gbank()
                sv = slot[:].rearrange("p (c n) -> p c n", c=8)
                c0 = ntile_idx * 128
                for kc in range(8):
                    self.mm(pb[:], hT[:, kc, c0:c0 + 128], sv[:, kc, :], kc == 0, kc == 7, [('hT', ntile_idx), skey], [pk])
                return pb, pk

            def w_cols(w2d, c0, n):
                return (lambda s: s[:, 0:8 * n].rearrange("p (c n) -> p c n", c=8),
                        w2d.rearrange("(c p) n -> p c n", p=128)[:, :, c0:c0 + n])

            def kv_tile(gt, ntile_idx, kslot, kskey, vslot, vskey, own_out_row):
                pb, pk = tok_mm(kslot, kskey, ntile_idx)
                Kf, Kfk = head_norm(pb, pk, gkB, 'gkB', 1.0)
                if own_out_row is not None:
                    self.dma(os.environ.get('KOQ', 'sp'), k_out[own_out_row:own_out_row + 128, :], Kf[:], [Kfk], [])
                ka, kak = nxt('kaug', kaug)
                self.memset('pool', ka[:, :, 64:96], 0.0, [kak])
                self.memset('pool', ka[:, :, 64 + gt // 2:65 + gt // 2], 1.0, [kak])
                self.cp('pool', ka[:, :, 0:64], Kf[:].rearrange("p (h d) -> p h d", h=H), [Kfk], [kak])
                pb2, pk2 = self.gbank()
                for h in range(H):
                    self.mm(pb2[0:64, h:h + 1], Kf[:, h * 64:(h + 1) * 64], onesf[:, 0:1], True, True, [Kfk, 'onesf'], [pk2])
                self.cp('dve', tsum[:, :, gt], pb2[0:64, 0:H], [pk2], ['tsum'])
                pb3, pk3 = self.gbank()
                p3 = pb3[:].bitcast(BF16)
                for h in range(H):
                    self.tr(p3[0:96, h * 128:(h + 1) * 128], ka[:, h, :], ident[:], [kak, 'ident'], [pk3])
                kts, ktsk = nxt('ktsb', ktsb)
                self.cp('act', kts[:], p3[0:96, :].rearrange("p (h t) -> p h t", h=H), [pk3], [ktsk])
                self.dma('sp', kt_scr[:, :, gt * 128:(gt + 1) * 128].rearrange("h r k -> r h k"), kts[:], [ktsk], ['kt_scr'])
                pbv, pkv = tok_mm(vslot, vskey, ntile_idx)
                if own_out_row is not None:
                    Vf, Vfk = nxt('f512', f512)
                    self.cp('act', Vf[:], pbv[:], [pkv], [Vfk])
                    self.dma(os.environ.get('KOQ', 'sp'), v_out[own_out_row:own_out_row + 128, :], Vf[:], [Vfk], [])
                vas, vask = nxt('vasb', vasb)
                if own_out_row is not None:
                    self.cp('dve', vas[:, :, 0:64], Vf[:].rearrange("p (h d) -> p h d", h=H), [Vfk], [vask])
                else:
                    self.cp('dve', vas[:, :, 0:64], pbv[:].rearrange("p (h d) -> p h d", h=H), [pkv], [vask])
                self.dma('sp', va_scr[:, :, gt, :].rearrange("h p c -> p h c"), vas[:], [vask], ['va_scr'])

            def load_x(row0, ntile):
                for ti in range(ntile):
                    self.dma('sp', xres[:, ti, :], xk[row0 + ti * 128:row0 + (ti + 1) * 128, :], (), [('xres', ti)])

            npre = P // 128 - 1
            kslot, kskey = self.wpiece([w_cols(w_in_e, 1536, 512)])
            vslot, vskey = self.wpiece([w_cols(w_in_e, 2048, 512)])
            gt = 0
            while gt < npre:
                nt_ = min(4, npre - gt)
                load_x(gt * 128, nt_)
                for ti in range(nt_):
                    norm_T(xres[:, ti, :], ('xres', ti), gE[:], 'gE', ti)
                for ti in range(nt_):
                    kv_tile(gt + ti, ti, kslot, kskey, vslot, vskey, None)
                gt += nt_

            groups = [(P // 128 - 1, 1, None)]
            for g in range(C // GS):
                groups.append((P // 128 + g * (GS // 128), GS // 128, g * GS))

            def attention(gt0, ntile):
                ntok = ntile * 128
                nkt = gt0 + ntile
                for h in range(H):
                    ob, okk = self.obank()
                    nhalf = (nkt + 31) // 32
                    first = True
                    for hf in range(nhalf):
                        k0 = hf * 32
                        k1 = min(nkt, k0 + 32)
                        i = self.kvi % 2
                        self.kvi += 1
                        ktb, vab = KTb[i], VAb[i]
                        self.dma('sp', ktb[:, 0:(k1 - k0) * 128], kt_scr[h, :, k0 * 128:k1 * 128], ['kt_scr'], [('KTb', i)])
                        self.dma('sp', vab[:, 0:k1 - k0, :], va_scr[h, :, k0:k1, :], ['va_scr'], [('VAb', i)])
                        for kt in range(k0, k1):
                            qlo = max(kt, gt0) - gt0
                            c0 = qlo * 128
                            sb_, sk = self.sbank()
                            has_d0 = kt >= gt0
                            has_c1 = (kt + 1 >= gt0) and (kt + 1 < gt0 + ntile)
                            self.mm(sb_[:, c0:ntok], ktb[:, (kt - k0) * 128:(kt - k0 + 1) * 128], QTaug[:, h, c0:ntok], True,
                                    not (has_d0 or has_c1), [('KTb', i), ('QTaug', h)], [sk])
                            if has_d0:
                                cc = (kt - gt0) * 128
                                self.mm(sb_[:, cc:cc + 128], ident[:], DC[:, h, 0, :], False, not has_c1, ['ident', 'DC'], [sk])
                            if has_c1:
                                cc = (kt + 1 - gt0) * 128
                                self.mm(sb_[:, cc:cc + 128], ident[:], DC[:, h, 1, :], False, True, ['ident', 'DC'], [sk])
                            pt, ptk = nxt('ptb', ptb)
                            self.act(pt[:, c0:ntok], sb_[:, c0:ntok], AF.Exp, [sk], [ptk])
                            self.mm(ob[:, c0:ntok], vab[:, kt - k0, :], pt[:, c0:ntok], first, kt == nkt - 1, [('VAb', i), ptk], [okk])
                            first = False
                    rec, rk = nxt('f512', f512)
                    self.recip(rec[0:64, 0:ntok], ob[64:128, 0:ntok], [okk], [rk])
                    po = (h % 2) * 64
                    self.tt('dve', catT[po:po + 64, 4 + h // 2, 0:ntok], ob[0:64, 0:ntok], rec[0:64, 0:ntok], ALU.mult,
                            [okk, rk], [('catT', 4 + h // 2, h % 2)])

            def out_proj(wsrc, ntile, catkeys):
                for half in range(2):
                    slot, skey = self.wpiece([w_cols(wsrc, half * 512, 512)])
                    sv = slot[:].rearrange("p (c n) -> p c n", c=8)
                    banks = [self.abank() for _ in range(ntile)]
                    for kc in range(8):
                        for ti in range(ntile):
                            self.mm(banks[ti][0][:], catT[:, kc, ti * 128:(ti + 1) * 128], sv[:, kc, :], kc == 0, kc == 7,
                                    catkeys(kc) + [skey], [banks[ti][1]])
                    for ti in range(ntile):
                        self.tt('dve', xres[:, ti, half * 512:(half + 1) * 512], banks[ti][0][:], xres[:, ti, half * 512:(half + 1) * 512],
                                ALU.add, [banks[ti][1], ('xres', ti)], [('xres', ti)])

            def ffn(l, ntile, first_own):
                ntok = ntile * 128
                for ti in range(ntile):
                    norm_T(xres[:, ti, :], ('xres', ti), gF[:, l, :], 'gF', ti)
                hkeys = [('hT', ti) for ti in range(ntile)]
                if first_own:
                    self.ts('dve', Ust[:, l, :, :], Ust[:, l, :, :], hv[:, 0, 0:1], None, ALU.mult, None, ['Ust', 'hvt'], ['Ust'])
                for c2 in range(NPAIR // 2):
                    slot, skey = self.wpiece([
                        (lambda s: s[:, 0:4096].rearrange("p (c n) -> p c n", c=8)[:, :, 0:256],
                         w_up[l].rearrange("(c p) n -> p c n", p=128)[:, :, c2 * 256:(c2 + 1) * 256]),
                        (lambda s: s[:, 0:4096].rearrange("p (c n) -> p c n", c=8)[:, :, 256:512],
                         w_up[l].rearrange("(c p) n -> p c n", p=128)[:, :, DFF + c2 * 256:DFF + (c2 + 1) * 256])])
                    sv = slot[:].rearrange("p (c n) -> p c n", c=8)
                    for cl in range(2):
                        c = c2 * 2 + cl
                        res = []
                        for part in range(2):
                            ci = c + part * NPAIR
                            pb, pk = self.gbank()
                            for kc in range(8):
                                self.mm(pb[:, 0:ntok], sv[:, kc, part * 256 + cl * 128: part * 256 + (cl + 1) * 128], hT[:, kc, 0:ntok],
                                        kc == 0, kc == 7, hkeys + [skey], [pk])
                            U, Uk = nxt('ubuf', self.ubuf)
                            self.cp('pool', U[:, 0:2], Ust[:, l, ci, :], ['Ust'], [Uk])
                            self.cp('act', U[:, 2:2 + ntok], pb[:, 0:ntok], [pk], [Uk])
                            self.cp('pool', Ust[:, l, ci, :], U[:, ntok:ntok + 2], [Uk], ['Ust'])
                            cv, cvk = nxt('f512', f512)
                            self.act(cv[:, 0:ntok], U[:, 2:2 + ntok], AF.Identity, [Uk, 'cfw', 'cfb'], [cvk],
                                     scale=cfw[:, l, ci, 2:3], bias=cfb[:, l, ci:ci + 1])
                            self.stt(cv[:, 0:ntok], U[:, 1:1 + ntok], cfw[:, l, ci, 1:2], cv[:, 0:ntok], ALU.mult, ALU.add, [Uk, cvk, 'cfw'], [cvk])
                            self.stt(cv[:, 0:ntok], U[:, 0:ntok], cfw[:, l, ci, 0:1], cv[:, 0:ntok], ALU.mult, ALU.add, [Uk, cvk, 'cfw'], [cvk])
                            res.append((cv, cvk))
                        (cg, cgk), (cu, cuk) = res
                        self.act(cg[:, 0:ntok], cg[:, 0:ntok], AF.Silu, [cgk], [cgk])
                        self.tt('dve', mT[:, c, 0:ntok], cg[:, 0:ntok], cu[:, 0:ntok], ALU.mult, [cgk, cuk], mkeys(c))
                ffn_down(l, ntile)

            def ffn_down(l, ntile):
                for half in range(2):
                    banks = [self.abank() for _ in range(ntile)]
                    for pi, (ca, cb) in enumerate(((0, 8), (8, 16), (16, 22))):
                        slot, skey = self.wpiece([(lambda s, n=cb - ca: s[:, 0:n * 512].rearrange("p (c n) -> p c n", c=n),
                                                   w_down[l, ca * 128:cb * 128, half * 512:(half + 1) * 512].rearrange("(c p) n -> p c n", p=128))])
                        sv = slot[:, 0:(cb - ca) * 512].rearrange("p (c n) -> p c n", c=cb - ca)
                        for c in range(ca, cb):
                            for ti in range(ntile):
                                self.mm(banks[ti][0][:], mT[:, c, ti * 128:(ti + 1) * 128], sv[:, c - ca, :], c == 0, c == NPAIR - 1,
                                        mkeys(c) + [skey], [banks[ti][1]])
                    for ti in range(ntile):
                        self.tt('dve', xres[:, ti, half * 512:(half + 1) * 512], banks[ti][0][:], xres[:, ti, half * 512:(half + 1) * 512],
                                ALU.add, [banks[ti][1], ('xres', ti)], [('xres', ti)])

            def ln_silu(cvs, ntok):
                pbm, pkm = self.gbank()
                pbs, pks = self.gbank()
                sqs = []
                for cc in range(4):
                    sq, sqk = nxt('f512', f512)
                    self.act(sq[:, 0:ntok], cvs[cc][0][:, 0:ntok], AF.Square, [cvs[cc][1]], [sqk])
                    sqs.append((sq, sqk))
                for cc in range(4):
                    self.mm(pbm[:, 0:ntok], onesf[:], cvs[cc][0][:, 0:ntok], cc == 0, cc == 3, ['onesf', cvs[cc][1]], [pkm])
                for cc in range(4):
                    self.mm(pbs[:, 0:ntok], onesf[:], sqs[cc][0][:, 0:ntok], cc == 0, cc == 3, ['onesf', sqs[cc][1]], [pks])
                mean, meank = nxt('f512', f512)
                self.ts('dve', mean[:, 0:ntok], pbm[:, 0:ntok], 1.0 / DA, None, ALU.mult, None, [pkm], [meank])
                var, vark = sqs[0]
                self.tt('dve', var[:, 0:ntok], mean[:, 0:ntok], mean[:, 0:ntok], ALU.mult, [meank], [vark])
                self.stt(var[:, 0:ntok], pbs[:, 0:ntok], 1.0 / DA, var[:, 0:ntok], ALU.mult, ALU.subtract, [pks, vark], [vark])
                self.act(var[:, 0:ntok], var[:, 0:ntok], AF.Sqrt, [vark], [vark], bias=eps_t[:, 0:1], scale=1.0)
                self.recip(var[:, 0:ntok], var[:, 0:ntok], [vark], [vark])
                for cc in range(4):
                    cv, cvk = cvs[cc]
                    self.tt('dve', cv[:, 0:ntok], cv[:, 0:ntok], mean[:, 0:ntok], ALU.subtract, [cvk, meank], [cvk])
                    self.tt('dve', cv[:, 0:ntok], cv[:, 0:ntok], var[:, 0:ntok], ALU.mult, [cvk, vark], [cvk])
                    self.act(catT[:, cc, 0:ntok], cv[:, 0:ntok], AF.Silu, [cvk, 'lng', 'lnb'], [('catT', cc, 0), ('catT', cc, 1)],
                             scale=lng[:, cc:cc + 1], bias=lnb[:, cc:cc + 1])

            self.marks = []
            mark = lambda n: self.marks.append((n, len(S.ops)))
            for (gt0, ntile, orow) in groups:
                ntok = ntile * 128
                mark('group %d start' % gt0)
                is_halo = orow is None
                first_own = (orow == 0)
                load_x(gt0 * 128, ntile)
                for ti in range(ntile):
                    norm_T(xres[:, ti, :], ('xres', ti), gE[:], 'gE', ti)
                hkeys = [('hT', ti) for ti in range(ntile)]
                kslot, kskey = self.wpiece([w_cols(w_in_e, 1536, 512)])
                vslot, vskey = self.wpiece([w_cols(w_in_e, 2048, 512)])
                for ti in range(ntile):
                    kv_tile(gt0 + ti, ti, kslot, kskey, vslot, vskey, (0 if os.environ.get("KHALO") else None) if is_halo else orow + ti * 128)
                mark('kv done')
                self.tt('dve', kmT[:], tsum[:].rearrange("p h (n two) -> p h n two", two=2)[:, :, :, 0],
                        tsum[:].rearrange("p h (n two) -> p h n two", two=2)[:, :, :, 1], ALU.add, ['tsum'], ['kmT'])
                qslot, qskey = self.wpiece([w_cols(w_in_e, 1024, 512)])
                for ti in range(ntile):
                    own = (gt0 + ti) // 2
                    pb, pk = tok_mm(qslot, qskey, ti)
                    Qf, Qfk = head_norm(pb, pk, gqB, 'gqB', HD ** -0.5)
                    qt_, qtk = nxt('qtf', qtf)
                    for hh2 in range(2):
                        pbq, pkq = self.gbank()
                        for h4 in range(4):
                            h = hh2 * 4 + h4
                            self.tr(pbq[0:64, h4 * 128:(h4 + 1) * 128], Qf[:, h * 64:(h + 1) * 64], identf[:], [Qfk, 'identf'], [pkq])
                        self.cp('act', qt_[:, hh2 * 4:(hh2 + 1) * 4, :], pbq[0:64, :].rearrange("p (h t) -> p h t", h=4), [pkq], [qtk])
                    self.cp('pool', QTaug[0:64, :, ti * 128:(ti + 1) * 128], qt_[:], [qtk], [('QTaug', h) for h in range(H)])
                    pbg, pkg = self.gbank()
                    for h in range(H):
                        self.mm(pbg[:, h * NB:(h + 1) * NB], qt_[:, h, :], kmT[:, h, :], True, True, [qtk, 'kmT'], [pkg])
                    g2, g2k = nxt('g2p', g2p)
                    self.tt('dve', g2[:], pbg[:, 0:H * NB].rearrange("p (h n) -> p h n", h=H),
                            gmask[:, 0, own:own + 1, :].to_broadcast([128, H, NB]), ALU.add, [pkg, 'gmaskt'], [g2k])
                    t8, t8k = nxt('t8p', t8p)
                    for h in range(H):
                        S.op('dve', (lambda e, o=t8[:, h, :], i=g2[:, h, :]: e.max(out=o, in_=i)), [g2k], [t8k])
                    c1, c1k = nxt('g2p', g2p)
                    self.tt('dve', c1[:], g2[:], t8[:, :, 2:3].to_broadcast([128, H, NB]), ALU.is_ge, [g2k, t8k], [c1k])
                    self.ts('dve', g2[:], g2[:], NEG, None, ALU.is_gt, None, [g2k], [g2k])
                    self.tt('dve', c1[:], c1[:], g2[:], ALU.mult, [c1k, g2k], [c1k])
                    self.tt('dve', c1[:], c1[:], scB[:].unsqueeze(2).to_broadcast([128, H, NB]), ALU.mult, [c1k, 'scB'], [c1k])
                    nm, nmk = nxt('nmt', nmt)
                    self.ts('dve', nm[:, :, 0:NB], c1[:], NEG, None, ALU.add, None, [c1k], [nmk])
                    self.cp('dve', nm[:, :, own], b31B[:, 0, :], ['b31B', nmk], [nmk])
                    pbn, pkn = self.gbank()
                    pn = pbn[:].bitcast(BF16)
                    for h in range(H):
                        self.tr(pn[0:32, h * 128:(h + 1) * 128], nm[:, h, :], ident[:], [nmk, 'ident'], [pkn])
                    self.cp('act', QTaug[64:96, :, ti * 128:(ti + 1) * 128], pn[0:32, :].rearrange("p (h t) -> p h t", h=H),
                            [pkn], [('QTaug', h) for h in range(H)])
                mark('q done')
                if first_own:
                    self.ts('dve', Ast[:], Ast[:], hv[:, 0, 0:1], None, ALU.mult, None, ['Ast', 'hvt'], ['Ast'])
                valslot, valk = self.wpiece([w_cols(w_in_e, 0, 512)])
                gateslot, gatek = self.wpiece([w_cols(w_in_e, 512, 512)])
                vv = valslot[:].rearrange("p (c n) -> p c n", c=8)
                gv = gateslot[:].rearrange("p (c n) -> p c n", c=8)
                cvs = []
                for cc in range(4):
                    pbv, pkv = self.gbank()
                    pbg, pkg = self.gbank()
                    for kc in range(8):
                        self.mm(pbv[:, 0:ntok], vv[:, kc, cc * 128:(cc + 1) * 128], hT[:, kc, 0:ntok], kc == 0, kc == 7, hkeys + [valk], [pkv])
                    for kc in range(8):
                        self.mm(pbg[:, 0:ntok], gv[:, kc, cc * 128:(cc + 1) * 128], hT[:, kc, 0:ntok], kc == 0, kc == 7, hkeys + [gatek], [pkg])
                    sg, sgk = nxt('f512', f512)
                    self.act(sg[:, 0:ntok], pbg[:, 0:ntok], AF.Sigmoid, [pkg], [sgk])
                    ab, abk = nxt('abuf', abuf)
                    self.cp('pool', ab[:, 0:30], Ast[:, cc, :], ['Ast'], [abk])
                    self.tt('dve', ab[:, 30:30 + ntok], pbv[:, 0:ntok], sg[:, 0:ntok], ALU.mult, [pkv, sgk], [abk])
                    self.cp('pool', Ast[:, cc, :], ab[:, ntok:ntok + 30], [abk], ['Ast'])
                    cv, cvk = nxt('cvA', self.cvA)
                    self.act(cv[:, 0:ntok], ab[:, 30:30 + ntok], AF.Identity, [abk, 'caw', 'cab'], [cvk],
                             scale=caw[:, cc, 30:31], bias=cab[:, cc:cc + 1])
                    for j in range(30):
                        self.stt(cv[:, 0:ntok], ab[:, j:j + ntok], caw[:, cc, j:j + 1], cv[:, 0:ntok], ALU.mult, ALU.add,
                                 [abk, cvk, 'caw'], [cvk])
                    cvs.append((cv, cvk))
                ln_silu(cvs, ntok)
                mark('mixerA done')
                attention(gt0, ntile)
                mark('attn done')
                out_proj(w_out_e, ntile, lambda kc: [('catT', kc, 0), ('catT', kc, 1)])
                mark('outproj done')
                ffn(0, ntile, first_own)
                mark('ffn0 done')
                for ti in range(ntile):
                    norm_T(xres[:, ti, :], ('xres', ti), gO[:], 'gO', ti)
                if first_own:
                    self.ts('dve', Cst[:], Cst[:], hv[:, 0, 0:1], None, ALU.mult, None, ['Cst', 'hvt'], ['Cst'])
                for c in range(8):
                    slot, skey = self.wpiece([
                        (lambda s, k=k: s[:, 0:8 * 384].rearrange("p (c n) -> p c n", c=8)[:, :, k * 128:(k + 1) * 128],
                         w_in_o.rearrange("(c p) n -> p c n", p=128)[:, :, k * 1024 + c * 128:k * 1024 + (c + 1) * 128])
                        for k in range(3)])
                    sv = slot[:, 0:8 * 384].rearrange("p (c n) -> p c n", c=8)
                    pbs_ = []
                    for k in range(3):
                        pb, pk = self.gbank()
                        for kc in range(8):
                            self.mm(pb[:, 0:ntok], sv[:, kc, k * 128:(k + 1) * 128], hT[:, kc, 0:ntok], kc == 0, kc == 7, hkeys + [skey], [pk])
                        pbs_.append((pb, pk))
                    uu, uuk = nxt('f512', f512)
                    self.cp('act', uu[:, 0:ntok], pbs_[2][0][:, 0:ntok], [pbs_[2][1]], [uuk])
                    cb_, cbk = nxt('ubuf', self.ubuf)
                    self.cp('pool', cb_[:, 0:2], Cst[:, c, :], ['Cst'], [cbk])
                    self.tt('dve', cb_[:, 2:2 + ntok], pbs_[1][0][:, 0:ntok], uu[:, 0:ntok], ALU.mult, [pbs_[1][1], uuk], [cbk])
                    self.cp('pool', Cst[:, c, :], cb_[:, ntok:ntok + 2], [cbk], ['Cst'])
                    cv, cvk = nxt('f512', f512)
                    self.act(cv[:, 0:ntok], cb_[:, 2:2 + ntok], AF.Identity, [cbk, 'ccw'], [cvk], scale=ccw[:, c, 2:3])
                    self.stt(cv[:, 0:ntok], cb_[:, 1:1 + ntok], ccw[:, c, 1:2], cv[:, 0:ntok], ALU.mult, ALU.add, [cbk, cvk, 'ccw'], [cvk])
                    self.stt(cv[:, 0:ntok], cb_[:, 0:ntok], ccw[:, c, 0:1], cv[:, 0:ntok], ALU.mult, ALU.add, [cbk, cvk, 'ccw'], [cvk])
                    self.tt('dve', catT[:, c, 0:ntok], pbs_[0][0][:, 0:ntok], cv[:, 0:ntok], ALU.mult, [pbs_[0][1], cvk],
                            [('catT', c, 0), ('catT', c, 1)])
                out_proj(w_out_o, ntile, lambda kc: [('catT', kc, 0), ('catT', kc, 1)])
                ffn(1, ntile, first_own)
                if not is_halo:
                    for ti in range(ntile):
                        self.dma('sp', y_out[orow + ti * 128:orow + (ti + 1) * 128, :], xres[:, ti, :], [('xres', ti)], [])


            def wslot_take():
                i = self.wi % self.NSLOT
                self.wi += 1
                return self.wsl[i], ('w', i)

            def sample_phase():
                def cload(name, shape, src, dt=F32, q='act'):
                    t = self.sb(name + "_t", shape, dt)
                    self.dma(q, t[:], src, (), [name])
                    return t
                oh16 = cload("oh16", [30, 256], oh16_d)
                W30 = cload("W30", [30, DA], conv_a_w[0:30, :])
                ohS = cload("ohS", [33, 128], ohS_d)
                ptb = cload("ptb", [128, 1, 256], pt_d.partition_broadcast(128), I32)
                rb0B = cload("rb0B", [128, 1, H], rel_bias[0:1, :].partition_broadcast(128))
                io = self.sb("io", [128, 1], I32)
                iof = self.sb("iof", [128, 1], F32)
                idx = self.sb("idx", [128, 256], I32)
                S.op('pool', lambda e: e.iota(io[:], pattern=[[0, 1]], base=0, channel_multiplier=1), (), ['io'])
                self.cp('pool', iof[:], io[:], ['io'], ['iof'])
                self.ts('dve', idx[:], ptb[:, 0, :], 128.0, iof[:, 0:1], ALU.mult, ALU.add, ['ptb', 'iof'], ['idx'])
                self.tt('dve', rb0B[:, 0, :], rb0B[:, 0, :], b31B[:, 0, :], ALU.subtract, ['rb0B', 'b31B'], ['rb0B'])
                biasP = self.sb("biasP", [128, H], F32)
                pb, pk = self.gbank()
                self.mm(pb[:, 0:H], ohS[:], rb[:], True, True, ['ohS', 'rb'], [pk])
                self.cp('dve', biasP[:], pb[:, 0:H], [pk], ['biasP'])
                VS = self.sb("VS", [128, 512], F32)
                AsT = self.sb("AsT", [128, 4, 16], F32)
                small = self.sb("smalls", [16, 64], F32)
                self.dma('sp', xres[0:16, 0, :], xs_d, (), [('xres', 0)])
                norm_T(xres[:, 0, :], ('xres', 0), gE[:], 'gE', 0)
                hk0 = [('hT', 0)]
                kslot, kskey = self.wpiece([w_cols(w_in_e, 1536, 512)])
                pb, pk = tok_mm(kslot, kskey, 0)
                Kf, Kfk = head_norm(pb, pk, gkB, 'gkB', 1.0)
                self.dma('sp', ks_out, Kf[0:16, :], [Kfk], [])
                vslot, vskey = self.wpiece([w_cols(w_in_e, 2048, 512)])
                pbv, pkv = tok_mm(vslot, vskey, 0)
                self.cp('act', VS[:], pbv[:], [pkv], ['VS'])
                self.dma('sp', vs_out, VS[0:16, :], ['VS'], [])
                qslot, qskey = self.wpiece([w_cols(w_in_e, 1024, 512)])
                pb, pk = tok_mm(qslot, qskey, 0)
                Qf, Qfk = head_norm(pb, pk, gqB, 'gqB', HD ** -0.5)
                self.dma('sp', qs_scr, Qf[0:16, :], [Qfk], ['qs_scr'])
                lself = small[:, 0:8]
                tmpq, tmpqk = nxt('f512', f512)
                self.tt('dve', tmpq[0:16, :], Qf[0:16, :], Kf[0:16, :], ALU.mult, [Qfk, Kfk], [tmpqk])
                self.red(lself, tmpq[0:16, :].rearrange("p (h d) -> p h d", h=H), ALU.add, [tmpqk], ['small'])
                valslot, valk = self.wpiece([w_cols(w_in_e, 0, 512)])
                gateslot, gatek = self.wpiece([w_cols(w_in_e, 512, 512)])
                vv = valslot[:].rearrange("p (c n) -> p c n", c=8)
                gv = gateslot[:].rearrange("p (c n) -> p c n", c=8)
                for cc in range(4):
                    pbv, pkv = self.gbank()
                    pbg, pkg = self.gbank()
                    for kc in range(8):
                        self.mm(pbv[:, 0:16], vv[:, kc, cc * 128:(cc + 1) * 128], hT[:, kc, 0:16], kc == 0, kc == 7, hk0 + [valk], [pkv])
                    for kc in range(8):
                        self.mm(pbg[:, 0:16], gv[:, kc, cc * 128:(cc + 1) * 128], hT[:, kc, 0:16], kc == 0, kc == 7, hk0 + [gatek], [pkg])
                    sg, sgk = nxt('f512', f512)
                    self.act(sg[:, 0:16], pbg[:, 0:16], AF.Sigmoid, [pkg], [sgk])
                    self.tt('dve', AsT[:, cc, :], pbv[:, 0:16], sg[:, 0:16], ALU.mult, [pkv, sgk], ['AsT'])
                accb, acck = self.gbank()
                for s_ in range(16):
                    stA, stAk = nxt('f512', f512)
                    self.dma('sp', stA[0:30, :], sta_d[s_], (), [stAk])
                    self.tt('dve', stA[0:30, :], stA[0:30, :], W30[:], ALU.mult, [stAk, 'W30'], [stAk])
                    self.mm(accb[0:16, :], oh16[:, s_ * 16:(s_ + 1) * 16], stA[0:30, :], s_ == 0, s_ == 15, ['oh16', stAk], [acck])
                cst, cstk = nxt('f512', f512)
                self.cp('dve', cst[0:16, :], accb[0:16, :], [acck], [cstk])
                pbt, pkt = self.gbank()
                for cc in range(4):
                    self.tr(pbt[:, cc * 16:(cc + 1) * 16], cst[0:16, cc * 128:(cc + 1) * 128], identf[0:16, 0:16], [cstk, 'identf'], [pkt])
                cvs = []
                for cc in range(4):
                    cv, cvk = nxt('cvA', self.cvA)
                    self.act(cv[:, 0:16], AsT[:, cc, :], AF.Identity, ['AsT', 'caw', 'cab'], [cvk], scale=caw[:, cc, 30:31], bias=cab[:, cc:cc + 1])
                    self.tt('dve', cv[:, 0:16], cv[:, 0:16], pbt[:, cc * 16:(cc + 1) * 16], ALU.add, [cvk, pkt], [cvk])
                    cvs.append((cv, cvk))
                ln_silu(cvs, 16)
                self.dma('act', cas_out[:, 0:29, :], sta_d[:, 1:30, :], (), [])
                pba, pka = self.gbank()
                for cc in range(4):
                    self.tr(pba[0:16, cc * 128:(cc + 1) * 128], AsT[:, cc, :], identf[:], ['AsT', 'identf'], [pka])
                atok, atokk = nxt('f512', f512)
                self.cp('dve', atok[0:16, :], pba[0:16, :], [pka], [atokk])
                self.dma('sp', cas_out[:, 29, :], atok[0:16, :], [atokk], [])
                lself = small[:, 0:8]
                denAll = small[:, 8:16]
                dtot = small[:, 16:24]
                self.tt('dve', lself, lself, rb0B[0:16, 0, :], ALU.add, ['small', 'rb0B'], ['small'])
                self.act(lself, lself, AF.Exp, ['small'], ['small'])
                self.memset('pool', denAll, 0.0, ['small'])
                Kb = big[:, 0:8192].bitcast(F32).rearrange("p (g n) -> p g n", g=8)
                Vb = big[:, 8192:16384].bitcast(F32).rearrange("p (g n) -> p g n", g=8)
                Kbk = [('KTb', 0), ('KTb', 1)]
                Vbk = [('VAb', 0), ('VAb', 1)]
                vbb = [wslot_take() for _ in range(2)]
                pzs, pzk = wslot_take()
                Pz = pzs[:].rearrange("p (pg hg c) -> p pg hg c", pg=16, hg=2)
                Pz4 = pzs[:].rearrange("p (pg h sl) -> p pg h sl", pg=16, h=H)
                self.memset('pool', pzs[:], 0.0, [pzk])
                Lt = self.sb("Lt", [128, 128], F32)
                Pf = self.sb("Pf", [128, 128], F32)
                gsb = self.sb("gsb", [128, 8, 8], F32)
                c1b = self.sb("c1b", [128, 8, 8], F32)
                t8s = self.sb("t8s", [128, H, 8], F32)
                den8 = self.sb("den8", [128, H], F32)
                ob0, ok0 = self.ps[6], ('ps', 6)
                ob1, ok1 = self.ps[7], ('ps', 7)
                obs = [(ob0, ok0), (ob1, ok1)]
                for s_ in range(16):
                    qb, qbk = nxt('f512', f512)
                    self.dma('sp', qb[:].rearrange("p (o n) -> p o n", o=1), qs_scr[s_:s_ + 1, :].partition_broadcast(128), ['qs_scr'], [qbk])
                    for half in range(2):
                        for pg in range(8):
                            col = s_ * 16 + half * 8 + pg
                            S.op('pool', (lambda e, o=Kb[:, pg, :], ix=idx[:, col:col + 1]: e.indirect_dma_start(
                                out=o, out_offset=None, in_=ck_d, in_offset=bass.IndirectOffsetOnAxis(ap=ix, axis=0))),
                                ['idx'], Kbk, dma=True)
                        self.tt('pool', Kb, Kb, qb[:].unsqueeze(1).to_broadcast([128, 8, 512]), ALU.mult, Kbk + [qbk], Kbk)
                        self.red(Lt[:, half * 64:(half + 1) * 64], Kb.rearrange("p g (h d) -> p (g h) d", h=H), ALU.add, Kbk, ['Lt'])
                    pbg, pkg = self.gbank()
                    self.mm(pbg[:, 0:128], onesf[:], Lt[:], True, True, ['onesf', 'Lt'], [pkg])
                    G4 = pbg[:, 0:128].rearrange("p (n two h) -> p n two h", n=8, two=2)
                    self.cp('dve', gsb[:], G4[:, :, 0, :], [pkg], ['gsb'])
                    self.tt('dve', gsb[:], gsb[:], G4[:, :, 1, :], ALU.add, [pkg, 'gsb'], ['gsb'])
                    for h in range(H):
                        S.op('dve', (lambda e, o=t8s[:, h, :], i=gsb[:, :, h]: e.max(out=o, in_=i)), ['gsb'], ['t8s'])
                    self.tt('dve', c1b[:], gsb[:], t8s[:, :, 2].unsqueeze(1).to_broadcast([128, 8, H]), ALU.is_ge, ['gsb', 't8s'], ['c1b'])
                    self.ts('dve', c1b[:], c1b[:], -NEG, NEG, ALU.mult, ALU.add, ['c1b'], ['c1b'])
                    L4 = Lt[:].rearrange("p (n two h) -> p n two h", n=8, two=2)
                    self.tt('dve', L4, L4, c1b[:].unsqueeze(2).to_broadcast([128, 8, 2, H]), ALU.add, ['Lt', 'c1b'], ['Lt'])
                    self.tt('dve', Lt[:, 120:128], Lt[:, 120:128], biasP[:], ALU.add, ['Lt', 'biasP'], ['Lt'])
                    self.act(Pz4[:, :, :, s_], Lt[:].rearrange("p (pg h) -> p pg h", pg=16), AF.Exp, ['Lt'], [pzk])
                    self.cp('dve', Pf[:].rearrange("p (pg h) -> p pg h", pg=16), Pz4[:, :, :, s_], [pzk], ['Pf'])
                    pbd, pkd = self.gbank()
                    self.mm(pbd[:, 0:128], onesf[:], Pf[:], True, True, ['onesf', 'Pf'], [pkd])
                    self.red(den8[:], pbd[:, 0:128].rearrange("p (pg h) -> p h pg", pg=16), ALU.add, [pkd], ['den8'])
                    self.stt(denAll, den8[0:16, :], identf[0:16, s_:s_ + 1], denAll, ALU.mult, ALU.add, ['den8', 'identf', 'small'], ['small'])
                    for half in range(2):
                        vb_, vbk_ = vbb[half]
                        vb3 = vb_[:].rearrange("p (g n) -> p g n", g=8)
                        for pg in range(8):
                            col = s_ * 16 + half * 8 + pg
                            S.op('pool', (lambda e, o=Vb[:, pg, :], ix=idx[:, col:col + 1]: e.indirect_dma_start(
                                out=o, out_offset=None, in_=cv_d, in_offset=bass.IndirectOffsetOnAxis(ap=ix, axis=0))),
                                ['idx'], Vbk, dma=True)
                        self.cp('pool', vb3, Vb, Vbk, [vbk_])
                        for pg in range(8):
                            page = half * 8 + pg
                            for hg in range(2):
                                self.mm(obs[hg][0][:], Pz[:, page, hg, :], vb3[:, pg, :], (s_ == 0 and page == 0), (s_ == 15 and page == 15),
                                        [pzk, vbk_], [obs[hg][1]])
                    S.op('pool', (lambda e, o=Pz4[:, :, :, s_]: e.memset(o, 0.0)), (), [pzk])
                otok_t, otokk = nxt('f512', f512)
                otok = otok_t[0:16, :]
                for h in range(H):
                    hg, hl = h // 4, h % 4
                    self.cp('dve', otok[:, h * 64:(h + 1) * 64], obs[hg][0][32 * hl:32 * hl + 16, h * 64:(h + 1) * 64], [obs[hg][1]], [otokk])
                o3 = otok.rearrange("p (h d) -> p h d", h=H)
                tv, tvk = nxt('f512', f512)
                tv3 = tv[0:16, :].rearrange("p (h d) -> p h d", h=H)
                self.tt('dve', tv3, VS[0:16, :].rearrange("p (h d) -> p h d", h=H), lself.unsqueeze(2).to_broadcast([16, H, HD]), ALU.mult,
                        ['VS', 'small'], [tvk])
                self.tt('dve', o3, o3, tv3, ALU.add, [otokk, tvk], [otokk])
                self.tt('dve', dtot, denAll, lself, ALU.add, ['small'], ['small'])
                self.recip(dtot, dtot, ['small'], ['small'])
                self.tt('dve', o3, o3, dtot.unsqueeze(2).to_broadcast([16, H, HD]), ALU.mult, [otokk, 'small'], [otokk])
                pbo, pko = self.gbank()
                for cc in range(4):
                    self.tr(pbo[:, cc * 16:(cc + 1) * 16], otok[:, cc * 128:(cc + 1) * 128], identf[0:16, 0:16], [otokk, 'identf'], [pko])
                self.cp('dve', catT[:, 4:8, 0:16], pbo[:, 0:64].rearrange("p (c t) -> p c t", c=4), [pko],
                        [('catT', 4 + c_, k_) for c_ in range(4) for k_ in range(2)])
                out_proj(w_out_e, 1, lambda kc: [('catT', kc, 0), ('catT', kc, 1)])
                ffn_s(0)
                norm_T(xres[:, 0, :], ('xres', 0), gO[:], 'gO', 0)
                stcT = self.sb("stcT", [128, 8, 2, 16], F32)
                cuT = self.sb("cuT", [128, 8, 16], F32)
                for j in range(2):
                    for hh in range(2):
                        tl, tlk = nxt('f512', f512)
                        self.dma('sp', tl[0:16, :], stc_d[:, j, hh * 512:(hh + 1) * 512], (), [tlk])
                        pbt, pkt = self.gbank()
                        for c4 in range(4):
                            self.tr(pbt[:, c4 * 16:(c4 + 1) * 16], tl[0:16, c4 * 128:(c4 + 1) * 128], identf[0:16, 0:16], [tlk, 'identf'], [pkt])
                        self.cp('dve', stcT[:, hh * 4:(hh + 1) * 4, j, :], pbt[:, 0:64].rearrange("p (c t) -> p c t", c=4), [pkt], ['stcT'])
                for c in range(8):
                    slot, skey = self.wpiece([
                        (lambda s, k=k: s[:, 0:8 * 384].rearrange("p (c n) -> p c n", c=8)[:, :, k * 128:(k + 1) * 128],
                         w_in_o.rearrange("(c p) n -> p c n", p=128)[:, :, k * 1024 + c * 128:k * 1024 + (c + 1) * 128])
                        for k in range(3)])
                    sv = slot[:, 0:8 * 384].rearrange("p (c n) -> p c n", c=8)
                    pbs_ = []
                    for k in range(3):
                        pb, pk = self.gbank()
                        for kc in range(8):
                            self.mm(pb[:, 0:16], sv[:, kc, k * 128:(k + 1) * 128], hT[:, kc, 0:16], kc == 0, kc == 7, hk0 + [skey], [pk])
                        pbs_.append((pb, pk))
                    uu, uuk = nxt('f512', f512)
                    self.cp('act', uu[:, 0:16], pbs_[2][0][:, 0:16], [pbs_[2][1]], [uuk])
                    self.tt('dve', cuT[:, c, :], pbs_[1][0][:, 0:16], uu[:, 0:16], ALU.mult, [pbs_[1][1], uuk], ['cuT'])
                    cv, cvk = nxt('f512', f512)
                    self.act(cv[:, 0:16], cuT[:, c, :], AF.Identity, ['cuT', 'ccw'], [cvk], scale=ccw[:, c, 2:3])
                    self.stt(cv[:, 0:16], stcT[:, c, 1, :], ccw[:, c, 1:2], cv[:, 0:16], ALU.mult, ALU.add, ['stcT', cvk, 'ccw'], [cvk])
                    self.stt(cv[:, 0:16], stcT[:, c, 0, :], ccw[:, c, 0:1], cv[:, 0:16], ALU.mult, ALU.add, ['stcT', cvk, 'ccw'], [cvk])
                    self.tt('dve', catT[:, c, 0:16], pbs_[0][0][:, 0:16], cv[:, 0:16], ALU.mult, [pbs_[0][1], cvk],
                            [('catT', c, 0), ('catT', c, 1)])
                self.dma('act', ccs_out[:, 0, :], stc_d[:, 1, :], (), [])
                for hh in range(2):
                    pbc, pkc = self.gbank()
                    for c4 in range(4):
                        self.tr(pbc[0:16, c4 * 128:(c4 + 1) * 128], cuT[:, hh * 4 + c4, :], identf[:], ['cuT', 'identf'], [pkc])
                    ct, ctk = nxt('f512', f512)
                    self.cp('dve', ct[0:16, :], pbc[0:16, :], [pkc], [ctk])
                    self.dma('sp', ccs_out[:, 1, hh * 512:(hh + 1) * 512], ct[0:16, :], [ctk], [])
                out_proj(w_out_o, 1, lambda kc: [('catT', kc, 0), ('catT', kc, 1)])
                ffn_s(1)
                self.dma('sp', ys_out, xres[0:16, 0, :], [('xres', 0)], [])

            def ffn_s(l):
                norm_T(xres[:, 0, :], ('xres', 0), gF[:, l, :], 'gF', 0)
                hk0 = [('hT', 0)]
                stfT = self.stfT
                upT = self.upT
                for j in range(2):
                    for blk in range(11):
                        tl, tlk = nxt('f512', f512)
                        self.dma('sp', tl[0:16, :], stf_d[l, :, j, blk * 512:(blk + 1) * 512], (), [tlk])
                        pbt, pkt = self.gbank()
                        for c4 in range(4):
                            self.tr(pbt[:, c4 * 16:(c4 + 1) * 16], tl[0:16, c4 * 128:(c4 + 1) * 128], identf[0:16, 0:16], [tlk, 'identf'], [pkt])
                        self.cp('dve', stfT[:, blk * 4:(blk + 1) * 4, j, :], pbt[:, 0:64].rearrange("p (c t) -> p c t", c=4), [pkt], ['stfT'])
                for c2 in range(NPAIR // 2):
                    slot, skey = self.wpiece([
                        (lambda s: s[:, 0:4096].rearrange("p (c n) -> p c n", c=8)[:, :, 0:256],
                         w_up[l].rearrange("(c p) n -> p c n", p=128)[:, :, c2 * 256:(c2 + 1) * 256]),
                        (lambda s: s[:, 0:4096].rearrange("p (c n) -> p c n", c=8)[:, :, 256:512],
                         w_up[l].rearrange("(c p) n -> p c n", p=128)[:, :, DFF + c2 * 256:DFF + (c2 + 1) * 256])])
                    sv = slot[:].rearrange("p (c n) -> p c n", c=8)
                    for cl in range(2):
                        c = c2 * 2 + cl
                        res = []
                        for part in range(2):
                            ci = c + part * NPAIR
                            pb, pk = self.gbank()
                            for kc in range(8):
                                self.mm(pb[:, 0:16], sv[:, kc, part * 256 + cl * 128: part * 256 + (cl + 1) * 128], hT[:, kc, 0:16],
                                        kc == 0, kc == 7, hk0 + [skey], [pk])
                            self.cp('act', upT[:, ci, :], pb[:, 0:16], [pk], ['upT'])
                            cv, cvk = nxt('f512', f512)
                            self.act(cv[:, 0:16], upT[:, ci, :], AF.Identity, ['upT', 'cfw', 'cfb'], [cvk],
                                     scale=cfw[:, l, ci, 2:3], bias=cfb[:, l, ci:ci + 1])
                            self.stt(cv[:, 0:16], stfT[:, ci, 1, :], cfw[:, l, ci, 1:2], cv[:, 0:16], ALU.mult, ALU.add, ['stfT', cvk, 'cfw'], [cvk])
                            self.stt(cv[:, 0:16], stfT[:, ci, 0, :], cfw[:, l, ci, 0:1], cv[:, 0:16], ALU.mult, ALU.add, ['stfT', cvk, 'cfw'], [cvk])
                            res.append((cv, cvk))
                        (cg, cgk), (cu, cuk) = res
                        self.act(cg[:, 0:16], cg[:, 0:16], AF.Silu, [cgk], [cgk])
                        self.tt('dve', mT[:, c, 0:16], cg[:, 0:16], cu[:, 0:16], ALU.mult, [cgk, cuk], mkeys(c))
                self.dma('act', fs_out[l, :, 0, :], stf_d[l, :, 1, :], (), [])
                for blk in range(11):
                    pbc, pkc = self.gbank()
                    for c4 in range(4):
                        self.tr(pbc[0:16, c4 * 128:(c4 + 1) * 128], upT[:, blk * 4 + c4, :], identf[:], ['upT', 'identf'], [pkc])
                    ct, ctk = nxt('f512', f512)
                    self.cp('dve', ct[0:16, :], pbc[0:16, :], [pkc], [ctk])
                    self.dma('sp', fs_out[l, :, 1, blk * 512:(blk + 1) * 512], ct[0:16, :], [ctk], [])
                ffn_down(l, 1)

            if self.do_sample:
                self.stfT = self.sb("stfT", [128, 44, 2, 16], F32)
                self.upT = self.sb("upT", [128, 44, 16], F32)
                sample_phase()

            def fm_out(src_fn, nchunk, r, dst):
                c = 0
                while c < nchunk:
                    n = min(4, nchunk - c)
                    pb, pk = self.gbank()
                    for i in range(n):
                        ap_, keys = src_fn(c + i)
                        self.tr(pb[0:r, i * 128:(i + 1) * 128], ap_, identf[:], keys + ['identf'], [pk])
                    o, ok = nxt('f512', f512)
                    self.cp('dve', o[0:r, 0:n * 128], pb[0:r, 0:n * 128], [pk], [ok])
                    self.dma('sp', dst[:, c * 128:(c + n) * 128], o[0:r, 0:n * 128], [ok], [])
                    c += n
            fm_out(lambda c: (Ast[:, c, :], ['Ast']), 4, 30, ca_out)
            fm_out(lambda c: (Cst[:, c, :], ['Cst']), 8, 2, cc_out)
            for l in range(2):
                fm_out(lambda c, l=l: (Ust[:, l, c, :], ['Ust']), 44, 2, f_out[l])


            self.nops = len(S.ops)
            if os.environ.get('KSKIPOUT'):
                S.ops = [o for o in S.ops if not (o['dma'] and len(o['w']) == 0)]
            if os.environ.get('KSTOP'):
                S.ops = S.ops[:int(os.environ['KSTOP'])]
            S.emit(st)
            print(self.marks)
            print('nops', self.nops, 'emitted', len(S.ops), 'sem counts', S.max_counts, flush=True)
        return nc


def _bf(x):
    return x


def make_consts(C):
    NB = 4 * C // 256
    oh = np.zeros((33, 384), np.float32)
    for m in range(383):
        d = m - 127
        if d < 0:
            oh[32, m] = 1.0
        else:
            oh[int(t5_bucket_np(np.array([d]))[0]), m] = 1.0
    return oh


def core_inputs(inputs, c, C, SEQ, do_sample=True):
    b, j = c // 4, c % 4
    P = 3 * C
    NK = 4 * C
    NB = NK // 256
    x = inputs['x_prompt'][b]
    xk = np.zeros((NK, D), np.float32)
    lo = C * j - P
    src_lo = max(lo, 0)
    xk[src_lo - lo:, :] = x[src_lo:C * j + C]
    nvalid0 = (src_lo - lo) // 256
    gmask = np.zeros((NB, NB), np.float32)
    for own in range(NB):
        for n in range(NB):
            if n >= own or n < nvalid0:
                gmask[own, n] = -60000.0
    hv = np.array([[0.0 if j == 0 else 1.0]], np.float32)
    m = dict(xk=xk, gmask=gmask, hv=hv, oh=make_consts(C))
    for k in ('w_in_e', 'w_out_e', 'w_in_o', 'w_out_o', 'norm_mix_e', 'norm_mix_o', 'conv_a_w', 'conv_a_b', 'ln_a_g',
              'ln_a_b', 'q_norm_g', 'k_norm_g', 'conv_c_w'):
        m[k] = np.ascontiguousarray(inputs[k][0])
    for k in ('w_up', 'w_down', 'norm_ffn', 'conv_f_w', 'conv_f_b', 'rel_bias'):
        m[k] = np.ascontiguousarray(inputs[k])
    if do_sample:
        s0 = 16 * c
        m['xs'] = np.ascontiguousarray(inputs['x_sample'][s0:s0 + 16, 0, :])
        m['pt'] = np.ascontiguousarray(inputs['page_table'][s0:s0 + 16].reshape(1, 256).astype(np.int32))
        ck = inputs['cache_k'][0]
        m['cache_k'] = ck.reshape(ck.shape[0] * 128, 512)
        cv = inputs['cache_v'][0]
        m['cache_v'] = cv.reshape(cv.shape[0] * 128, 512)
        m['state_a'] = np.ascontiguousarray(inputs['state_conv_a'][0, s0:s0 + 16])
        m['state_c'] = np.ascontiguousarray(inputs['state_conv_c'][0, s0:s0 + 16])
        m['state_f'] = np.ascontiguousarray(inputs['state_ffn'][:, s0:s0 + 16])
        ohS = np.zeros((33, 128), np.float32)
        for kk in range(128):
            ohS[int(t5_bucket_np(np.array([128 - kk]))[0]), kk] = 1.0
        m['ohS'] = ohS
        oh16 = np.zeros((30, 16, 16), np.float32)
        for s_ in range(16):
            oh16[:, s_, s_] = 1.0
        m['oh16'] = oh16.reshape(30, 256)
    return m


_NC_CACHE = {}


def run_all(inputs, C, SEQ, do_sample=True):
    npool = inputs['cache_k'].shape[1] if do_sample else 2560
    key = (C, do_sample, npool)
    if key not in _NC_CACHE:
        bld = Builder(C, do_sample=do_sample, npool=npool)
        _NC_CACHE[key] = bld.build()
    nc = _NC_CACHE[key]
    in_maps = [core_inputs(inputs, c, C, SEQ, do_sample) for c in range(8)]
    res = run_bass_kernel_spmd(nc, in_maps, core_ids=list(range(8)))
    return res.results


def run_prompt(inputs, C, SEQ):
    return run_all(inputs, C, SEQ, do_sample=False)


def assemble(res, C, SEQ):
    y = np.zeros((2, SEQ, D), np.float32)
    k = np.zeros((2, SEQ, 512), np.float32)
    v = np.zeros((2, SEQ, 512), np.float32)
    for c in range(8):
        b, j = c // 4, c % 4
        y[b, C * j:C * j + C] = res[c]['y_out']
        k[b, C * j:C * j + C] = res[c]['k_out']
        v[b, C * j:C * j + C] = res[c]['v_out']
    npg = SEQ // 128
    k_prompt = k.reshape(1, 2, npg, 128, H, HD)
    v_prompt = v.reshape(1, 2, npg, 128, H, HD)
    a_p = np.stack([res[3]['ca_out'], res[7]['ca_out']])[None]
    c_p = np.stack([res[3]['cc_out'], res[7]['cc_out']])[None]
    f_p = np.stack([np.stack([res[3]['f_out'][l], res[7]['f_out'][l]]) for l in range(2)])
    ys = np.concatenate([res[c]['ys_out'] for c in range(8)], 0)[:, None, :]
    ks = np.concatenate([res[c]['ks_out'] for c in range(8)], 0).reshape(1, 128, 1, H, HD)
    vs = np.concatenate([res[c]['vs_out'] for c in range(8)], 0).reshape(1, 128, 1, H, HD)
    a_s = np.concatenate([res[c]['cas_out'] for c in range(8)], 0)[None]
    c_s = np.concatenate([res[c]['ccs_out'] for c in range(8)], 0)[None]
    f_s = np.concatenate([res[c]['fs_out'] for c in range(8)], 1)
    outs = (y, ys, k_prompt, v_prompt, a_p, c_p, f_p, ks, vs, a_s, c_s, f_s)
    return tuple(np.ascontiguousarray(o, dtype=np.float32) for o in outs)


def kernel(**inputs):
    inputs = {k: np.asarray(v) for k, v in inputs.items()}
    SEQ = inputs['x_prompt'].shape[1]
    C = SEQ // 4
    res = run_all(inputs, C, SEQ, do_sample=True)
    return assemble(res, C, SEQ)
```

```python
import math
import os
from contextlib import ExitStack
import numpy as np
import concourse.bass as bass
import concourse.mybir as mybir
from concourse.bass_utils import run_bass_kernel_spmd

F32 = mybir.dt.float32
BF16 = mybir.dt.bfloat16
I32 = mybir.dt.int32
AF = mybir.ActivationFunctionType
ALU = mybir.AluOpType
AX = mybir.AxisListType

D = 1024
DA = 512
H = 8
HD = 64
DFF = 2816
NPAIR = 22
EPS = 1e-6
NEG = -30000.0
ENG = ('pe', 'act', 'dve', 'pool', 'sp')


class Sched:
    NDMA = 12

    def __init__(self, nc):
        self.nc = nc
        self.ops = []
        self.eng = {'pe': nc.tensor, 'act': nc.scalar, 'dve': nc.vector, 'pool': nc.gpsimd, 'sp': nc.sync}

    def op(self, engine, fn, reads=(), writes=(), dma=False):
        self.ops.append(dict(e=engine, fn=fn, r=tuple(reads), w=tuple(writes), dma=dma,
                             sig=False, seq=None, waits=[]))

    def emit(self, stack):
        nc = self.nc
        ops = self.ops
        last_w = {}
        readers = {}
        seqc = {e: 0 for e in ENG}
        waited = {e: {y: -1 for y in ENG} for e in ENG}
        waited_dma = {e: set() for e in ENG}
        dma_cnt = {e: 0 for e in ENG}
        for i, o in enumerate(ops):
            e = o['e']
            deps = set()
            for k in o['r']:
                if k in last_w:
                    deps.add(last_w[k])
            for k in o['w']:
                if k in last_w:
                    deps.add(last_w[k])
                for j in readers.get(k, ()):
                    deps.add(j)
            deps.discard(i)
            need = {}
            for j in deps:
                oj = ops[j]
                if oj['dma']:
                    if j not in waited_dma[e]:
                        waited_dma[e].add(j)
                        o['waits'].append(('dma', j))
                else:
                    y = oj['e']
                    if y == 'pe' and e == 'pe':
                        continue
                    if oj['seq'] > waited[e][y]:
                        need[y] = max(need.get(y, -1), oj['seq'])
            for y, s in need.items():
                waited[e][y] = s
                o['waits'].append(('cmp', y, s))
            if o['dma']:
                n = dma_cnt[e]
                dma_cnt[e] += 1
                o['dsem'] = (e, n % self.NDMA)
                o['dval'] = 16 * (n // self.NDMA + 1)
            else:
                o['seq'] = seqc[e]
                seqc[e] += 1
            for k in o['r']:
                readers.setdefault(k, []).append(i)
            for k in o['w']:
                last_w[k] = i
                readers[k] = []
        by_seq = {e: {} for e in ENG}
        for i, o in enumerate(ops):
            if not o['dma']:
                by_seq[o['e']][o['seq']] = i
        for o in ops:
            for w in o['waits']:
                if w[0] == 'cmp':
                    ops[by_seq[w[1]][w[2]]]['sig'] = True
        cnt = {e: 0 for e in ENG}
        for o in ops:
            if not o['dma'] and o['sig']:
                cnt[o['e']] += 1
                o['cnt'] = cnt[o['e']]
        self.max_counts = dict(cnt)
        csem = {e: stack.enter_context(nc.semaphore('c_' + e)) for e in ENG}
        dsem = {}
        for e in ENG:
            for n in range(min(self.NDMA, dma_cnt[e])):
                dsem[(e, n)] = stack.enter_context(nc.semaphore('d_%s_%d' % (e, n)))
        for i, o in enumerate(ops):
            e = o['e']
            eng = self.eng[e]
            for w in o['waits']:
                if w[0] == 'dma':
                    oj = ops[w[1]]
                    eng.wait_ge(dsem[oj['dsem']], oj['dval'])
                else:
                    oj = ops[by_seq[w[1]][w[2]]]
                    eng.wait_ge(csem[w[1]], oj['cnt'])
            if o['dma']:
                if o['dval'] > 16:
                    eng.wait_ge(dsem[o['dsem']], o['dval'] - 16)
                ins = o['fn'](eng)
                ins.then_inc(dsem[o['dsem']], 16)
            else:
                ins = o['fn'](eng)
                if o['sig']:
                    ins.then_inc(csem[e], 1)
        last_dma = {}
        for o in ops:
            if o['dma']:
                last_dma[o['dsem']] = o['dval']
        for k, v in last_dma.items():
            nc.sync.wait_ge(dsem[k], v)


def t5_bucket_np(n):
    n = np.maximum(n, 0)
    nf = np.maximum(n, 1).astype(np.float32)
    large = 16 + (np.log(nf / np.float32(16)) / np.float32(math.log(128 / 16)) * np.float32(16)).astype(np.int32)
    large = np.minimum(large, 31)
    return np.where(n < 16, n, large)


class Builder:
    def __init__(self, C, nsamp=16, do_sample=True, npool=2560):
        self.C = C
        self.P = 3 * C
        self.NK = 4 * C
        self.NT = self.NK // 128
        self.NB = self.NK // 256
        self.GS = min(512, C)
        self.nsamp = nsamp
        self.do_sample = do_sample
        self.npool = npool
        self.nc = bass.Bass("TRN2", target_bir_lowering=False)
        self.ukey = 0

    def din(self, name, shape, dt=F32):
        return self.nc.dram_tensor(name, list(shape), dt, kind="ExternalInput").ap()

    def dout(self, name, shape, dt=F32):
        return self.nc.dram_tensor(name, list(shape), dt, kind="ExternalOutput").ap()

    def dscr(self, name, shape, dt):
        return self.nc.dram_tensor(name, list(shape), dt, kind="Internal").ap()

    def sb(self, name, shape, dt):
        return self.st.enter_context(self.nc.sbuf_tensor(name, list(shape), dt))

    def uk(self, p='t'):
        self.ukey += 1
        return (p, self.ukey)

    def mm(self, out, lhsT, rhs, start, stop, r, w):
        self.S.op('pe', lambda e: e.matmul(out, lhsT=lhsT, rhs=rhs, start=start, stop=stop), r, w)

    def tr(self, out, in_, ident, r, w):
        self.S.op('pe', lambda e: e.transpose(out=out, in_=in_, identity=ident), r, w)

    def act(self, out, in_, func, r, w, scale=None, bias=None, accum=None):
        kw = {}
        if scale is not None:
            kw['scale'] = scale
        if bias is not None:
            kw['bias'] = bias
        if accum is not None:
            kw['accum_out'] = accum
        self.S.op('act', lambda e: e.activation(out=out, in_=in_, func=func, **kw), r, w)

    def tt(self, eng, out, in0, in1, op, r, w):
        self.S.op(eng, lambda e: e.tensor_tensor(out=out, in0=in0, in1=in1, op=op), r, w)

    def ts(self, eng, out, in0, s1, s2, op0, op1, r, w):
        if op1 is None:
            self.S.op(eng, lambda e: e.tensor_scalar(out=out, in0=in0, scalar1=s1, scalar2=None, op0=op0), r, w)
        else:
            self.S.op(eng, lambda e: e.tensor_scalar(out=out, in0=in0, scalar1=s1, scalar2=s2, op0=op0, op1=op1), r, w)

    def stt(self, out, in0, scalar, in1, op0, op1, r, w):
        self.S.op('dve', lambda e: e.scalar_tensor_tensor(out=out, in0=in0, scalar=scalar, in1=in1, op0=op0, op1=op1), r, w)

    def cp(self, eng, out, in_, r, w):
        if eng == 'act':
            self.S.op('act', lambda e: e.activation(out=out, in_=in_, func=AF.Copy), r, w)
        else:
            self.S.op(eng, lambda e: e.tensor_copy(out=out, in_=in_), r, w)

    def red(self, out, in_, op, r, w, axis=AX.X):
        self.S.op('dve', lambda e: e.tensor_reduce(out=out, in_=in_, axis=axis, op=op), r, w)

    def recip(self, out, in_, r, w):
        self.S.op('dve', lambda e: e.reciprocal(out=out, in_=in_), r, w)

    def memset(self, eng, ap, val, w):
        self.S.op(eng, lambda e: e.memset(ap, val), (), w)

    def dma(self, q, out, in_, r, w, slow=False):
        if slow:
            self.S.op(q, lambda e: e.dma_start(out=out, in_=in_, allow_slow_non_contiguous=True), r, w, dma=True)
        else:
            self.S.op(q, lambda e: e.dma_start(out=out, in_=in_), r, w, dma=True)

    def gbank(self):
        i = self.gi % 4
        self.gi += 1
        return self.ps[i], ('ps', i)

    def abank(self):
        i = 4 + self.ai % 4
        self.ai += 1
        return self.ps[i], ('ps', i)

    def sbank(self):
        i = 4 + self.si % 2
        self.si += 1
        return self.ps[i], ('ps', i)

    def obank(self):
        i = 6 + self.oi % 2
        self.oi += 1
        return self.ps[i], ('ps', i)

    def wpiece(self, loads):
        i = self.wi % self.NSLOT
        self.wi += 1
        slot = self.wsl[i]
        key = ('w', i)
        for dstf, src in loads:
            self.wser = getattr(self, 'wser', 0) + 1
            self.dma('pool', dstf(slot), src, (), [key, ('wser', self.wser % 2)])
        return slot, key

    def build(self):
        nc = self.nc
        C, P, NK, NT, NB, GS = self.C, self.P, self.NK, self.NT, self.NB, self.GS
        xk = self.din("xk", [NK, D])
        gmask_d = self.din("gmask", [NB, NB])
        hv_d = self.din("hv", [1, 1])
        oh_d = self.din("oh", [33, 384])
        w_in_e = self.din("w_in_e", [D, 2560])
        w_out_e = self.din("w_out_e", [D, D])
        w_in_o = self.din("w_in_o", [D, 3072])
        w_out_o = self.din("w_out_o", [D, D])
        w_up = self.din("w_up", [2, D, 2 * DFF])
        w_down = self.din("w_down", [2, DFF, D])
        rel_bias = self.din("rel_bias", [32, H])
        norm_mix_e = self.din("norm_mix_e", [D])
        norm_mix_o = self.din("norm_mix_o", [D])
        norm_ffn = self.din("norm_ffn", [2, D])
        conv_a_w = self.din("conv_a_w", [31, DA])
        conv_a_b = self.din("conv_a_b", [DA])
        ln_a_g = self.din("ln_a_g", [DA])
        ln_a_b = self.din("ln_a_b", [DA])
        q_norm_g = self.din("q_norm_g", [HD])
        k_norm_g = self.din("k_norm_g", [HD])
        conv_c_w = self.din("conv_c_w", [3, D])
        conv_f_w = self.din("conv_f_w", [2, 3, 2 * DFF])
        conv_f_b = self.din("conv_f_b", [2, 2 * DFF])
        y_out = self.dout("y_out", [C, D])
        k_out = self.dout("k_out", [C, 512])
        v_out = self.dout("v_out", [C, 512])
        ca_out = self.dout("ca_out", [30, DA])
        cc_out = self.dout("cc_out", [2, D])
        f_out = self.dout("f_out", [2, 2, 2 * DFF])
        if self.do_sample:
            NPOOL = self.npool
            xs_d = self.din("xs", [16, D])
            pt_d = self.din("pt", [1, 256], I32)
            ck_d = self.din("cache_k", [NPOOL * 64, 1024])
            cv_d = self.din("cache_v", [NPOOL * 64, 1024])
            sta_d = self.din("state_a", [16, 30, DA])
            stc_d = self.din("state_c", [16, 2, D])
            stf_d = self.din("state_f", [2, 16, 2, 2 * DFF])
            ohS_d = self.din("ohS", [33, 256])
            oh16_d = self.din("oh16", [30, 256])
            ys_out = self.dout("ys_out", [16, D])
            ks_out = self.dout("ks_out", [16, 512])
            vs_out = self.dout("vs_out", [16, 512])
            cas_out = self.dout("cas_out", [16, 30, DA])
            ccs_out = self.dout("ccs_out", [16, 2, D])
            fs_out = self.dout("fs_out", [2, 16, 2, 2 * DFF])
        qs_scr = self.dscr("qs_scr", [16, 512], F32)
        kt_scr = self.dscr("kt_scr", [H, 96, NK], BF16)
        va_scr = self.dscr("va_scr", [H, 128, NT, 128], BF16)
        fv_scr = self.dscr("fv_scr", [H, 384], F32)
        if os.environ.get("KDBG"):
            k_out = self.dscr("dbg_scr", [C, 512], F32)

        with ExitStack() as st:
            self.st = st
            self.S = Sched(nc)
            S = self.S
            self.gi = 0
            self.ai = 0
            self.si = 0
            self.oi = 0
            self.wi = 0
            self.NSLOT = 5
            self.ps = [st.enter_context(nc.psum_tensor("ps%d" % i, [128, 512], F32)) for i in range(8)]
            self.wsl = [self.sb("wslot%d" % i, [128, 4096], BF16) for i in range(self.NSLOT)]
            identf = self.sb("identf", [128, 128], F32)
            ident = self.sb("ident", [128, 128], BF16)
            onesf = self.sb("onesf", [128, 128], F32)
            eps_t = self.sb("eps_t", [128, 1], F32)
            self.memset('pool', identf[:], 0.0, ['identf'])
            S.op('pool', lambda e: e.affine_select(out=identf[:], in_=identf[:], pattern=[[-1, 128]], compare_op=ALU.not_equal,
                                                   fill=1.0, base=0, channel_multiplier=1), ['identf'], ['identf'])
            self.cp('dve', ident[:], identf[:], ['identf'], ['ident'])
            self.memset('pool', onesf[:], 1.0, ['onesf'])
            self.memset('pool', eps_t[:], EPS, ['eps_t'])
            self.ident, self.identf, self.onesf, self.eps_t = ident, identf, onesf, eps_t

            def colload(name, src_ap, shape):
                t = self.sb(name, shape, F32)
                self.dma('act', t[:], src_ap, (), [name], slow=True)
                return t
            def colload2(name, shape, parts):
                t = self.sb(name, shape, F32)
                for dst_fn, src in parts:
                    self.dma('act', dst_fn(t), src, (), [name], slow=True)
                return t
            gE = colload("gE", norm_mix_e.rearrange("(c p) -> p c", p=128), [128, 8])
            gO = colload("gO", norm_mix_o.rearrange("(c p) -> p c", p=128), [128, 8])
            gF = colload2("gF", [128, 2, 8], [(lambda t, l=l: t[:, l, :], norm_ffn[l].rearrange("(c p) -> p c", p=128)) for l in range(2)])
            caw = colload2("caw", [128, 4, 31], [(lambda t, c=c: t[:, c, :], conv_a_w[:, c * 128:(c + 1) * 128].rearrange("j p -> p j"))
                                                  for c in range(4)])
            cab = colload("cab", conv_a_b.rearrange("(c p) -> p c", p=128), [128, 4])
            lng = colload("lng", ln_a_g.rearrange("(c p) -> p c", p=128), [128, 4])
            lnb = colload("lnb", ln_a_b.rearrange("(c p) -> p c", p=128), [128, 4])
            ccw = colload2("ccw", [128, 8, 3], [(lambda t, j=j: t[:, :, j], conv_c_w[j].rearrange("(c p) -> p c", p=128)) for j in range(3)])
            cfw = colload2("cfw", [128, 2, 44, 3], [(lambda t, l=l, j=j: t[:, l, :, j], conv_f_w[l, j].rearrange("(c p) -> p c", p=128))
                                                     for l in range(2) for j in range(3)])
            cfb = colload2("cfb", [128, 2, 44], [(lambda t, l=l: t[:, l, :], conv_f_b[l].rearrange("(c p) -> p c", p=128)) for l in range(2)])
            gqB = colload("gqB", q_norm_g.rearrange("(o d) -> o d", o=1).partition_broadcast(128), [128, 1, HD])
            gkB = colload("gkB", k_norm_g.rearrange("(o d) -> o d", o=1).partition_broadcast(128), [128, 1, HD])
            b31B = colload("b31B", rel_bias[31:32, :].partition_broadcast(128), [128, 1, H])
            gmask = self.sb("gmaskt", [128, 1, NB, NB], BF16)
            self.dma('pool', gmask[:], gmask_d.rearrange("(o a) b -> o a b", o=1).partition_broadcast(128), (), ['gmaskt'])
            hv = colload("hvt", hv_d.partition_broadcast(128), [128, 1, 1])
            scB = self.sb("scB", [128, H], F32)
            self.ts('dve', scB[:], b31B[:, 0, :], -NEG, None, ALU.add, None, ['b31B'], ['scB'])

            rb = self.sb("rb", [33, H], F32)
            rb31 = self.sb("rb31", [33, 1, H], F32)
            ohs = self.sb("ohs", [33, 384], F32)
            self.dma('act', rb[0:32, :], rel_bias, (), ['rb'])
            self.dma('act', rb31[:], rel_bias[31:32, :].partition_broadcast(33), (), ['rb31'])
            self.dma('act', ohs[:], oh_d, (), ['ohs'])
            self.tt('dve', rb[0:32, :], rb[0:32, :], rb31[0:32, 0, :], ALU.subtract, ['rb', 'rb31'], ['rb'])
            self.memset('pool', rb[32:33, :], NEG, ['rb'])
            pb, pk = self.gbank()
            self.mm(pb[0:H, 0:384], rb[:], ohs[:], True, True, ['rb', 'ohs'], [pk])
            fv = self.sb("fv", [H, 384], F32)
            self.cp('dve', fv[:], pb[0:H, 0:384], [pk], ['fv'])
            self.dma('sp', fv_scr, fv[:], ['fv'], ['fv_scr'])
            DC = self.sb("DC", [128, H, 2, 128], BF16)

            xres = self.sb("xres", [128, 4, D], F32)
            hT = self.sb("hT", [128, 8, GS], BF16)
            catT = self.sb("catT", [128, 8, GS], BF16)
            big = self.sb("big", [128, 16384], BF16)
            mT = big[:, 0:NPAIR * GS].rearrange("p (c t) -> p c t", c=NPAIR)
            Ast = self.sb("Ast", [128, 4, 30], F32)
            Cst = self.sb("Cst", [128, 8, 2], F32)
            Ust = self.sb("Ust", [128, 2, 44, 2], F32)
            tsum = self.sb("tsum", [64, H, NT], F32)
            kmT = self.sb("kmT", [64, H, NB], F32)
            QTaug = self.sb("QTaug", [96, H, GS], BF16)
            self.memset('pool', Ast[:], 0.0, ['Ast'])
            self.memset('pool', Cst[:], 0.0, ['Cst'])
            self.memset('pool', Ust[:], 0.0, ['Ust'])
            self.memset('pool', tsum[:], 0.0, ['tsum'])
            KTb = [big[0:96, i * 4096:(i + 1) * 4096] for i in range(2)]
            VAb = [big[:, 8192 + i * 4096:8192 + (i + 1) * 4096].rearrange("p (t c) -> p t c", c=128) for i in range(2)]

            def mkeys(c):
                a = ('KTb', 0) if c * GS < 4096 else (('KTb', 1) if c * GS < 8192 else (('VAb', 0) if c * GS < 12288 else ('VAb', 1)))
                b = ('KTb', 0) if (c + 1) * GS - 1 < 4096 else (('KTb', 1) if (c + 1) * GS - 1 < 8192 else (('VAb', 0) if (c + 1) * GS - 1 < 12288 else ('VAb', 1)))
                return list({('mT', c), a, b})
            self.kvi = 0
            def pool(name, n, shape, dt):
                return [self.sb("%s%d" % (name, i), shape, dt) for i in range(n)]
            sqb = pool("sqb", 1, [128, D], BF16)
            hb = pool("hb", 2, [128, D], BF16)
            st1 = pool("st1", 4, [128, 8], F32)
            f512 = pool("f512", 5, [128, 512], F32)
            kaug = pool("kaug", 2, [128, H, 96], BF16)
            nmt = pool("nmt", 2, [128, H, 32], BF16)
            g2p = pool("g2p", 2, [128, H, NB], F32)
            t8p = pool("t8p", 2, [128, H, 8], F32)
            qtf = pool("qtf", 1, [64, H, 128], F32)
            ktsb = pool("ktsb", 2, [96, H, 128], BF16)
            vasb = pool("vasb", 2, [128, H, 128], BF16)
            for i in range(2):
                self.memset('pool', vasb[i][:], 1.0, [('vasb', i)])
            self.cvA = pool("cvA", 4, [128, GS], F32)
            self.ubuf = pool("ubuf", 3, [128, 2 + GS], F32)
            abuf = pool("abuf", 2, [128, 30 + GS], F32)
            ptb = pool("ptb", 3, [128, 512], BF16)
            for i in range(2):
                self.memset('pool', nmt[i][:], 0.0, [('nmt', i)])
            self.rr = {}

            def nxt(name, lst):
                i = self.rr.get(name, 0)
                self.rr[name] = i + 1
                return lst[i % len(lst)], (name, i % len(lst))

            Jf = self.sb("Jf", [128, 128], F32)
            self.memset('pool', Jf[:], 0.0, ['Jf'])
            S.op('pool', lambda e: e.affine_select(out=Jf[:], in_=Jf[:], pattern=[[1, 128]], compare_op=ALU.not_equal,
                                                   fill=1.0, base=-127, channel_multiplier=1), ['Jf'], ['Jf'])
            for h in range(H):
                dct, dctk = nxt('f512', f512)
                for k, off in ((0, 0), (1, 128)):
                    src = bass.AP(tensor=fv_scr.tensor, offset=h * 384 + off, ap=[[1, 128], [1, 128]])
                    self.dma('sp', dct[:, k * 128:(k + 1) * 128], src, ['fv_scr'], [dctk], slow=True)
                pbj, pkj = self.gbank()
                self.mm(pbj[:, 0:256], Jf[:], dct[:, 0:256], True, True, ['Jf', dctk], [pkj])
                self.cp('dve', DC[:, h, :, :], pbj[:, 0:256].rearrange("p (k t) -> p k t", k=2), [pkj], ['DC'])
            def norm_T(xt, xkey, gcol, gkey, ntile_idx):
                sq, sqk = nxt('sqb', sqb)
                s1, s1k = nxt('st1', st1)
                self.act(sq[:], xt, AF.Square, [xkey], [sqk, s1k], accum=s1[:, 0:1])
                self.act(s1[:, 1:2], s1[:, 0:1], AF.Sqrt, [s1k], [s1k], scale=1.0 / D, bias=eps_t[:, 0:1])
                self.recip(s1[:, 2:3], s1[:, 1:2], [s1k], [s1k])
                hh, hk = nxt('hb', hb)
                self.ts('dve', hh[:], xt, s1[:, 2:3], None, ALU.mult, None, [xkey, s1k], [hk])
                pb, pk = self.gbank()
                pbf = pb[:].bitcast(BF16)
                for kc in range(8):
                    self.tr(pbf[:, kc * 128:(kc + 1) * 128], hh[:, kc * 128:(kc + 1) * 128], ident[:], [hk, 'ident'], [pk])
                c0 = ntile_idx * 128
                self.tt('dve', hT[:, :, c0:c0 + 128], pbf.rearrange("p (c t) -> p c t", c=8),
                        gcol.unsqueeze(2).to_broadcast([128, 8, 128]), ALU.mult, [pk, gkey], [('hT', ntile_idx)])

            def head_norm(pb, pk, gB, gBkey, extra_scale):
                sq, sqk = nxt('f512', f512)
                s1, s1k = nxt('st1', st1)
                self.act(sq[:], pb[:], AF.Square, [pk], [sqk])
                self.red(s1[:, 0:8], sq[:].rearrange("p (h d) -> p h d", h=H), ALU.add, [sqk], [s1k])
                s2, s2k = nxt('st1', st1)
                self.act(s2[:], s1[:], AF.Sqrt, [s1k], [s2k], scale=1.0 / HD, bias=eps_t[:, 0:1])
                self.recip(s1[:], s2[:], [s2k], [s1k])
                if extra_scale != 1.0:
                    self.ts('dve', s1[:], s1[:], extra_scale, None, ALU.mult, None, [s1k], [s1k])
                o, ok = nxt('f512', f512)
                o3 = o[:].rearrange("p (h d) -> p h d", h=H)
                self.tt('dve', o3, pb[:].rearrange("p (h d) -> p h d", h=H), s1[:].unsqueeze(2).to_broadcast([128, H, HD]),
                        ALU.mult, [pk, s1k], [ok])
                self.tt('dve', o3, o3, gB[:].to_broadcast([128, H, HD]), ALU.mult, [ok, gBkey], [ok])
                return o, ok

            def tok_mm(slot, skey, ntile_idx):
                pb, pk = self.gbank()
                sv = slot[:].rearrange("p (c n) -> p c n", c=8)
                c0 = ntile_idx * 128
                for kc in range(8):
                    self.mm(pb[:], hT[:, kc, c0:c0 + 128], sv[:, kc, :], kc == 0, kc == 7, [('hT', ntile_idx), skey], [pk])
                return pb, pk

            def w_cols(w2d, c0, n):
                return (lambda s: s[:, 0:8 * n].rearrange("p (c n) -> p c n", c=8),
                        w2d.rearrange("(c p) n -> p c n", p=128)[:, :, c0:c0 + n])

            def kv_tile(gt, ntile_idx, kslot, kskey, vslot, vskey, own_out_row):
                pb, pk = tok_mm(kslot, kskey, ntile_idx)
                Kf, Kfk = head_norm(pb, pk, gkB, 'gkB', 1.0)
                if own_out_row is not None:
                    self.dma(os.environ.get('KOQ', 'sp'), k_out[own_out_row:own_out_row + 128, :], Kf[:], [Kfk], [])
                ka, kak = nxt('kaug', kaug)
                self.memset('pool', ka[:, :, 64:96], 0.0, [kak])
                self.memset('pool', ka[:, :, 64 + gt // 2:65 + gt // 2], 1.0, [kak])
                self.cp('pool', ka[:, :, 0:64], Kf[:].rearrange("p (h d) -> p h d", h=H), [Kfk], [kak])
                pb2, pk2 = self.gbank()
                for h in range(H):
                    self.mm(pb2[0:64, h:h + 1], Kf[:, h * 64:(h + 1) * 64], onesf[:, 0:1], True, True, [Kfk, 'onesf'], [pk2])
                self.cp('dve', tsum[:, :, gt], pb2[0:64, 0:H], [pk2], ['tsum'])
                pb3, pk3 = self.gbank()
                p3 = pb3[:].bitcast(BF16)
                for h in range(H):
                    self.tr(p3[0:96, h * 128:(h + 1) * 128], ka[:, h, :], ident[:], [kak, 'ident'], [pk3])
                kts, ktsk = nxt('ktsb', ktsb)
                self.cp('act', kts[:], p3[0:96, :].rearrange("p (h t) -> p h t", h=H), [pk3], [ktsk])
                self.dma('sp', kt_scr[:, :, gt * 128:(gt + 1) * 128].rearrange("h r k -> r h k"), kts[:], [ktsk], ['kt_scr'])
                pbv, pkv = tok_mm(vslot, vskey, ntile_idx)
                if own_out_row is not None:
                    Vf, Vfk = nxt('f512', f512)
                    self.cp('act', Vf[:], pbv[:], [pkv], [Vfk])
                    self.dma(os.environ.get('KOQ', 'sp'), v_out[own_out_row:own_out_row + 128, :], Vf[:], [Vfk], [])
                vas, vask = nxt('vasb', vasb)
                if own_out_row is not None:
                    self.cp('dve', vas[:, :, 0:64], Vf[:].rearrange("p (h d) -> p h d", h=H), [Vfk], [vask])
                else:
                    self.cp('dve', vas[:, :, 0:64], pbv[:].rearrange("p (h d) -> p h d", h=H), [pkv], [vask])
                self.dma('sp', va_scr[:, :, gt, :].rearrange("h p c -> p h c"), vas[:], [vask], ['va_scr'])

            def load_x(row0, ntile):
                for ti in range(ntile):
                    self.dma('sp', xres[:, ti, :], xk[row0 + ti * 128:row0 + (ti + 1) * 128, :], (), [('xres', ti)])

            npre = P // 128 - 1
            kslot, kskey = self.wpiece([w_cols(w_in_e, 1536, 512)])
            vslot, vskey = self.wpiece([w_cols(w_in_e, 2048, 512)])
            gt = 0
            while gt < npre:
                nt_ = min(4, npre - gt)
                load_x(gt * 128, nt_)
                for ti in range(nt_):
                    norm_T(xres[:, ti, :], ('xres', ti), gE[:], 'gE', ti)
                for ti in range(nt_):
                    kv_tile(gt + ti, ti, kslot, kskey, vslot, vskey, None)
                gt += nt_

            groups = [(P // 128 - 1, 1, None)]
            for g in range(C // GS):
                groups.append((P // 128 + g * (GS // 128), GS // 128, g * GS))

            def attention(gt0, ntile):
                ntok = ntile * 128
                nkt = gt0 + ntile
                for h in range(H):
                    ob, okk = self.obank()
                    nhalf = (nkt + 31) // 32
                    first = True
                    pend = None
                    for hf in range(nhalf):
                        k0 = hf * 32
                        k1 = min(nkt, k0 + 32)
                        i = self.kvi % 2
                        self.kvi += 1
                        ktb, vab = KTb[i], VAb[i]
                        self.dma('sp', ktb[:, 0:(k1 - k0) * 128], kt_scr[h, :, k0 * 128:k1 * 128], ['kt_scr'], [('KTb', i)])
                        self.dma('sp', vab[:, 0:k1 - k0, :], va_scr[h, :, k0:k1, :], ['va_scr'], [('VAb', i)])
                        for kt in range(k0, k1):
                            qlo = max(kt, gt0) - gt0
                            c0 = qlo * 128
                            sb_, sk = self.sbank()
                            has_d0 = kt >= gt0
                            has_c1 = (kt + 1 >= gt0) and (kt + 1 < gt0 + ntile)
                            self.mm(sb_[:, c0:ntok], ktb[:, (kt - k0) * 128:(kt - k0 + 1) * 128], QTaug[:, h, c0:ntok], True,
                                    not (has_d0 or has_c1), [('KTb', i), ('QTaug', h)], [sk])
                            if has_d0:
                                cc = (kt - gt0) * 128
                                self.mm(sb_[:, cc:cc + 128], ident[:], DC[:, h, 0, :], False, not has_c1, ['ident', 'DC'], [sk])
                            if has_c1:
                                cc = (kt + 1 - gt0) * 128
                                self.mm(sb_[:, cc:cc + 128], ident[:], DC[:, h, 1, :], False, True, ['ident', 'DC'], [sk])
                            pt, ptk = nxt('ptb', ptb)
                            self.act(pt[:, c0:ntok], sb_[:, c0:ntok], AF.Exp, [sk], [ptk])
                            if pend is not None:
                                self.mm(*pend[0], **pend[1])
                            pend = ((ob[:, c0:ntok], vab[:, kt - k0, :], pt[:, c0:ntok], first, kt == nkt - 1, [('VAb', i), ptk], [okk]), {})
                            first = False
                    if pend is not None:
                        self.mm(*pend[0], **pend[1])
                        pend = None
                    rec, rk = nxt('f512', f512)
                    self.recip(rec[0:64, 0:ntok], ob[64:128, 0:ntok], [okk], [rk])
                    po = (h % 2) * 64
                    self.tt('dve', catT[po:po + 64, 4 + h // 2, 0:ntok], ob[0:64, 0:ntok], rec[0:64, 0:ntok], ALU.mult,
                            [okk, rk], [('catT', 4 + h // 2, h % 2)])

            def out_proj(wsrc, ntile, catkeys):
                for half in range(2):
                    slot, skey = self.wpiece([w_cols(wsrc, half * 512, 512)])
                    sv = slot[:].rearrange("p (c n) -> p c n", c=8)
                    banks = [self.abank() for _ in range(ntile)]
                    for kc in range(8):
                        for ti in range(ntile):
                            self.mm(banks[ti][0][:], catT[:, kc, ti * 128:(ti + 1) * 128], sv[:, kc, :], kc == 0, kc == 7,
                                    catkeys(kc) + [skey], [banks[ti][1]])
                    for ti in range(ntile):
                        self.tt('dve', xres[:, ti, half * 512:(half + 1) * 512], banks[ti][0][:], xres[:, ti, half * 512:(half + 1) * 512],
                                ALU.add, [banks[ti][1], ('xres', ti)], [('xres', ti)])

            def ffn(l, ntile, first_own):
                ntok = ntile * 128
                for ti in range(ntile):
                    norm_T(xres[:, ti, :], ('xres', ti), gF[:, l, :], 'gF', ti)
                hkeys = [('hT', ti) for ti in range(ntile)]
                if first_own:
                    self.ts('dve', Ust[:, l, :, :], Ust[:, l, :, :], hv[:, 0, 0:1], None, ALU.mult, None, ['Ust', 'hvt'], ['Ust'])
                for c2 in range(NPAIR // 2):
                    slot, skey = self.wpiece([
                        (lambda s: s[:, 0:4096].rearrange("p (c n) -> p c n", c=8)[:, :, 0:256],
                         w_up[l].rearrange("(c p) n -> p c n", p=128)[:, :, c2 * 256:(c2 + 1) * 256]),
                        (lambda s: s[:, 0:4096].rearrange("p (c n) -> p c n", c=8)[:, :, 256:512],
                         w_up[l].rearrange("(c p) n -> p c n", p=128)[:, :, DFF + c2 * 256:DFF + (c2 + 1) * 256])])
                    sv = slot[:].rearrange("p (c n) -> p c n", c=8)
                    for cl in range(2):
                        c = c2 * 2 + cl
                        res = []
                        for part in range(2):
                            ci = c + part * NPAIR
                            pb, pk = self.gbank()
                            for kc in range(8):
                                self.mm(pb[:, 0:ntok], sv[:, kc, part * 256 + cl * 128: part * 256 + (cl + 1) * 128], hT[:, kc, 0:ntok],
                                        kc == 0, kc == 7, hkeys + [skey], [pk])
                            U, Uk = nxt('ubuf', self.ubuf)
                            self.cp('pool', U[:, 0:2], Ust[:, l, ci, :], ['Ust'], [Uk])
                            self.cp('act', U[:, 2:2 + ntok], pb[:, 0:ntok], [pk], [Uk])
                            self.cp('pool', Ust[:, l, ci, :], U[:, ntok:ntok + 2], [Uk], ['Ust'])
                            cv, cvk = nxt('f512', f512)
                            self.act(cv[:, 0:ntok], U[:, 2:2 + ntok], AF.Identity, [Uk, 'cfw', 'cfb'], [cvk],
                                     scale=cfw[:, l, ci, 2:3], bias=cfb[:, l, ci:ci + 1])
                            self.stt(cv[:, 0:ntok], U[:, 1:1 + ntok], cfw[:, l, ci, 1:2], cv[:, 0:ntok], ALU.mult, ALU.add, [Uk, cvk, 'cfw'], [cvk])
                            self.stt(cv[:, 0:ntok], U[:, 0:ntok], cfw[:, l, ci, 0:1], cv[:, 0:ntok], ALU.mult, ALU.add, [Uk, cvk, 'cfw'], [cvk])
                            res.append((cv, cvk))
                        (cg, cgk), (cu, cuk) = res
                        self.act(cg[:, 0:ntok], cg[:, 0:ntok], AF.Silu, [cgk], [cgk])
                        self.tt('dve', mT[:, c, 0:ntok], cg[:, 0:ntok], cu[:, 0:ntok], ALU.mult, [cgk, cuk], mkeys(c))
                ffn_down(l, ntile)

            def ffn_down(l, ntile):
                for half in range(2):
                    banks = [self.abank() for _ in range(ntile)]
                    for pi, (ca, cb) in enumerate(((0, 8), (8, 16), (16, 22))):
                        slot, skey = self.wpiece([(lambda s, n=cb - ca: s[:, 0:n * 512].rearrange("p (c n) -> p c n", c=n),
                                                   w_down[l, ca * 128:cb * 128, half * 512:(half + 1) * 512].rearrange("(c p) n -> p c n", p=128))])
                        sv = slot[:, 0:(cb - ca) * 512].rearrange("p (c n) -> p c n", c=cb - ca)
                        for c in range(ca, cb):
                            for ti in range(ntile):
                                self.mm(banks[ti][0][:], mT[:, c, ti * 128:(ti + 1) * 128], sv[:, c - ca, :], c == 0, c == NPAIR - 1,
                                        mkeys(c) + [skey], [banks[ti][1]])
                    for ti in range(ntile):
                        self.tt('dve', xres[:, ti, half * 512:(half + 1) * 512], banks[ti][0][:], xres[:, ti, half * 512:(half + 1) * 512],
                                ALU.add, [banks[ti][1], ('xres', ti)], [('xres', ti)])

            def ln_silu(cvs, ntok):
                pbm, pkm = self.gbank()
                pbs, pks = self.gbank()
                sqs = []
                for cc in range(4):
                    sq, sqk = nxt('f512', f512)
                    self.act(sq[:, 0:ntok], cvs[cc][0][:, 0:ntok], AF.Square, [cvs[cc][1]], [sqk])
                    sqs.append((sq, sqk))
                for cc in range(4):
                    self.mm(pbm[:, 0:ntok], onesf[:], cvs[cc][0][:, 0:ntok], cc == 0, cc == 3, ['onesf', cvs[cc][1]], [pkm])
                for cc in range(4):
                    self.mm(pbs[:, 0:ntok], onesf[:], sqs[cc][0][:, 0:ntok], cc == 0, cc == 3, ['onesf', sqs[cc][1]], [pks])
                mean, meank = nxt('f512', f512)
                self.ts('dve', mean[:, 0:ntok], pbm[:, 0:ntok], 1.0 / DA, None, ALU.mult, None, [pkm], [meank])
                var, vark = sqs[0]
                self.tt('dve', var[:, 0:ntok], mean[:, 0:ntok], mean[:, 0:ntok], ALU.mult, [meank], [vark])
                self.stt(var[:, 0:ntok], pbs[:, 0:ntok], 1.0 / DA, var[:, 0:ntok], ALU.mult, ALU.subtract, [pks, vark], [vark])
                self.act(var[:, 0:ntok], var[:, 0:ntok], AF.Sqrt, [vark], [vark], bias=eps_t[:, 0:1], scale=1.0)
                self.recip(var[:, 0:ntok], var[:, 0:ntok], [vark], [vark])
                for cc in range(4):
                    cv, cvk = cvs[cc]
                    self.tt('dve', cv[:, 0:ntok], cv[:, 0:ntok], mean[:, 0:ntok], ALU.subtract, [cvk, meank], [cvk])
                    self.tt('dve', cv[:, 0:ntok], cv[:, 0:ntok], var[:, 0:ntok], ALU.mult, [cvk, vark], [cvk])
                    self.act(catT[:, cc, 0:ntok], cv[:, 0:ntok], AF.Silu, [cvk, 'lng', 'lnb'], [('catT', cc, 0), ('catT', cc, 1)],
                             scale=lng[:, cc:cc + 1], bias=lnb[:, cc:cc + 1])

            self.marks = []
            mark = lambda n: self.marks.append((n, len(S.ops)))
            for (gt0, ntile, orow) in groups:
                ntok = ntile * 128
                mark('group %d start' % gt0)
                is_halo = orow is None
                first_own = (orow == 0)
                load_x(gt0 * 128, ntile)
                for ti in range(ntile):
                    norm_T(xres[:, ti, :], ('xres', ti), gE[:], 'gE', ti)
                hkeys = [('hT', ti) for ti in range(ntile)]
                kslot, kskey = self.wpiece([w_cols(w_in_e, 1536, 512)])
                vslot, vskey = self.wpiece([w_cols(w_in_e, 2048, 512)])
                for ti in range(ntile):
                    kv_tile(gt0 + ti, ti, kslot, kskey, vslot, vskey, (0 if os.environ.get("KHALO") else None) if is_halo else orow + ti * 128)
                mark('kv done')
                self.tt('dve', kmT[:], tsum[:].rearrange("p h (n two) -> p h n two", two=2)[:, :, :, 0],
                        tsum[:].rearrange("p h (n two) -> p h n two", two=2)[:, :, :, 1], ALU.add, ['tsum'], ['kmT'])
                qslot, qskey = self.wpiece([w_cols(w_in_e, 1024, 512)])
                for ti in range(ntile):
                    own = (gt0 + ti) // 2
                    pb, pk = tok_mm(qslot, qskey, ti)
                    Qf, Qfk = head_norm(pb, pk, gqB, 'gqB', HD ** -0.5)
                    qt_, qtk = nxt('qtf', qtf)
                    for hh2 in range(2):
                        pbq, pkq = self.gbank()
                        for h4 in range(4):
                            h = hh2 * 4 + h4
                            self.tr(pbq[0:64, h4 * 128:(h4 + 1) * 128], Qf[:, h * 64:(h + 1) * 64], identf[:], [Qfk, 'identf'], [pkq])
                        self.cp('act', qt_[:, hh2 * 4:(hh2 + 1) * 4, :], pbq[0:64, :].rearrange("p (h t) -> p h t", h=4), [pkq], [qtk])
                    self.cp('pool', QTaug[0:64, :, ti * 128:(ti + 1) * 128], qt_[:], [qtk], [('QTaug', h) for h in range(H)])
                    pbg, pkg = self.gbank()
                    for h in range(H):
                        self.mm(pbg[:, h * NB:(h + 1) * NB], qt_[:, h, :], kmT[:, h, :], True, True, [qtk, 'kmT'], [pkg])
                    g2, g2k = nxt('g2p', g2p)
                    self.tt('dve', g2[:], pbg[:, 0:H * NB].rearrange("p (h n) -> p h n", h=H),
                            gmask[:, 0, own:own + 1, :].to_broadcast([128, H, NB]), ALU.add, [pkg, 'gmaskt'], [g2k])
                    t8, t8k = nxt('t8p', t8p)
                    for h in range(H):
                        S.op('dve', (lambda e, o=t8[:, h, :], i=g2[:, h, :]: e.max(out=o, in_=i)), [g2k], [t8k])
                    c1, c1k = nxt('g2p', g2p)
                    self.tt('dve', c1[:], g2[:], t8[:, :, 2:3].to_broadcast([128, H, NB]), ALU.is_ge, [g2k, t8k], [c1k])
                    self.ts('dve', g2[:], g2[:], NEG, None, ALU.is_gt, None, [g2k], [g2k])
                    self.tt('dve', c1[:], c1[:], g2[:], ALU.mult, [c1k, g2k], [c1k])
                    self.tt('dve', c1[:], c1[:], scB[:].unsqueeze(2).to_broadcast([128, H, NB]), ALU.mult, [c1k, 'scB'], [c1k])
                    nm, nmk = nxt('nmt', nmt)
                    self.ts('dve', nm[:, :, 0:NB], c1[:], NEG, None, ALU.add, None, [c1k], [nmk])
                    self.cp('dve', nm[:, :, own], b31B[:, 0, :], ['b31B', nmk], [nmk])
                    pbn, pkn = self.gbank()
                    pn = pbn[:].bitcast(BF16)
                    for h in range(H):
                        self.tr(pn[0:32, h * 128:(h + 1) * 128], nm[:, h, :], ident[:], [nmk, 'ident'], [pkn])
                    self.cp('act', QTaug[64:96, :, ti * 128:(ti + 1) * 128], pn[0:32, :].rearrange("p (h t) -> p h t", h=H),
                            [pkn], [('QTaug', h) for h in range(H)])
                mark('q done')
                if first_own:
                    self.ts('dve', Ast[:], Ast[:], hv[:, 0, 0:1], None, ALU.mult, None, ['Ast', 'hvt'], ['Ast'])
                valslot, valk = self.wpiece([w_cols(w_in_e, 0, 512)])
                gateslot, gatek = self.wpiece([w_cols(w_in_e, 512, 512)])
                vv = valslot[:].rearrange("p (c n) -> p c n", c=8)
                gv = gateslot[:].rearrange("p (c n) -> p c n", c=8)
                cvs = []
                for cc in range(4):
                    pbv, pkv = self.gbank()
                    pbg, pkg = self.gbank()
                    for kc in range(8):
                        self.mm(pbv[:, 0:ntok], vv[:, kc, cc * 128:(cc + 1) * 128], hT[:, kc, 0:ntok], kc == 0, kc == 7, hkeys + [valk], [pkv])
                    for kc in range(8):
                        self.mm(pbg[:, 0:ntok], gv[:, kc, cc * 128:(cc + 1) * 128], hT[:, kc, 0:ntok], kc == 0, kc == 7, hkeys + [gatek], [pkg])
                    sg, sgk = nxt('f512', f512)
                    self.act(sg[:, 0:ntok], pbg[:, 0:ntok], AF.Sigmoid, [pkg], [sgk])
                    ab, abk = nxt('abuf', abuf)
                    self.cp('pool', ab[:, 0:30], Ast[:, cc, :], ['Ast'], [abk])
                    self.tt('dve', ab[:, 30:30 + ntok], pbv[:, 0:ntok], sg[:, 0:ntok], ALU.mult, [pkv, sgk], [abk])
                    self.cp('pool', Ast[:, cc, :], ab[:, ntok:ntok + 30], [abk], ['Ast'])
                    cv, cvk = nxt('cvA', self.cvA)
                    self.act(cv[:, 0:ntok], ab[:, 30:30 + ntok], AF.Identity, [abk, 'caw', 'cab'], [cvk],
                             scale=caw[:, cc, 30:31], bias=cab[:, cc:cc + 1])
                    for j in range(30):
                        self.stt(cv[:, 0:ntok], ab[:, j:j + ntok], caw[:, cc, j:j + 1], cv[:, 0:ntok], ALU.mult, ALU.add,
                                 [abk, cvk, 'caw'], [cvk])
                    cvs.append((cv, cvk))
                ln_silu(cvs, ntok)
                mark('mixerA done')
                attention(gt0, ntile)
                mark('attn done')
                out_proj(w_out_e, ntile, lambda kc: [('catT', kc, 0), ('catT', kc, 1)])
                mark('outproj done')
                ffn(0, ntile, first_own)
                mark('ffn0 done')
                for ti in range(ntile):
                    norm_T(xres[:, ti, :], ('xres', ti), gO[:], 'gO', ti)
                if first_own:
                    self.ts('dve', Cst[:], Cst[:], hv[:, 0, 0:1], None, ALU.mult, None, ['Cst', 'hvt'], ['Cst'])
                for c in range(8):
                    slot, skey = self.wpiece([
                        (lambda s, k=k: s[:, 0:8 * 384].rearrange("p (c n) -> p c n", c=8)[:, :, k * 128:(k + 1) * 128],
                         w_in_o.rearrange("(c p) n -> p c n", p=128)[:, :, k * 1024 + c * 128:k * 1024 + (c + 1) * 128])
                        for k in range(3)])
                    sv = slot[:, 0:8 * 384].rearrange("p (c n) -> p c n", c=8)
                    pbs_ = []
                    for k in range(3):
                        pb, pk = self.gbank()
                        for kc in range(8):
                            self.mm(pb[:, 0:ntok], sv[:, kc, k * 128:(k + 1) * 128], hT[:, kc, 0:ntok], kc == 0, kc == 7, hkeys + [skey], [pk])
                        pbs_.append((pb, pk))
                    uu, uuk = nxt('f512', f512)
                    self.cp('act', uu[:, 0:ntok], pbs_[2][0][:, 0:ntok], [pbs_[2][1]], [uuk])
                    cb_, cbk = nxt('ubuf', self.ubuf)
                    self.cp('pool', cb_[:, 0:2], Cst[:, c, :], ['Cst'], [cbk])
                    self.tt('dve', cb_[:, 2:2 + ntok], pbs_[1][0][:, 0:ntok], uu[:, 0:ntok], ALU.mult, [pbs_[1][1], uuk], [cbk])
                    self.cp('pool', Cst[:, c, :], cb_[:, ntok:ntok + 2], [cbk], ['Cst'])
                    cv, cvk = nxt('f512', f512)
                    self.act(cv[:, 0:ntok], cb_[:, 2:2 + ntok], AF.Identity, [cbk, 'ccw'], [cvk], scale=ccw[:, c, 2:3])
                    self.stt(cv[:, 0:ntok], cb_[:, 1:1 + ntok], ccw[:, c, 1:2], cv[:, 0:ntok], ALU.mult, ALU.add, [cbk, cvk, 'ccw'], [cvk])
                    self.stt(cv[:, 0:ntok], cb_[:, 0:ntok], ccw[:, c, 0:1], cv[:, 0:ntok], ALU.mult, ALU.add, [cbk, cvk, 'ccw'], [cvk])
                    self.tt('dve', catT[:, c, 0:ntok], pbs_[0][0][:, 0:ntok], cv[:, 0:ntok], ALU.mult, [pbs_[0][1], cvk],
                            [('catT', c, 0), ('catT', c, 1)])
                out_proj(w_out_o, ntile, lambda kc: [('catT', kc, 0), ('catT', kc, 1)])
                ffn(1, ntile, first_own)
                if not is_halo:
                    for ti in range(ntile):
                        self.dma('sp', y_out[orow + ti * 128:orow + (ti + 1) * 128, :], xres[:, ti, :], [('xres', ti)], [])


            def wslot_take():
                i = self.wi % self.NSLOT
                self.wi += 1
                return self.wsl[i], ('w', i)

            def sample_phase():
                def cload(name, shape, src, dt=F32, q='act'):
                    t = self.sb(name + "_t", shape, dt)
                    self.dma(q, t[:], src, (), [name])
                    return t
                oh16 = cload("oh16", [30, 256], oh16_d)
                W30 = cload("W30", [30, DA], conv_a_w[0:30, :])
                ohS = cload("ohS", [33, 256], ohS_d)
                ptb = self.sb("ptb_t", [128, 1, 128], I32)
                pt3 = pt_d.rearrange("o (sn two) -> o sn two", two=2)
                self.dma('act', ptb[0:64], pt3[:, :, 0].partition_broadcast(64), (), ['ptb'], slow=True)
                self.dma('act', ptb[64:128], pt3[:, :, 1].partition_broadcast(64), (), ['ptb'], slow=True)
                rb0B = cload("rb0B", [128, 1, H], rel_bias[0:1, :].partition_broadcast(128))
                io = self.sb("io", [128, 1], I32)
                iof = self.sb("iof", [128, 1], F32)
                idx = self.sb("idx", [128, 128], I32)
                S.op('pool', lambda e: e.iota(io[0:64, :], pattern=[[0, 1]], base=0, channel_multiplier=1), (), ['io'])
                S.op('pool', lambda e: e.iota(io[64:128, :], pattern=[[0, 1]], base=0, channel_multiplier=1), (), ['io'])
                self.cp('pool', iof[:], io[:], ['io'], ['iof'])
                self.ts('dve', idx[:], ptb[:, 0, :], 64.0, iof[:, 0:1], ALU.mult, ALU.add, ['ptb', 'iof'], ['idx'])
                self.tt('dve', rb0B[:, 0, :], rb0B[:, 0, :], b31B[:, 0, :], ALU.subtract, ['rb0B', 'b31B'], ['rb0B'])
                biasP = self.sb("biasP", [128, 2 * H], F32)
                pb, pk = self.gbank()
                for slot in range(2):
                    self.mm(pb[:, slot * H:(slot + 1) * H], ohS[:, slot * 128:(slot + 1) * 128], rb[:], True, True, ['ohS', 'rb'], [pk])
                self.cp('dve', biasP[:], pb[:, 0:2 * H], [pk], ['biasP'])
                VS = self.sb("VS", [128, 512], F32)
                AsT = self.sb("AsT", [128, 4, 16], F32)
                small = self.sb("smalls", [16, 64], F32)
                self.dma('sp', xres[0:16, 0, :], xs_d, (), [('xres', 0)])
                norm_T(xres[:, 0, :], ('xres', 0), gE[:], 'gE', 0)
                hk0 = [('hT', 0)]
                kslot, kskey = self.wpiece([w_cols(w_in_e, 1536, 512)])
                pb, pk = tok_mm(kslot, kskey, 0)
                Kf, Kfk = head_norm(pb, pk, gkB, 'gkB', 1.0)
                self.dma('sp', ks_out, Kf[0:16, :], [Kfk], [])
                vslot, vskey = self.wpiece([w_cols(w_in_e, 2048, 512)])
                pbv, pkv = tok_mm(vslot, vskey, 0)
                self.cp('act', VS[:], pbv[:], [pkv], ['VS'])
                self.dma('sp', vs_out, VS[0:16, :], ['VS'], [])
                qslot, qskey = self.wpiece([w_cols(w_in_e, 1024, 512)])
                pb, pk = tok_mm(qslot, qskey, 0)
                Qf, Qfk = head_norm(pb, pk, gqB, 'gqB', HD ** -0.5)
                self.dma('sp', qs_scr, Qf[0:16, :], [Qfk], ['qs_scr'])
                lself = small[:, 0:8]
                tmpq, tmpqk = nxt('f512', f512)
                self.tt('dve', tmpq[0:16, :], Qf[0:16, :], Kf[0:16, :], ALU.mult, [Qfk, Kfk], [tmpqk])
                self.red(lself, tmpq[0:16, :].rearrange("p (h d) -> p h d", h=H), ALU.add, [tmpqk], ['small'])
                valslot, valk = self.wpiece([w_cols(w_in_e, 0, 512)])
                gateslot, gatek = self.wpiece([w_cols(w_in_e, 512, 512)])
                vv = valslot[:].rearrange("p (c n) -> p c n", c=8)
                gv = gateslot[:].rearrange("p (c n) -> p c n", c=8)
                for cc in range(4):
                    pbv, pkv = self.gbank()
                    pbg, pkg = self.gbank()
                    for kc in range(8):
                        self.mm(pbv[:, 0:16], vv[:, kc, cc * 128:(cc + 1) * 128], hT[:, kc, 0:16], kc == 0, kc == 7, hk0 + [valk], [pkv])
                    for kc in range(8):
                        self.mm(pbg[:, 0:16], gv[:, kc, cc * 128:(cc + 1) * 128], hT[:, kc, 0:16], kc == 0, kc == 7, hk0 + [gatek], [pkg])
                    sg, sgk = nxt('f512', f512)
                    self.act(sg[:, 0:16], pbg[:, 0:16], AF.Sigmoid, [pkg], [sgk])
                    self.tt('dve', AsT[:, cc, :], pbv[:, 0:16], sg[:, 0:16], ALU.mult, [pkv, sgk], ['AsT'])
                accb, acck = self.gbank()
                for s_ in range(16):
                    stA, stAk = nxt('f512', f512)
                    self.dma('sp', stA[0:30, :], sta_d[s_], (), [stAk])
                    self.tt('dve', stA[0:30, :], stA[0:30, :], W30[:], ALU.mult, [stAk, 'W30'], [stAk])
                    self.mm(accb[0:16, :], oh16[:, s_ * 16:(s_ + 1) * 16], stA[0:30, :], s_ == 0, s_ == 15, ['oh16', stAk], [acck])
                cst, cstk = nxt('f512', f512)
                self.cp('dve', cst[0:16, :], accb[0:16, :], [acck], [cstk])
                pbt, pkt = self.gbank()
                for cc in range(4):
                    self.tr(pbt[:, cc * 16:(cc + 1) * 16], cst[0:16, cc * 128:(cc + 1) * 128], identf[0:16, 0:16], [cstk, 'identf'], [pkt])
                cvs = []
                for cc in range(4):
                    cv, cvk = nxt('cvA', self.cvA)
                    self.act(cv[:, 0:16], AsT[:, cc, :], AF.Identity, ['AsT', 'caw', 'cab'], [cvk], scale=caw[:, cc, 30:31], bias=cab[:, cc:cc + 1])
                    self.tt('dve', cv[:, 0:16], cv[:, 0:16], pbt[:, cc * 16:(cc + 1) * 16], ALU.add, [cvk, pkt], [cvk])
                    cvs.append((cv, cvk))
                ln_silu(cvs, 16)
                self.dma('act', cas_out[:, 0:29, :], sta_d[:, 1:30, :], (), [])
                pba, pka = self.gbank()
                for cc in range(4):
                    self.tr(pba[0:16, cc * 128:(cc + 1) * 128], AsT[:, cc, :], identf[:], ['AsT', 'identf'], [pka])
                atok, atokk = nxt('f512', f512)
                self.cp('dve', atok[0:16, :], pba[0:16, :], [pka], [atokk])
                self.dma('sp', cas_out[:, 29, :], atok[0:16, :], [atokk], [])
                lself = small[:, 0:8]
                denAll = small[:, 8:16]
                dtot = small[:, 16:24]
                self.tt('dve', lself, lself, rb0B[0:16, 0, :], ALU.add, ['small', 'rb0B'], ['small'])
                self.act(lself, lself, AF.Exp, ['small'], ['small'])
                self.memset('pool', denAll, 0.0, ['small'])
                bufA = big[:, 0:8192].bitcast(F32).rearrange("p (g n) -> p g n", g=8)
                bufB = big[:, 8192:16384].bitcast(F32).rearrange("p (g n) -> p g n", g=8)
                bufs = [(bufA, [('KTb', 0), ('KTb', 1)]), (bufB, [('VAb', 0), ('VAb', 1)])]
                self.bi = 0

                def gather(src, s_, half):
                    bf_, bk_ = bufs[self.bi % 2]
                    self.bi += 1
                    for bl in range(4):
                        col = s_ * 8 + half * 4 + bl
                        S.op('pool', (lambda e, o=bf_[:, 2 * bl:2 * bl + 2, :].rearrange("p a n -> p (a n)"), ix=idx[:, col:col + 1]:
                                      e.indirect_dma_start(out=o, out_offset=None, in_=src,
                                                           in_offset=bass.IndirectOffsetOnAxis(ap=ix, axis=0))),
                             ['idx'], bk_, dma=True)
                    return bf_, bk_
                vbb = [wslot_take() for _ in range(2)]
                pzs, pzk = wslot_take()
                Pz = pzs[:].rearrange("p (pg hg c) -> p pg hg c", pg=16, hg=2)
                Pz4 = pzs[:].rearrange("p (pg h sl) -> p pg h sl", pg=16, h=H)
                self.memset('pool', pzs[:], 0.0, [pzk])
                Lt = self.sb("Lt", [128, 128], F32)
                Pf = self.sb("Pf", [128, 128], F32)
                gsb = self.sb("gsb", [128, 8, 8], F32)
                c1b = self.sb("c1b", [128, 8, 8], F32)
                t8s = self.sb("t8s", [128, H, 8], F32)
                den8 = self.sb("den8", [128, H], F32)
                ob0, ok0 = self.ps[6], ('ps', 6)
                ob1, ok1 = self.ps[7], ('ps', 7)
                obs = [(ob0, ok0), (ob1, ok1)]
                for s_ in range(16):
                    qb, qbk = nxt('f512', f512)
                    self.dma('sp', qb[:].rearrange("p (o n) -> p o n", o=1), qs_scr[s_:s_ + 1, :].partition_broadcast(128), ['qs_scr'], [qbk])
                    for half in range(2):
                        Kb, Kbk = gather(ck_d, s_, half)
                        self.tt('dve', Kb, Kb, qb[:].unsqueeze(1).to_broadcast([128, 8, 512]), ALU.mult, Kbk + [qbk], Kbk)
                        self.red(Lt[:, half * 64:(half + 1) * 64], Kb.rearrange("p g (h d) -> p (g h) d", h=H), ALU.add, Kbk, ['Lt'])
                    pbg, pkg = self.gbank()
                    self.mm(pbg[:, 0:128], onesf[:], Lt[:], True, True, ['onesf', 'Lt'], [pkg])
                    G4 = pbg[:, 0:128].rearrange("p (n two h) -> p n two h", n=8, two=2)
                    self.cp('dve', gsb[:], G4[:, :, 0, :], [pkg], ['gsb'])
                    self.tt('dve', gsb[:], gsb[:], G4[:, :, 1, :], ALU.add, [pkg, 'gsb'], ['gsb'])
                    for h in range(H):
                        S.op('dve', (lambda e, o=t8s[:, h, :], i=gsb[:, :, h]: e.max(out=o, in_=i)), ['gsb'], ['t8s'])
                    self.tt('dve', c1b[:], gsb[:], t8s[:, :, 2].unsqueeze(1).to_broadcast([128, 8, H]), ALU.is_ge, ['gsb', 't8s'], ['c1b'])
                    self.ts('dve', c1b[:], c1b[:], -NEG, NEG, ALU.mult, ALU.add, ['c1b'], ['c1b'])
                    L4 = Lt[:].rearrange("p (n two h) -> p n two h", n=8, two=2)
                    self.tt('dve', L4, L4, c1b[:].unsqueeze(2).to_broadcast([128, 8, 2, H]), ALU.add, ['Lt', 'c1b'], ['Lt'])
                    self.tt('dve', Lt[:, 112:128], Lt[:, 112:128], biasP[:], ALU.add, ['Lt', 'biasP'], ['Lt'])
                    self.act(Pz4[:, :, :, s_], Lt[:].rearrange("p (pg h) -> p pg h", pg=16), AF.Exp, ['Lt'], [pzk])
                    self.cp('dve', Pf[:].rearrange("p (pg h) -> p pg h", pg=16), Pz4[:, :, :, s_], [pzk], ['Pf'])
                    pbd, pkd = self.gbank()
                    self.mm(pbd[:, 0:128], onesf[:], Pf[:], True, True, ['onesf', 'Pf'], [pkd])
                    self.red(den8[:], pbd[:, 0:128].rearrange("p (pg h) -> p h pg", pg=16), ALU.add, [pkd], ['den8'])
                    self.stt(denAll, den8[0:16, :], identf[0:16, s_:s_ + 1], denAll, ALU.mult, ALU.add, ['den8', 'identf', 'small'], ['small'])
                    for half in range(2):
                        vb_, vbk_ = vbb[half]
                        vb3 = vb_[:].rearrange("p (g n) -> p g n", g=8)
                        Vb, Vbk = gather(cv_d, s_, half)
                        self.cp('act', vb3, Vb, Vbk, [vbk_])
                        for pg in range(8):
                            page = half * 8 + pg
                            for hg in range(2):
                                self.mm(obs[hg][0][:], Pz[:, page, hg, :], vb3[:, pg, :], (s_ == 0 and page == 0), (s_ == 15 and page == 15),
                                        [pzk, vbk_], [obs[hg][1]])
                    S.op('dve', (lambda e, o=Pz4[:, :, :, s_]: e.memset(o, 0.0)), (), [pzk])
                otok_t, otokk = nxt('f512', f512)
                otok = otok_t[0:16, :]
                for h in range(H):
                    hg, hl = h // 4, h % 4
                    self.cp('dve', otok[:, h * 64:(h + 1) * 64], obs[hg][0][32 * hl:32 * hl + 16, h * 64:(h + 1) * 64], [obs[hg][1]], [otokk])
                o3 = otok.rearrange("p (h d) -> p h d", h=H)
                tv, tvk = nxt('f512', f512)
                tv3 = tv[0:16, :].rearrange("p (h d) -> p h d", h=H)
                self.tt('dve', tv3, VS[0:16, :].rearrange("p (h d) -> p h d", h=H), lself.unsqueeze(2).to_broadcast([16, H, HD]), ALU.mult,
                        ['VS', 'small'], [tvk])
                self.tt('dve', o3, o3, tv3, ALU.add, [otokk, tvk], [otokk])
                self.tt('dve', dtot, denAll, lself, ALU.add, ['small'], ['small'])
                self.recip(dtot, dtot, ['small'], ['small'])
                self.tt('dve', o3, o3, dtot.unsqueeze(2).to_broadcast([16, H, HD]), ALU.mult, [otokk, 'small'], [otokk])
                pbo, pko = self.gbank()
                for cc in range(4):
                    self.tr(pbo[:, cc * 16:(cc + 1) * 16], otok[:, cc * 128:(cc + 1) * 128], identf[0:16, 0:16], [otokk, 'identf'], [pko])
                self.cp('dve', catT[:, 4:8, 0:16], pbo[:, 0:64].rearrange("p (c t) -> p c t", c=4), [pko],
                        [('catT', 4 + c_, k_) for c_ in range(4) for k_ in range(2)])
                out_proj(w_out_e, 1, lambda kc: [('catT', kc, 0), ('catT', kc, 1)])
                ffn_s(0)
                norm_T(xres[:, 0, :], ('xres', 0), gO[:], 'gO', 0)
                stcT = self.sb("stcT", [128, 8, 2, 16], F32)
                cuT = self.sb("cuT", [128, 8, 16], F32)
                for j in range(2):
                    for hh in range(2):
                        tl, tlk = nxt('f512', f512)
                        self.dma('sp', tl[0:16, :], stc_d[:, j, hh * 512:(hh + 1) * 512], (), [tlk])
                        pbt, pkt = self.gbank()
                        for c4 in range(4):
                            self.tr(pbt[:, c4 * 16:(c4 + 1) * 16], tl[0:16, c4 * 128:(c4 + 1) * 128], identf[0:16, 0:16], [tlk, 'identf'], [pkt])
                        self.cp('dve', stcT[:, hh * 4:(hh + 1) * 4, j, :], pbt[:, 0:64].rearrange("p (c t) -> p c t", c=4), [pkt], ['stcT'])
                for c in range(8):
                    slot, skey = self.wpiece([
                        (lambda s, k=k: s[:, 0:8 * 384].rearrange("p (c n) -> p c n", c=8)[:, :, k * 128:(k + 1) * 128],
                         w_in_o.rearrange("(c p) n -> p c n", p=128)[:, :, k * 1024 + c * 128:k * 1024 + (c + 1) * 128])
                        for k in range(3)])
                    sv = slot[:, 0:8 * 384].rearrange("p (c n) -> p c n", c=8)
                    pbs_ = []
                    for k in range(3):
                        pb, pk = self.gbank()
                        for kc in range(8):
                            self.mm(pb[:, 0:16], sv[:, kc, k * 128:(k + 1) * 128], hT[:, kc, 0:16], kc == 0, kc == 7, hk0 + [skey], [pk])
                        pbs_.append((pb, pk))
                    uu, uuk = nxt('f512', f512)
                    self.cp('act', uu[:, 0:16], pbs_[2][0][:, 0:16], [pbs_[2][1]], [uuk])
                    self.tt('dve', cuT[:, c, :], pbs_[1][0][:, 0:16], uu[:, 0:16], ALU.mult, [pbs_[1][1], uuk], ['cuT'])
                    cv, cvk = nxt('f512', f512)
                    self.act(cv[:, 0:16], cuT[:, c, :], AF.Identity, ['cuT', 'ccw'], [cvk], scale=ccw[:, c, 2:3])
                    self.stt(cv[:, 0:16], stcT[:, c, 1, :], ccw[:, c, 1:2], cv[:, 0:16], ALU.mult, ALU.add, ['stcT', cvk, 'ccw'], [cvk])
                    self.stt(cv[:, 0:16], stcT[:, c, 0, :], ccw[:, c, 0:1], cv[:, 0:16], ALU.mult, ALU.add, ['stcT', cvk, 'ccw'], [cvk])
                    self.tt('dve', catT[:, c, 0:16], pbs_[0][0][:, 0:16], cv[:, 0:16], ALU.mult, [pbs_[0][1], cvk],
                            [('catT', c, 0), ('catT', c, 1)])
                self.dma('act', ccs_out[:, 0, :], stc_d[:, 1, :], (), [])
                for hh in range(2):
                    pbc, pkc = self.gbank()
                    for c4 in range(4):
                        self.tr(pbc[0:16, c4 * 128:(c4 + 1) * 128], cuT[:, hh * 4 + c4, :], identf[:], ['cuT', 'identf'], [pkc])
                    ct, ctk = nxt('f512', f512)
                    self.cp('dve', ct[0:16, :], pbc[0:16, :], [pkc], [ctk])
                    self.dma('sp', ccs_out[:, 1, hh * 512:(hh + 1) * 512], ct[0:16, :], [ctk], [])
                out_proj(w_out_o, 1, lambda kc: [('catT', kc, 0), ('catT', kc, 1)])
                ffn_s(1)
                self.dma('sp', ys_out, xres[0:16, 0, :], [('xres', 0)], [])

            def ffn_s(l):
                norm_T(xres[:, 0, :], ('xres', 0), gF[:, l, :], 'gF', 0)
                hk0 = [('hT', 0)]
                stfT = self.stfT
                upT = self.upT
                for j in range(2):
                    for blk in range(11):
                        tl, tlk = nxt('f512', f512)
                        self.dma('sp', tl[0:16, :], stf_d[l, :, j, blk * 512:(blk + 1) * 512], (), [tlk])
                        pbt, pkt = self.gbank()
                        for c4 in range(4):
                            self.tr(pbt[:, c4 * 16:(c4 + 1) * 16], tl[0:16, c4 * 128:(c4 + 1) * 128], identf[0:16, 0:16], [tlk, 'identf'], [pkt])
                        self.cp('dve', stfT[:, blk * 4:(blk + 1) * 4, j, :], pbt[:, 0:64].rearrange("p (c t) -> p c t", c=4), [pkt], ['stfT'])
                for c2 in range(NPAIR // 2):
                    slot, skey = self.wpiece([
                        (lambda s: s[:, 0:4096].rearrange("p (c n) -> p c n", c=8)[:, :, 0:256],
                         w_up[l].rearrange("(c p) n -> p c n", p=128)[:, :, c2 * 256:(c2 + 1) * 256]),
                        (lambda s: s[:, 0:4096].rearrange("p (c n) -> p c n", c=8)[:, :, 256:512],
                         w_up[l].rearrange("(c p) n -> p c n", p=128)[:, :, DFF + c2 * 256:DFF + (c2 + 1) * 256])])
                    sv = slot[:].rearrange("p (c n) -> p c n", c=8)
                    for cl in range(2):
                        c = c2 * 2 + cl
                        res = []
                        for part in range(2):
                            ci = c + part * NPAIR
                            pb, pk = self.gbank()
                            for kc in range(8):
                                self.mm(pb[:, 0:16], sv[:, kc, part * 256 + cl * 128: part * 256 + (cl + 1) * 128], hT[:, kc, 0:16],
                                        kc == 0, kc == 7, hk0 + [skey], [pk])
                            self.cp('act', upT[:, ci, :], pb[:, 0:16], [pk], ['upT'])
                            cv, cvk = nxt('f512', f512)
                            self.act(cv[:, 0:16], upT[:, ci, :], AF.Identity, ['upT', 'cfw', 'cfb'], [cvk],
                                     scale=cfw[:, l, ci, 2:3], bias=cfb[:, l, ci:ci + 1])
                            self.stt(cv[:, 0:16], stfT[:, ci, 1, :], cfw[:, l, ci, 1:2], cv[:, 0:16], ALU.mult, ALU.add, ['stfT', cvk, 'cfw'], [cvk])
                            self.stt(cv[:, 0:16], stfT[:, ci, 0, :], cfw[:, l, ci, 0:1], cv[:, 0:16], ALU.mult, ALU.add, ['stfT', cvk, 'cfw'], [cvk])
                            res.append((cv, cvk))
                        (cg, cgk), (cu, cuk) = res
                        self.act(cg[:, 0:16], cg[:, 0:16], AF.Silu, [cgk], [cgk])
                        self.tt('dve', mT[:, c, 0:16], cg[:, 0:16], cu[:, 0:16], ALU.mult, [cgk, cuk], mkeys(c))
                self.dma('act', fs_out[l, :, 0, :], stf_d[l, :, 1, :], (), [])
                for blk in range(11):
                    pbc, pkc = self.gbank()
                    for c4 in range(4):
                        self.tr(pbc[0:16, c4 * 128:(c4 + 1) * 128], upT[:, blk * 4 + c4, :], identf[:], ['upT', 'identf'], [pkc])
                    ct, ctk = nxt('f512', f512)
                    self.cp('dve', ct[0:16, :], pbc[0:16, :], [pkc], [ctk])
                    self.dma('sp', fs_out[l, :, 1, blk * 512:(blk + 1) * 512], ct[0:16, :], [ctk], [])
                ffn_down(l, 1)

            if self.do_sample:
                self.stfT = self.sb("stfT", [128, 44, 2, 16], F32)
                self.upT = self.sb("upT", [128, 44, 16], F32)
                sample_phase()

            def fm_out(src_fn, nchunk, r, dst):
                c = 0
                while c < nchunk:
                    n = min(4, nchunk - c)
                    pb, pk = self.gbank()
                    for i in range(n):
                        ap_, keys = src_fn(c + i)
                        self.tr(pb[0:r, i * 128:(i + 1) * 128], ap_, identf[:], keys + ['identf'], [pk])
                    o, ok = nxt('f512', f512)
                    self.cp('dve', o[0:r, 0:n * 128], pb[0:r, 0:n * 128], [pk], [ok])
                    self.dma('sp', dst[:, c * 128:(c + n) * 128], o[0:r, 0:n * 128], [ok], [])
                    c += n
            fm_out(lambda c: (Ast[:, c, :], ['Ast']), 4, 30, ca_out)
            fm_out(lambda c: (Cst[:, c, :], ['Cst']), 8, 2, cc_out)
            for l in range(2):
                fm_out(lambda c, l=l: (Ust[:, l, c, :], ['Ust']), 44, 2, f_out[l])


            self.nops = len(S.ops)
            if os.environ.get('KSKIPOUT'):
                S.ops = [o for o in S.ops if not (o['dma'] and len(o['w']) == 0)]
            if os.environ.get('KSTOP'):
                S.ops = S.ops[:int(os.environ['KSTOP'])]
            S.emit(st)
            print(self.marks)
            print('nops', self.nops, 'emitted', len(S.ops), 'sem counts', S.max_counts, flush=True)
        return nc


def _bf(x):
    return x


def make_consts(C):
    NB = 4 * C // 256
    oh = np.zeros((33, 384), np.float32)
    for m in range(383):
        d = m - 127
        if d < 0:
            oh[32, m] = 1.0
        else:
            oh[int(t5_bucket_np(np.array([d]))[0]), m] = 1.0
    return oh


def core_inputs(inputs, c, C, SEQ, do_sample=True):
    b, j = c // 4, c % 4
    P = 3 * C
    NK = 4 * C
    NB = NK // 256
    x = inputs['x_prompt'][b]
    xk = np.zeros((NK, D), np.float32)
    lo = C * j - P
    src_lo = max(lo, 0)
    xk[src_lo - lo:, :] = x[src_lo:C * j + C]
    nvalid0 = (src_lo - lo) // 256
    gmask = np.zeros((NB, NB), np.float32)
    for own in range(NB):
        for n in range(NB):
            if n >= own or n < nvalid0:
                gmask[own, n] = -60000.0
    hv = np.array([[0.0 if j == 0 else 1.0]], np.float32)
    m = dict(xk=xk, gmask=gmask, hv=hv, oh=make_consts(C))
    for k in ('w_in_e', 'w_out_e', 'w_in_o', 'w_out_o', 'norm_mix_e', 'norm_mix_o', 'conv_a_w', 'conv_a_b', 'ln_a_g',
              'ln_a_b', 'q_norm_g', 'k_norm_g', 'conv_c_w'):
        m[k] = np.ascontiguousarray(inputs[k][0])
    for k in ('w_up', 'w_down', 'norm_ffn', 'conv_f_w', 'conv_f_b', 'rel_bias'):
        m[k] = np.ascontiguousarray(inputs[k])
    if do_sample:
        s0 = 16 * c
        m['xs'] = np.ascontiguousarray(inputs['x_sample'][s0:s0 + 16, 0, :])
        m['pt'] = np.ascontiguousarray(inputs['page_table'][s0:s0 + 16].reshape(1, 256).astype(np.int32))
        ck = inputs['cache_k'][0]
        m['cache_k'] = ck.reshape(ck.shape[0] * 64, 1024)
        cv = inputs['cache_v'][0]
        m['cache_v'] = cv.reshape(cv.shape[0] * 64, 1024)
        m['state_a'] = np.ascontiguousarray(inputs['state_conv_a'][0, s0:s0 + 16])
        m['state_c'] = np.ascontiguousarray(inputs['state_conv_c'][0, s0:s0 + 16])
        m['state_f'] = np.ascontiguousarray(inputs['state_ffn'][:, s0:s0 + 16])
        ohS = np.zeros((33, 2, 128), np.float32)
        for slot in range(2):
            for p in range(64, 128):
                kk = 2 * (p - 64) + slot
                ohS[int(t5_bucket_np(np.array([128 - kk]))[0]), slot, p] = 1.0
        ohS = ohS.reshape(33, 256)
        m['ohS'] = ohS
        oh16 = np.zeros((30, 16, 16), np.float32)
        for s_ in range(16):
            oh16[:, s_, s_] = 1.0
        m['oh16'] = oh16.reshape(30, 256)
    return m


_NC_CACHE = {}


def run_all(inputs, C, SEQ, do_sample=True):
    npool = inputs['cache_k'].shape[1] if do_sample else 2560
    key = (C, do_sample, npool)
    if key not in _NC_CACHE:
        bld = Builder(C, do_sample=do_sample, npool=npool)
        _NC_CACHE[key] = bld.build()
    nc = _NC_CACHE[key]
    in_maps = [core_inputs(inputs, c, C, SEQ, do_sample) for c in range(8)]
    res = run_bass_kernel_spmd(nc, in_maps, core_ids=list(range(8)))
    return res.results


def run_prompt(inputs, C, SEQ):
    return run_all(inputs, C, SEQ, do_sample=False)


def assemble(res, C, SEQ):
    y = np.zeros((2, SEQ, D), np.float32)
    k = np.zeros((2, SEQ, 512), np.float32)
    v = np.zeros((2, SEQ, 512), np.float32)
    for c in range(8):
        b, j = c // 4, c % 4
        y[b, C * j:C * j + C] = res[c]['y_out']
        k[b, C * j:C * j + C] = res[c]['k_out']
        v[b, C * j:C * j + C] = res[c]['v_out']
    npg = SEQ // 128
    k_prompt = k.reshape(1, 2, npg, 128, H, HD)
    v_prompt = v.reshape(1, 2, npg, 128, H, HD)
    a_p = np.stack([res[3]['ca_out'], res[7]['ca_out']])[None]
    c_p = np.stack([res[3]['cc_out'], res[7]['cc_out']])[None]
    f_p = np.stack([np.stack([res[3]['f_out'][l], res[7]['f_out'][l]]) for l in range(2)])
    ys = np.concatenate([res[c]['ys_out'] for c in range(8)], 0)[:, None, :]
    ks = np.concatenate([res[c]['ks_out'] for c in range(8)], 0).reshape(1, 128, 1, H, HD)
    vs = np.concatenate([res[c]['vs_out'] for c in range(8)], 0).reshape(1, 128, 1, H, HD)
    a_s = np.concatenate([res[c]['cas_out'] for c in range(8)], 0)[None]
    c_s = np.concatenate([res[c]['ccs_out'] for c in range(8)], 0)[None]
    f_s = np.concatenate([res[c]['fs_out'] for c in range(8)], 1)
    outs = (y, ys, k_prompt, v_prompt, a_p, c_p, f_p, ks, vs, a_s, c_s, f_s)
    return tuple(np.ascontiguousarray(o, dtype=np.float32) for o in outs)


def kernel(**inputs):
    inputs = {k: np.asarray(v) for k, v in inputs.items()}
    SEQ = inputs['x_prompt'].shape[1]
    C = SEQ // 4
    res = run_all(inputs, C, SEQ, do_sample=True)
    return assemble(res, C, SEQ)
```

```python
import math
import os
from contextlib import ExitStack
import numpy as np
import concourse.bass as bass
import concourse.mybir as mybir
from concourse.bass_utils import run_bass_kernel_spmd

F32 = mybir.dt.float32
BF16 = mybir.dt.bfloat16
I32 = mybir.dt.int32
AF = mybir.ActivationFunctionType
ALU = mybir.AluOpType
AX = mybir.AxisListType

D = 1024
DA = 512
H = 8
HD = 64
DFF = 2816
NPAIR = 22
EPS = 1e-6
NEG = -30000.0
ENG = ('pe', 'act', 'dve', 'pool', 'sp')


class Sched:
    NDMA = 12

    def __init__(self, nc):
        self.nc = nc
        self.ops = []
        self.eng = {'pe': nc.tensor, 'act': nc.scalar, 'dve': nc.vector, 'pool': nc.gpsimd, 'sp': nc.sync}

    def op(self, engine, fn, reads=(), writes=(), dma=False):
        self.ops.append(dict(e=engine, fn=fn, r=tuple(reads), w=tuple(writes), dma=dma,
                             sig=False, seq=None, waits=[]))

    def emit(self, stack):
        nc = self.nc
        ops = self.ops
        last_w = {}
        readers = {}
        seqc = {e: 0 for e in ENG}
        waited = {e: {y: -1 for y in ENG} for e in ENG}
        waited_dma = {e: set() for e in ENG}
        dma_cnt = {e: 0 for e in ENG}
        for i, o in enumerate(ops):
            e = o['e']
            deps = set()
            for k in o['r']:
                if k in last_w:
                    deps.add(last_w[k])
            for k in o['w']:
                if k in last_w:
                    deps.add(last_w[k])
                for j in readers.get(k, ()):
                    deps.add(j)
            deps.discard(i)
            need = {}
            for j in deps:
                oj = ops[j]
                if oj['dma']:
                    if j not in waited_dma[e]:
                        waited_dma[e].add(j)
                        o['waits'].append(('dma', j))
                else:
                    y = oj['e']
                    if y == 'pe' and e == 'pe':
                        continue
                    if oj['seq'] > waited[e][y]:
                        need[y] = max(need.get(y, -1), oj['seq'])
            for y, s in need.items():
                waited[e][y] = s
                o['waits'].append(('cmp', y, s))
            if o['dma']:
                n = dma_cnt[e]
                dma_cnt[e] += 1
                o['dsem'] = (e, n % self.NDMA)
                o['dval'] = 16 * (n // self.NDMA + 1)
            else:
                o['seq'] = seqc[e]
                seqc[e] += 1
            for k in o['r']:
                readers.setdefault(k, []).append(i)
            for k in o['w']:
                last_w[k] = i
                readers[k] = []
        by_seq = {e: {} for e in ENG}
        for i, o in enumerate(ops):
            if not o['dma']:
                by_seq[o['e']][o['seq']] = i
        for o in ops:
            for w in o['waits']:
                if w[0] == 'cmp':
                    ops[by_seq[w[1]][w[2]]]['sig'] = True
        cnt = {e: 0 for e in ENG}
        for o in ops:
            if not o['dma'] and o['sig']:
                cnt[o['e']] += 1
                o['cnt'] = cnt[o['e']]
        self.max_counts = dict(cnt)
        csem = {e: stack.enter_context(nc.semaphore('c_' + e)) for e in ENG}
        dsem = {}
        for e in ENG:
            for n in range(min(self.NDMA, dma_cnt[e])):
                dsem[(e, n)] = stack.enter_context(nc.semaphore('d_%s_%d' % (e, n)))
        for i, o in enumerate(ops):
            e = o['e']
            eng = self.eng[e]
            for w in o['waits']:
                if w[0] == 'dma':
                    oj = ops[w[1]]
                    eng.wait_ge(dsem[oj['dsem']], oj['dval'])
                else:
                    oj = ops[by_seq[w[1]][w[2]]]
                    eng.wait_ge(csem[w[1]], oj['cnt'])
            if o['dma']:
                if o['dval'] > 16:
                    eng.wait_ge(dsem[o['dsem']], o['dval'] - 16)
                ins = o['fn'](eng)
                ins.then_inc(dsem[o['dsem']], 16)
            else:
                ins = o['fn'](eng)
                if o['sig']:
                    ins.then_inc(csem[e], 1)
        last_dma = {}
        for o in ops:
            if o['dma']:
                last_dma[o['dsem']] = o['dval']
        for k, v in last_dma.items():
            nc.sync.wait_ge(dsem[k], v)


def t5_bucket_np(n):
    n = np.maximum(n, 0)
    nf = np.maximum(n, 1).astype(np.float32)
    large = 16 + (np.log(nf / np.float32(16)) / np.float32(math.log(128 / 16)) * np.float32(16)).astype(np.int32)
    large = np.minimum(large, 31)
    return np.where(n < 16, n, large)


class Builder:
    def __init__(self, C, nsamp=16, do_sample=True, npool=2560):
        self.C = C
        self.P = 3 * C
        self.NK = 4 * C
        self.NT = self.NK // 128
        self.NB = self.NK // 256
        self.GS = min(512, C)
        self.nsamp = nsamp
        self.do_sample = do_sample
        self.npool = npool
        self.nc = bass.Bass("TRN2", target_bir_lowering=False)
        self.ukey = 0

    def din(self, name, shape, dt=F32):
        return self.nc.dram_tensor(name, list(shape), dt, kind="ExternalInput").ap()

    def dout(self, name, shape, dt=F32):
        return self.nc.dram_tensor(name, list(shape), dt, kind="ExternalOutput").ap()

    def dscr(self, name, shape, dt):
        return self.nc.dram_tensor(name, list(shape), dt, kind="Internal").ap()

    def sb(self, name, shape, dt):
        return self.st.enter_context(self.nc.sbuf_tensor(name, list(shape), dt))

    def uk(self, p='t'):
        self.ukey += 1
        return (p, self.ukey)

    def mm(self, out, lhsT, rhs, start, stop, r, w):
        self.S.op('pe', lambda e: e.matmul(out, lhsT=lhsT, rhs=rhs, start=start, stop=stop), r, w)

    def tr(self, out, in_, ident, r, w):
        self.S.op('pe', lambda e: e.transpose(out=out, in_=in_, identity=ident), r, w)

    def act(self, out, in_, func, r, w, scale=None, bias=None, accum=None):
        kw = {}
        if scale is not None:
            kw['scale'] = scale
        if bias is not None:
            kw['bias'] = bias
        if accum is not None:
            kw['accum_out'] = accum
        self.S.op('act', lambda e: e.activation(out=out, in_=in_, func=func, **kw), r, w)

    def tt(self, eng, out, in0, in1, op, r, w):
        self.S.op(eng, lambda e: e.tensor_tensor(out=out, in0=in0, in1=in1, op=op), r, w)

    def ts(self, eng, out, in0, s1, s2, op0, op1, r, w):
        if op1 is None:
            self.S.op(eng, lambda e: e.tensor_scalar(out=out, in0=in0, scalar1=s1, scalar2=None, op0=op0), r, w)
        else:
            self.S.op(eng, lambda e: e.tensor_scalar(out=out, in0=in0, scalar1=s1, scalar2=s2, op0=op0, op1=op1), r, w)

    def stt(self, out, in0, scalar, in1, op0, op1, r, w):
        self.S.op('dve', lambda e: e.scalar_tensor_tensor(out=out, in0=in0, scalar=scalar, in1=in1, op0=op0, op1=op1), r, w)

    def cp(self, eng, out, in_, r, w):
        if eng == 'act':
            self.S.op('act', lambda e: e.activation(out=out, in_=in_, func=AF.Copy), r, w)
        else:
            self.S.op(eng, lambda e: e.tensor_copy(out=out, in_=in_), r, w)

    def red(self, out, in_, op, r, w, axis=AX.X):
        self.S.op('dve', lambda e: e.tensor_reduce(out=out, in_=in_, axis=axis, op=op), r, w)

    def recip(self, out, in_, r, w):
        self.S.op('dve', lambda e: e.reciprocal(out=out, in_=in_), r, w)

    def memset(self, eng, ap, val, w):
        self.S.op(eng, lambda e: e.memset(ap, val), (), w)

    def dma(self, q, out, in_, r, w, slow=False):
        if slow:
            self.S.op(q, lambda e: e.dma_start(out=out, in_=in_, allow_slow_non_contiguous=True), r, w, dma=True)
        else:
            self.S.op(q, lambda e: e.dma_start(out=out, in_=in_), r, w, dma=True)

    def gbank(self):
        i = self.gi % 4
        self.gi += 1
        return self.ps[i], ('ps', i)

    def abank(self):
        i = 4 + self.ai % 4
        self.ai += 1
        return self.ps[i], ('ps', i)

    def sbank(self):
        i = 4 + self.si % 2
        self.si += 1
        return self.ps[i], ('ps', i)

    def obank(self):
        i = 6 + self.oi % 2
        self.oi += 1
        return self.ps[i], ('ps', i)

    def wpiece(self, loads):
        i = self.wi % self.NSLOT
        self.wi += 1
        slot = self.wsl[i]
        key = ('w', i)
        nws = int(os.environ.get('KWS', '4'))
        if self.wpend.get(i) is not None:
            j = self.wpend.pop(i)
            self.dma('act', self.wscr[j], slot[:, :], [key], [('wscr', j)])
        pid = tuple((src.tensor.name, src.offset, str(src.ap)) for _, src in loads)
        if pid in self.wcache:
            j = self.wcache[pid]
            for i2, j2 in list(self.wpend.items()):
                if j2 == j:
                    self.wpend.pop(i2)
                    self.dma('act', self.wscr[j], self.wsl[i2][:, :], [('w', i2)], [('wscr', j)])
            self.wser += 1
            self.dma('pool', slot[:, :], self.wscr[j], [('wscr', j)], [key, ('wser', self.wser % nws)])
        else:
            j = len(self.wcache)
            self.wcache[pid] = j
            for dstf, src in loads:
                self.wser += 1
                self.dma('pool', dstf(slot), src, (), [key, ('wser', self.wser % nws)])
            self.wpend[i] = j
        return slot, key

    def build(self):
        nc = self.nc
        C, P, NK, NT, NB, GS = self.C, self.P, self.NK, self.NT, self.NB, self.GS
        xk = self.din("xk", [NK, D])
        gmask_d = self.din("gmask", [NB, NB])
        hv_d = self.din("hv", [1, 1])
        oh_d = self.din("oh", [33, 384])
        w_in_e = self.din("w_in_e", [D, 2560])
        w_out_e = self.din("w_out_e", [D, D])
        w_in_o = self.din("w_in_o", [D, 3072])
        w_out_o = self.din("w_out_o", [D, D])
        w_up = self.din("w_up", [2, D, 2 * DFF])
        w_down = self.din("w_down", [2, DFF, D])
        rel_bias = self.din("rel_bias", [32, H])
        norm_mix_e = self.din("norm_mix_e", [D])
        norm_mix_o = self.din("norm_mix_o", [D])
        norm_ffn = self.din("norm_ffn", [2, D])
        conv_a_w = self.din("conv_a_w", [31, DA])
        conv_a_b = self.din("conv_a_b", [DA])
        ln_a_g = self.din("ln_a_g", [DA])
        ln_a_b = self.din("ln_a_b", [DA])
        q_norm_g = self.din("q_norm_g", [HD])
        k_norm_g = self.din("k_norm_g", [HD])
        conv_c_w = self.din("conv_c_w", [3, D])
        conv_f_w = self.din("conv_f_w", [2, 3, 2 * DFF])
        conv_f_b = self.din("conv_f_b", [2, 2 * DFF])
        y_out = self.dout("y_out", [C, D])
        k_out = self.dout("k_out", [C, 512])
        v_out = self.dout("v_out", [C, 512])
        ca_out = self.dout("ca_out", [30, DA])
        cc_out = self.dout("cc_out", [2, D])
        f_out = self.dout("f_out", [2, 2, 2 * DFF])
        if self.do_sample:
            NPOOL = self.npool
            xs_d = self.din("xs", [16, D])
            pt_d = self.din("pt", [1, 256], I32)
            ck_d = self.din("cache_k", [NPOOL * 64, 1024])
            cv_d = self.din("cache_v", [NPOOL * 64, 1024])
            sta_d = self.din("state_a", [16, 30, DA])
            stc_d = self.din("state_c", [16, 2, D])
            stf_d = self.din("state_f", [2, 16, 2, 2 * DFF])
            ohS_d = self.din("ohS", [33, 256])
            oh16_d = self.din("oh16", [30, 256])
            ys_out = self.dout("ys_out", [16, D])
            ks_out = self.dout("ks_out", [16, 512])
            vs_out = self.dout("vs_out", [16, 512])
            cas_out = self.dout("cas_out", [16, 30, DA])
            ccs_out = self.dout("ccs_out", [16, 2, D])
            fs_out = self.dout("fs_out", [2, 16, 2, 2 * DFF])
        qs_scr = self.dscr("qs_scr", [16, 512], F32)
        wscr = self.dscr("wscr", [64, 128, 4096], BF16)
        kt_scr = self.dscr("kt_scr", [H, 96, NK], BF16)
        va_scr = self.dscr("va_scr", [H, 128, NT, 128], BF16)
        fv_scr = self.dscr("fv_scr", [H, 384], F32)
        if os.environ.get("KDBG"):
            k_out = self.dscr("dbg_scr", [C, 512], F32)

        with ExitStack() as st:
            self.st = st
            self.S = Sched(nc)
            S = self.S
            self.gi = 0
            self.ai = 0
            self.si = 0
            self.oi = 0
            self.wi = 0
            self.wser = 0
            self.wcache = {}
            self.wpend = {}
            self.wscr = wscr
            self.NSLOT = 5
            self.ps = [st.enter_context(nc.psum_tensor("ps%d" % i, [128, 512], F32)) for i in range(8)]
            self.wsl = [self.sb("wslot%d" % i, [128, 4096], BF16) for i in range(self.NSLOT)]
            identf = self.sb("identf", [128, 128], F32)
            ident = self.sb("ident", [128, 128], BF16)
            onesf = self.sb("onesf", [128, 128], F32)
            eps_t = self.sb("eps_t", [128, 1], F32)
            self.memset('pool', identf[:], 0.0, ['identf'])
            S.op('pool', lambda e: e.affine_select(out=identf[:], in_=identf[:], pattern=[[-1, 128]], compare_op=ALU.not_equal,
                                                   fill=1.0, base=0, channel_multiplier=1), ['identf'], ['identf'])
            self.cp('dve', ident[:], identf[:], ['identf'], ['ident'])
            self.memset('pool', onesf[:], 1.0, ['onesf'])
            self.memset('pool', eps_t[:], EPS, ['eps_t'])
            self.ident, self.identf, self.onesf, self.eps_t = ident, identf, onesf, eps_t

            def colload(name, src_ap, shape):
                t = self.sb(name, shape, F32)
                self.dma('act', t[:], src_ap, (), [name], slow=True)
                return t
            def colload2(name, shape, parts):
                t = self.sb(name, shape, F32)
                for dst_fn, src in parts:
                    self.dma('act', dst_fn(t), src, (), [name], slow=True)
                return t
            gE = colload("gE", norm_mix_e.rearrange("(c p) -> p c", p=128), [128, 8])
            gO = colload("gO", norm_mix_o.rearrange("(c p) -> p c", p=128), [128, 8])
            gF = colload2("gF", [128, 2, 8], [(lambda t, l=l: t[:, l, :], norm_ffn[l].rearrange("(c p) -> p c", p=128)) for l in range(2)])
            caw = colload2("caw", [128, 4, 31], [(lambda t, c=c: t[:, c, :], conv_a_w[:, c * 128:(c + 1) * 128].rearrange("j p -> p j"))
                                                  for c in range(4)])
            cab = colload("cab", conv_a_b.rearrange("(c p) -> p c", p=128), [128, 4])
            lng = colload("lng", ln_a_g.rearrange("(c p) -> p c", p=128), [128, 4])
            lnb = colload("lnb", ln_a_b.rearrange("(c p) -> p c", p=128), [128, 4])
            ccw = colload2("ccw", [128, 8, 3], [(lambda t, j=j: t[:, :, j], conv_c_w[j].rearrange("(c p) -> p c", p=128)) for j in range(3)])
            cfw = colload2("cfw", [128, 2, 44, 3], [(lambda t, l=l, j=j: t[:, l, :, j], conv_f_w[l, j].rearrange("(c p) -> p c", p=128))
                                                     for l in range(2) for j in range(3)])
            cfb = colload2("cfb", [128, 2, 44], [(lambda t, l=l: t[:, l, :], conv_f_b[l].rearrange("(c p) -> p c", p=128)) for l in range(2)])
            gqB = colload("gqB", q_norm_g.rearrange("(o d) -> o d", o=1).partition_broadcast(128), [128, 1, HD])
            gkB = colload("gkB", k_norm_g.rearrange("(o d) -> o d", o=1).partition_broadcast(128), [128, 1, HD])
            b31B = colload("b31B", rel_bias[31:32, :].partition_broadcast(128), [128, 1, H])
            gmask = self.sb("gmaskt", [128, 1, NB, NB], BF16)
            self.dma('pool', gmask[:], gmask_d.rearrange("(o a) b -> o a b", o=1).partition_broadcast(128), (), ['gmaskt'])
            hv = colload("hvt", hv_d.partition_broadcast(128), [128, 1, 1])
            scB = self.sb("scB", [128, H], F32)
            self.ts('dve', scB[:], b31B[:, 0, :], -NEG, None, ALU.add, None, ['b31B'], ['scB'])

            rb = self.sb("rb", [33, H], F32)
            rb31 = self.sb("rb31", [33, 1, H], F32)
            ohs = self.sb("ohs", [33, 384], F32)
            self.dma('act', rb[0:32, :], rel_bias, (), ['rb'])
            self.dma('act', rb31[:], rel_bias[31:32, :].partition_broadcast(33), (), ['rb31'])
            self.dma('act', ohs[:], oh_d, (), ['ohs'])
            self.tt('dve', rb[0:32, :], rb[0:32, :], rb31[0:32, 0, :], ALU.subtract, ['rb', 'rb31'], ['rb'])
            self.memset('pool', rb[32:33, :], NEG, ['rb'])
            pb, pk = self.gbank()
            self.mm(pb[0:H, 0:384], rb[:], ohs[:], True, True, ['rb', 'ohs'], [pk])
            fv = self.sb("fv", [H, 384], F32)
            self.cp('dve', fv[:], pb[0:H, 0:384], [pk], ['fv'])
            self.dma('sp', fv_scr, fv[:], ['fv'], ['fv_scr'])
            DC = self.sb("DC", [128, H, 2, 128], BF16)

            xres = self.sb("xres", [128, 4, D], F32)
            hT = self.sb("hT", [128, 8, GS], BF16)
            catT = self.sb("catT", [128, 8, GS], BF16)
            big = self.sb("big", [128, 16384], BF16)
            mT = big[:, 0:NPAIR * GS].rearrange("p (c t) -> p c t", c=NPAIR)
            Ast = self.sb("Ast", [128, 4, 30], F32)
            Cst = self.sb("Cst", [128, 8, 2], F32)
            Ust = self.sb("Ust", [128, 2, 44, 2], F32)
            tsum = self.sb("tsum", [64, H, NT], F32)
            kmT = self.sb("kmT", [64, H, NB], F32)
            QTaug = self.sb("QTaug", [96, H, GS], BF16)
            self.memset('pool', Ast[:], 0.0, ['Ast'])
            self.memset('pool', Cst[:], 0.0, ['Cst'])
            self.memset('pool', Ust[:], 0.0, ['Ust'])
            self.memset('pool', tsum[:], 0.0, ['tsum'])
            KTb = [big[0:96, i * 4096:(i + 1) * 4096] for i in range(2)]
            VAb = [big[:, 8192 + i * 4096:8192 + (i + 1) * 4096].rearrange("p (t c) -> p t c", c=128) for i in range(2)]

            def mkeys(c):
                a = ('KTb', 0) if c * GS < 4096 else (('KTb', 1) if c * GS < 8192 else (('VAb', 0) if c * GS < 12288 else ('VAb', 1)))
                b = ('KTb', 0) if (c + 1) * GS - 1 < 4096 else (('KTb', 1) if (c + 1) * GS - 1 < 8192 else (('VAb', 0) if (c + 1) * GS - 1 < 12288 else ('VAb', 1)))
                return list({('mT', c), a, b})
            self.kvi = 0
            def pool(name, n, shape, dt):
                return [self.sb("%s%d" % (name, i), shape, dt) for i in range(n)]
            sqb = pool("sqb", 1, [128, D], BF16)
            hb = pool("hb", 2, [128, D], BF16)
            st1 = pool("st1", 4, [128, 8], F32)
            f512 = pool("f512", 5, [128, 512], F32)
            kaug = pool("kaug", 2, [128, H, 96], BF16)
            nmt = pool("nmt", 2, [128, H, 32], BF16)
            g2p = pool("g2p", 2, [128, H, NB], F32)
            t8p = pool("t8p", 2, [128, H, 8], F32)
            qtf = pool("qtf", 1, [64, H, 128], F32)
            ktsb = pool("ktsb", 2, [96, H, 128], BF16)
            vasb = pool("vasb", 2, [128, H, 128], BF16)
            for i in range(2):
                self.memset('pool', vasb[i][:], 1.0, [('vasb', i)])
            self.cvA = pool("cvA", 4, [128, GS], F32)
            self.ubuf = pool("ubuf", 3, [128, 2 + GS], F32)
            abuf = pool("abuf", 2, [128, 30 + GS], F32)
            ptb = pool("ptb", 3, [128, 512], BF16)
            for i in range(2):
                self.memset('pool', nmt[i][:], 0.0, [('nmt', i)])
            self.rr = {}

            def nxt(name, lst):
                i = self.rr.get(name, 0)
                self.rr[name] = i + 1
                return lst[i % len(lst)], (name, i % len(lst))

            Jf = self.sb("Jf", [128, 128], F32)
            self.memset('pool', Jf[:], 0.0, ['Jf'])
            S.op('pool', lambda e: e.affine_select(out=Jf[:], in_=Jf[:], pattern=[[1, 128]], compare_op=ALU.not_equal,
                                                   fill=1.0, base=-127, channel_multiplier=1), ['Jf'], ['Jf'])
            for h in range(H):
                dct, dctk = nxt('f512', f512)
                for k, off in ((0, 0), (1, 128)):
                    src = bass.AP(tensor=fv_scr.tensor, offset=h * 384 + off, ap=[[1, 128], [1, 128]])
                    self.dma('sp', dct[:, k * 128:(k + 1) * 128], src, ['fv_scr'], [dctk], slow=True)
                pbj, pkj = self.gbank()
                self.mm(pbj[:, 0:256], Jf[:], dct[:, 0:256], True, True, ['Jf', dctk], [pkj])
                self.cp('dve', DC[:, h, :, :], pbj[:, 0:256].rearrange("p (k t) -> p k t", k=2), [pkj], ['DC'])
            def norm_T(xt, xkey, gcol, gkey, ntile_idx):
                sq, sqk = nxt('sqb', sqb)
                s1, s1k = nxt('st1', st1)
                self.act(sq[:], xt, AF.Square, [xkey], [sqk, s1k], accum=s1[:, 0:1])
                self.act(s1[:, 1:2], s1[:, 0:1], AF.Sqrt, [s1k], [s1k], scale=1.0 / D, bias=eps_t[:, 0:1])
                self.recip(s1[:, 2:3], s1[:, 1:2], [s1k], [s1k])
                hh, hk = nxt('hb', hb)
                self.ts('dve', hh[:], xt, s1[:, 2:3], None, ALU.mult, None, [xkey, s1k], [hk])
                pb, pk = self.gbank()
                pbf = pb[:].bitcast(BF16)
                for kc in range(8):
                    self.tr(pbf[:, kc * 128:(kc + 1) * 128], hh[:, kc * 128:(kc + 1) * 128], ident[:], [hk, 'ident'], [pk])
                c0 = ntile_idx * 128
                self.tt('dve', hT[:, :, c0:c0 + 128], pbf.rearrange("p (c t) -> p c t", c=8),
                        gcol.unsqueeze(2).to_broadcast([128, 8, 128]), ALU.mult, [pk, gkey], [('hT', ntile_idx)])

            def head_norm(pb, pk, gB, gBkey, extra_scale):
                sq, sqk = nxt('f512', f512)
                s1, s1k = nxt('st1', st1)
                self.act(sq[:], pb[:], AF.Square, [pk], [sqk])
                self.red(s1[:, 0:8], sq[:].rearrange("p (h d) -> p h d", h=H), ALU.add, [sqk], [s1k])
                s2, s2k = nxt('st1', st1)
                self.act(s2[:], s1[:], AF.Sqrt, [s1k], [s2k], scale=1.0 / HD, bias=eps_t[:, 0:1])
                self.recip(s1[:], s2[:], [s2k], [s1k])
                if extra_scale != 1.0:
                    self.ts('dve', s1[:], s1[:], extra_scale, None, ALU.mult, None, [s1k], [s1k])
                o, ok = nxt('f512', f512)
                o3 = o[:].rearrange("p (h d) -> p h d", h=H)
                self.tt('dve', o3, pb[:].rearrange("p (h d) -> p h d", h=H), s1[:].unsqueeze(2).to_broadcast([128, H, HD]),
                        ALU.mult, [pk, s1k], [ok])
                self.tt('dve', o3, o3, gB[:].to_broadcast([128, H, HD]), ALU.mult, [ok, gBkey], [ok])
                return o, ok

            def tok_mm(slot, skey, ntile_idx):
                pb, pk = self.gbank()
                sv = slot[:].rearrange("p (c n) -> p c n", c=8)
                c0 = ntile_idx * 128
                for kc in range(8):
                    self.mm(pb[:], hT[:, kc, c0:c0 + 128], sv[:, kc, :], kc == 0, kc == 7, [('hT', ntile_idx), skey], [pk])
                return pb, pk

            def w_cols(w2d, c0, n):
                return (lambda s: s[:, 0:8 * n].rearrange("p (c n) -> p c n", c=8),
                        w2d.rearrange("(c p) n -> p c n", p=128)[:, :, c0:c0 + n])

            def kv_a(gt, ntile_idx, kslot, kskey, vslot, vskey, own_out_row):
                pb, pk = tok_mm(kslot, kskey, ntile_idx)
                Kf, Kfk = head_norm(pb, pk, gkB, 'gkB', 1.0)
                if own_out_row is not None:
                    self.dma(os.environ.get('KOQ', 'sp'), k_out[own_out_row:own_out_row + 128, :], Kf[:], [Kfk], [])
                ka, kak = nxt('kaug', kaug)
                self.memset('pool', ka[:, :, 64:96], 0.0, [kak])
                self.memset('pool', ka[:, :, 64 + gt // 2:65 + gt // 2], 1.0, [kak])
                self.cp('pool', ka[:, :, 0:64], Kf[:].rearrange("p (h d) -> p h d", h=H), [Kfk], [kak])
                pbv, pkv = tok_mm(vslot, vskey, ntile_idx)
                vas, vask = nxt('vasb', vasb)
                if own_out_row is not None:
                    Vf, Vfk = nxt('f512', f512)
                    self.cp('act', Vf[:], pbv[:], [pkv], [Vfk])
                    self.dma(os.environ.get('KOQ', 'sp'), v_out[own_out_row:own_out_row + 128, :], Vf[:], [Vfk], [])
                    self.cp('dve', vas[:, :, 0:64], Vf[:].rearrange("p (h d) -> p h d", h=H), [Vfk], [vask])
                else:
                    self.cp('dve', vas[:, :, 0:64], pbv[:].rearrange("p (h d) -> p h d", h=H), [pkv], [vask])
                self.dma('sp', va_scr[:, :, gt, :].rearrange("h p c -> p h c"), vas[:], [vask], ['va_scr'])
                return (gt, Kf, Kfk, ka, kak)

            def kv_b(st_):
                gt, Kf, Kfk, ka, kak = st_
                pb2, pk2 = self.gbank()
                for h in range(H):
                    self.mm(pb2[0:64, h:h + 1], Kf[:, h * 64:(h + 1) * 64], onesf[:, 0:1], True, True, [Kfk, 'onesf'], [pk2])
                self.cp('dve', tsum[:, :, gt], pb2[0:64, 0:H], [pk2], ['tsum'])
                pb3, pk3 = self.gbank()
                p3 = pb3[:].bitcast(BF16)
                for h in range(H):
                    self.tr(p3[0:96, h * 128:(h + 1) * 128], ka[:, h, :], ident[:], [kak, 'ident'], [pk3])
                kts, ktsk = nxt('ktsb', ktsb)
                self.cp('act', kts[:], p3[0:96, :].rearrange("p (h t) -> p h t", h=H), [pk3], [ktsk])
                self.dma('sp', kt_scr[:, :, gt * 128:(gt + 1) * 128].rearrange("h r k -> r h k"), kts[:], [ktsk], ['kt_scr'])

            def kv_tiles(gt0, n, kslot, kskey, vslot, vskey, orow0):
                prev = None
                for ti in range(n):
                    cur = kv_a(gt0 + ti, ti, kslot, kskey, vslot, vskey, None if orow0 is None else orow0 + ti * 128)
                    if prev is not None:
                        kv_b(prev)
                    prev = cur
                kv_b(prev)

            def load_x(row0, ntile):
                for ti in range(ntile):
                    self.dma('sp', xres[:, ti, :], xk[row0 + ti * 128:row0 + (ti + 1) * 128, :], (), [('xres', ti)])

            npre = P // 128 - 1
            kslot, kskey = self.wpiece([w_cols(w_in_e, 1536, 512)])
            vslot, vskey = self.wpiece([w_cols(w_in_e, 2048, 512)])
            gt = 0
            while gt < npre:
                nt_ = min(4, npre - gt)
                load_x(gt * 128, nt_)
                for ti in range(nt_):
                    norm_T(xres[:, ti, :], ('xres', ti), gE[:], 'gE', ti)
                kv_tiles(gt, nt_, kslot, kskey, vslot, vskey, None)
                gt += nt_

            groups = [(P // 128 - 1, 1, None)]
            for g in range(C // GS):
                groups.append((P // 128 + g * (GS // 128), GS // 128, g * GS))

            def attention(gt0, ntile):
                ntok = ntile * 128
                nkt = gt0 + ntile
                for h in range(H):
                    ob, okk = self.obank()
                    nhalf = (nkt + 31) // 32
                    first = True
                    pend = None
                    for hf in range(nhalf):
                        k0 = hf * 32
                        k1 = min(nkt, k0 + 32)
                        i = self.kvi % 2
                        self.kvi += 1
                        ktb, vab = KTb[i], VAb[i]
                        self.dma('sp', ktb[:, 0:(k1 - k0) * 128], kt_scr[h, :, k0 * 128:k1 * 128], ['kt_scr'], [('KTb', i)])
                        self.dma('sp', vab[:, 0:k1 - k0, :], va_scr[h, :, k0:k1, :], ['va_scr'], [('VAb', i)])
                        for kt in range(k0, k1):
                            qlo = max(kt, gt0) - gt0
                            c0 = qlo * 128
                            sb_, sk = self.sbank()
                            has_d0 = kt >= gt0
                            has_c1 = (kt + 1 >= gt0) and (kt + 1 < gt0 + ntile)
                            self.mm(sb_[:, c0:ntok], ktb[:, (kt - k0) * 128:(kt - k0 + 1) * 128], QTaug[:, h, c0:ntok], True,
                                    not (has_d0 or has_c1), [('KTb', i), ('QTaug', h)], [sk])
                            if has_d0:
                                cc = (kt - gt0) * 128
                                self.mm(sb_[:, cc:cc + 128], ident[:], DC[:, h, 0, :], False, not has_c1, ['ident', 'DC'], [sk])
                            if has_c1:
                                cc = (kt + 1 - gt0) * 128
                                self.mm(sb_[:, cc:cc + 128], ident[:], DC[:, h, 1, :], False, True, ['ident', 'DC'], [sk])
                            pt, ptk = nxt('ptb', ptb)
                            self.act(pt[:, c0:ntok], sb_[:, c0:ntok], AF.Exp, [sk], [ptk])
                            if pend is not None:
                                self.mm(*pend[0], **pend[1])
                            pend = ((ob[:, c0:ntok], vab[:, kt - k0, :], pt[:, c0:ntok], first, kt == nkt - 1, [('VAb', i), ptk], [okk]), {})
                            first = False
                    if pend is not None:
                        self.mm(*pend[0], **pend[1])
                        pend = None
                    rec, rk = nxt('f512', f512)
                    self.recip(rec[0:64, 0:ntok], ob[64:128, 0:ntok], [okk], [rk])
                    po = (h % 2) * 64
                    self.tt('dve', catT[po:po + 64, 4 + h // 2, 0:ntok], ob[0:64, 0:ntok], rec[0:64, 0:ntok], ALU.mult,
                            [okk, rk], [('catT', 4 + h // 2, h % 2)])

            def out_proj(wsrc, ntile, catkeys):
                for half in range(2):
                    slot, skey = self.wpiece([w_cols(wsrc, half * 512, 512)])
                    sv = slot[:].rearrange("p (c n) -> p c n", c=8)
                    banks = [self.abank() for _ in range(ntile)]
                    for kc in range(8):
                        for ti in range(ntile):
                            self.mm(banks[ti][0][:], catT[:, kc, ti * 128:(ti + 1) * 128], sv[:, kc, :], kc == 0, kc == 7,
                                    catkeys(kc) + [skey], [banks[ti][1]])
                    for ti in range(ntile):
                        self.tt('dve', xres[:, ti, half * 512:(half + 1) * 512], banks[ti][0][:], xres[:, ti, half * 512:(half + 1) * 512],
                                ALU.add, [banks[ti][1], ('xres', ti)], [('xres', ti)])

            def ffn(l, ntile, first_own):
                ntok = ntile * 128
                for ti in range(ntile):
                    norm_T(xres[:, ti, :], ('xres', ti), gF[:, l, :], 'gF', ti)
                hkeys = [('hT', ti) for ti in range(ntile)]
                if first_own:
                    self.ts('dve', Ust[:, l, :, :], Ust[:, l, :, :], hv[:, 0, 0:1], None, ALU.mult, None, ['Ust', 'hvt'], ['Ust'])
                for c4 in range((NPAIR + 3) // 4):
                    ncol = min(4, NPAIR - c4 * 4) * 128
                    uslots = [self.wpiece([w_cols(w_up[l], part * DFF + c4 * 512, ncol)]) for part in range(2)]
                    for cl in range(ncol // 128):
                        c = c4 * 4 + cl
                        res = []
                        for part in range(2):
                            ci = c + part * NPAIR
                            pb, pk = self.gbank()
                            for kc in range(8):
                                self.mm(pb[:, 0:ntok], uslots[part][0][:, 0:8 * ncol].rearrange("p (c n) -> p c n", c=8)[:, kc, cl * 128:(cl + 1) * 128],
                                        hT[:, kc, 0:ntok], kc == 0, kc == 7, hkeys + [uslots[part][1]], [pk])
                            U, Uk = nxt('ubuf', self.ubuf)
                            self.cp('pool', U[:, 0:2], Ust[:, l, ci, :], ['Ust'], [Uk])
                            self.cp('act', U[:, 2:2 + ntok], pb[:, 0:ntok], [pk], [Uk])
                            self.cp('pool', Ust[:, l, ci, :], U[:, ntok:ntok + 2], [Uk], ['Ust'])
                            cv, cvk = nxt('f512', f512)
                            self.act(cv[:, 0:ntok], U[:, 2:2 + ntok], AF.Identity, [Uk, 'cfw', 'cfb'], [cvk],
                                     scale=cfw[:, l, ci, 2:3], bias=cfb[:, l, ci:ci + 1])
                            self.stt(cv[:, 0:ntok], U[:, 1:1 + ntok], cfw[:, l, ci, 1:2], cv[:, 0:ntok], ALU.mult, ALU.add, [Uk, cvk, 'cfw'], [cvk])
                            self.stt(cv[:, 0:ntok], U[:, 0:ntok], cfw[:, l, ci, 0:1], cv[:, 0:ntok], ALU.mult, ALU.add, [Uk, cvk, 'cfw'], [cvk])
                            res.append((cv, cvk))
                        (cg, cgk), (cu, cuk) = res
                        self.act(cg[:, 0:ntok], cg[:, 0:ntok], AF.Silu, [cgk], [cgk])
                        self.tt('dve', mT[:, c, 0:ntok], cg[:, 0:ntok], cu[:, 0:ntok], ALU.mult, [cgk, cuk], mkeys(c))
                ffn_down(l, ntile)

            def ffn_down(l, ntile):
                for half in range(2):
                    banks = [self.abank() for _ in range(ntile)]
                    for pi, (ca, cb) in enumerate(((0, 8), (8, 16), (16, 22))):
                        slot, skey = self.wpiece([(lambda s, n=cb - ca: s[:, 0:n * 512].rearrange("p (c n) -> p c n", c=n),
                                                   w_down[l, ca * 128:cb * 128, half * 512:(half + 1) * 512].rearrange("(c p) n -> p c n", p=128))])
                        sv = slot[:, 0:(cb - ca) * 512].rearrange("p (c n) -> p c n", c=cb - ca)
                        for c in range(ca, cb):
                            for ti in range(ntile):
                                self.mm(banks[ti][0][:], mT[:, c, ti * 128:(ti + 1) * 128], sv[:, c - ca, :], c == 0, c == NPAIR - 1,
                                        mkeys(c) + [skey], [banks[ti][1]])
                    for ti in range(ntile):
                        self.tt('dve', xres[:, ti, half * 512:(half + 1) * 512], banks[ti][0][:], xres[:, ti, half * 512:(half + 1) * 512],
                                ALU.add, [banks[ti][1], ('xres', ti)], [('xres', ti)])

            def ln_silu(cvs, ntok):
                pbm, pkm = self.gbank()
                pbs, pks = self.gbank()
                sqs = []
                for cc in range(4):
                    sq, sqk = nxt('f512', f512)
                    self.act(sq[:, 0:ntok], cvs[cc][0][:, 0:ntok], AF.Square, [cvs[cc][1]], [sqk])
                    sqs.append((sq, sqk))
                for cc in range(4):
                    self.mm(pbm[:, 0:ntok], onesf[:], cvs[cc][0][:, 0:ntok], cc == 0, cc == 3, ['onesf', cvs[cc][1]], [pkm])
                for cc in range(4):
                    self.mm(pbs[:, 0:ntok], onesf[:], sqs[cc][0][:, 0:ntok], cc == 0, cc == 3, ['onesf', sqs[cc][1]], [pks])
                mean, meank = nxt('f512', f512)
                self.ts('dve', mean[:, 0:ntok], pbm[:, 0:ntok], 1.0 / DA, None, ALU.mult, None, [pkm], [meank])
                var, vark = sqs[0]
                self.tt('dve', var[:, 0:ntok], mean[:, 0:ntok], mean[:, 0:ntok], ALU.mult, [meank], [vark])
                self.stt(var[:, 0:ntok], pbs[:, 0:ntok], 1.0 / DA, var[:, 0:ntok], ALU.mult, ALU.subtract, [pks, vark], [vark])
                self.act(var[:, 0:ntok], var[:, 0:ntok], AF.Sqrt, [vark], [vark], bias=eps_t[:, 0:1], scale=1.0)
                self.recip(var[:, 0:ntok], var[:, 0:ntok], [vark], [vark])
                for cc in range(4):
                    cv, cvk = cvs[cc]
                    self.tt('dve', cv[:, 0:ntok], cv[:, 0:ntok], mean[:, 0:ntok], ALU.subtract, [cvk, meank], [cvk])
                    self.tt('dve', cv[:, 0:ntok], cv[:, 0:ntok], var[:, 0:ntok], ALU.mult, [cvk, vark], [cvk])
                    self.act(catT[:, cc, 0:ntok], cv[:, 0:ntok], AF.Silu, [cvk, 'lng', 'lnb'], [('catT', cc, 0), ('catT', cc, 1)],
                             scale=lng[:, cc:cc + 1], bias=lnb[:, cc:cc + 1])

            self.marks = []
            mark = lambda n: self.marks.append((n, len(S.ops)))
            for (gt0, ntile, orow) in groups:
                ntok = ntile * 128
                mark('group %d start' % gt0)
                is_halo = orow is None
                first_own = (orow == 0)
                load_x(gt0 * 128, ntile)
                for ti in range(ntile):
                    norm_T(xres[:, ti, :], ('xres', ti), gE[:], 'gE', ti)
                hkeys = [('hT', ti) for ti in range(ntile)]
                kslot, kskey = self.wpiece([w_cols(w_in_e, 1536, 512)])
                vslot, vskey = self.wpiece([w_cols(w_in_e, 2048, 512)])
                kv_tiles(gt0, ntile, kslot, kskey, vslot, vskey, None if is_halo else orow)
                mark('kv done')
                self.tt('dve', kmT[:], tsum[:].rearrange("p h (n two) -> p h n two", two=2)[:, :, :, 0],
                        tsum[:].rearrange("p h (n two) -> p h n two", two=2)[:, :, :, 1], ALU.add, ['tsum'], ['kmT'])
                qslot, qskey = self.wpiece([w_cols(w_in_e, 1024, 512)])
                for ti in range(ntile):
                    own = (gt0 + ti) // 2
                    pb, pk = tok_mm(qslot, qskey, ti)
                    Qf, Qfk = head_norm(pb, pk, gqB, 'gqB', HD ** -0.5)
                    qt_, qtk = nxt('qtf', qtf)
                    for hh2 in range(2):
                        pbq, pkq = self.gbank()
                        for h4 in range(4):
                            h = hh2 * 4 + h4
                            self.tr(pbq[0:64, h4 * 128:(h4 + 1) * 128], Qf[:, h * 64:(h + 1) * 64], identf[:], [Qfk, 'identf'], [pkq])
                        self.cp('act', qt_[:, hh2 * 4:(hh2 + 1) * 4, :], pbq[0:64, :].rearrange("p (h t) -> p h t", h=4), [pkq], [qtk])
                    self.cp('pool', QTaug[0:64, :, ti * 128:(ti + 1) * 128], qt_[:], [qtk], [('QTaug', h) for h in range(H)])
                    pbg, pkg = self.gbank()
                    for h in range(H):
                        self.mm(pbg[:, h * NB:(h + 1) * NB], qt_[:, h, :], kmT[:, h, :], True, True, [qtk, 'kmT'], [pkg])
                    g2, g2k = nxt('g2p', g2p)
                    self.tt('dve', g2[:], pbg[:, 0:H * NB].rearrange("p (h n) -> p h n", h=H),
                            gmask[:, 0, own:own + 1, :].to_broadcast([128, H, NB]), ALU.add, [pkg, 'gmaskt'], [g2k])
                    t8, t8k = nxt('t8p', t8p)
                    for h in range(H):
                        S.op('dve', (lambda e, o=t8[:, h, :], i=g2[:, h, :]: e.max(out=o, in_=i)), [g2k], [t8k])
                    c1, c1k = nxt('g2p', g2p)
                    self.tt('dve', c1[:], g2[:], t8[:, :, 2:3].to_broadcast([128, H, NB]), ALU.is_ge, [g2k, t8k], [c1k])
                    self.ts('dve', g2[:], g2[:], NEG, None, ALU.is_gt, None, [g2k], [g2k])
                    self.tt('dve', c1[:], c1[:], g2[:], ALU.mult, [c1k, g2k], [c1k])
                    self.tt('dve', c1[:], c1[:], scB[:].unsqueeze(2).to_broadcast([128, H, NB]), ALU.mult, [c1k, 'scB'], [c1k])
                    nm, nmk = nxt('nmt', nmt)
                    self.ts('dve', nm[:, :, 0:NB], c1[:], NEG, None, ALU.add, None, [c1k], [nmk])
                    self.cp('dve', nm[:, :, own], b31B[:, 0, :], ['b31B', nmk], [nmk])
                    pbn, pkn = self.gbank()
                    pn = pbn[:].bitcast(BF16)
                    for h in range(H):
                        self.tr(pn[0:32, h * 128:(h + 1) * 128], nm[:, h, :], ident[:], [nmk, 'ident'], [pkn])
                    self.cp('act', QTaug[64:96, :, ti * 128:(ti + 1) * 128], pn[0:32, :].rearrange("p (h t) -> p h t", h=H),
                            [pkn], [('QTaug', h) for h in range(H)])
                mark('q done')
                if first_own:
                    self.ts('dve', Ast[:], Ast[:], hv[:, 0, 0:1], None, ALU.mult, None, ['Ast', 'hvt'], ['Ast'])
                valslot, valk = self.wpiece([w_cols(w_in_e, 0, 512)])
                gateslot, gatek = self.wpiece([w_cols(w_in_e, 512, 512)])
                vv = valslot[:].rearrange("p (c n) -> p c n", c=8)
                gv = gateslot[:].rearrange("p (c n) -> p c n", c=8)
                cvs = []
                for cc in range(4):
                    pbv, pkv = self.gbank()
                    pbg, pkg = self.gbank()
                    for kc in range(8):
                        self.mm(pbv[:, 0:ntok], vv[:, kc, cc * 128:(cc + 1) * 128], hT[:, kc, 0:ntok], kc == 0, kc == 7, hkeys + [valk], [pkv])
                    for kc in range(8):
                        self.mm(pbg[:, 0:ntok], gv[:, kc, cc * 128:(cc + 1) * 128], hT[:, kc, 0:ntok], kc == 0, kc == 7, hkeys + [gatek], [pkg])
                    sg, sgk = nxt('f512', f512)
                    self.act(sg[:, 0:ntok], pbg[:, 0:ntok], AF.Sigmoid, [pkg], [sgk])
                    ab, abk = nxt('abuf', abuf)
                    self.cp('pool', ab[:, 0:30], Ast[:, cc, :], ['Ast'], [abk])
                    self.tt('dve', ab[:, 30:30 + ntok], pbv[:, 0:ntok], sg[:, 0:ntok], ALU.mult, [pkv, sgk], [abk])
                    self.cp('pool', Ast[:, cc, :], ab[:, ntok:ntok + 30], [abk], ['Ast'])
                    cv, cvk = nxt('cvA', self.cvA)
                    self.act(cv[:, 0:ntok], ab[:, 30:30 + ntok], AF.Identity, [abk, 'caw', 'cab'], [cvk],
                             scale=caw[:, cc, 30:31], bias=cab[:, cc:cc + 1])
                    for j in range(30):
                        self.stt(cv[:, 0:ntok], ab[:, j:j + ntok], caw[:, cc, j:j + 1], cv[:, 0:ntok], ALU.mult, ALU.add,
                                 [abk, cvk, 'caw'], [cvk])
                    cvs.append((cv, cvk))
                ln_silu(cvs, ntok)
                mark('mixerA done')
                attention(gt0, ntile)
                mark('attn done')
                out_proj(w_out_e, ntile, lambda kc: [('catT', kc, 0), ('catT', kc, 1)])
                mark('outproj done')
                ffn(0, ntile, first_own)
                mark('ffn0 done')
                for ti in range(ntile):
                    norm_T(xres[:, ti, :], ('xres', ti), gO[:], 'gO', ti)
                if first_own:
                    self.ts('dve', Cst[:], Cst[:], hv[:, 0, 0:1], None, ALU.mult, None, ['Cst', 'hvt'], ['Cst'])
                for c in range(8):
                    if c % 4 == 0:
                        cslots = [self.wpiece([w_cols(w_in_o, k * 1024 + (c // 4) * 512, 512)]) for k in range(3)]
                    pbs_ = []
                    for k in range(3):
                        pb, pk = self.gbank()
                        sv = cslots[k][0][:].rearrange("p (c n) -> p c n", c=8)
                        for kc in range(8):
                            self.mm(pb[:, 0:ntok], sv[:, kc, (c % 4) * 128:(c % 4 + 1) * 128], hT[:, kc, 0:ntok], kc == 0, kc == 7,
                                    hkeys + [cslots[k][1]], [pk])
                        pbs_.append((pb, pk))
                    uu, uuk = nxt('f512', f512)
                    self.cp('act', uu[:, 0:ntok], pbs_[2][0][:, 0:ntok], [pbs_[2][1]], [uuk])
                    cb_, cbk = nxt('ubuf', self.ubuf)
                    self.cp('pool', cb_[:, 0:2], Cst[:, c, :], ['Cst'], [cbk])
                    self.tt('dve', cb_[:, 2:2 + ntok], pbs_[1][0][:, 0:ntok], uu[:, 0:ntok], ALU.mult, [pbs_[1][1], uuk], [cbk])
                    self.cp('pool', Cst[:, c, :], cb_[:, ntok:ntok + 2], [cbk], ['Cst'])
                    cv, cvk = nxt('f512', f512)
                    self.act(cv[:, 0:ntok], cb_[:, 2:2 + ntok], AF.Identity, [cbk, 'ccw'], [cvk], scale=ccw[:, c, 2:3])
                    self.stt(cv[:, 0:ntok], cb_[:, 1:1 + ntok], ccw[:, c, 1:2], cv[:, 0:ntok], ALU.mult, ALU.add, [cbk, cvk, 'ccw'], [cvk])
                    self.stt(cv[:, 0:ntok], cb_[:, 0:ntok], ccw[:, c, 0:1], cv[:, 0:ntok], ALU.mult, ALU.add, [cbk, cvk, 'ccw'], [cvk])
                    self.tt('dve', catT[:, c, 0:ntok], pbs_[0][0][:, 0:ntok], cv[:, 0:ntok], ALU.mult, [pbs_[0][1], cvk],
                            [('catT', c, 0), ('catT', c, 1)])
                out_proj(w_out_o, ntile, lambda kc: [('catT', kc, 0), ('catT', kc, 1)])
                ffn(1, ntile, first_own)
                if not is_halo:
                    for ti in range(ntile):
                        self.dma('sp', y_out[orow + ti * 128:orow + (ti + 1) * 128, :], xres[:, ti, :], [('xres', ti)], [])


            def wslot_take():
                i = self.wi % self.NSLOT
                self.wi += 1
                return self.wsl[i], ('w', i)

            def sample_phase():
                def cload(name, shape, src, dt=F32, q='act'):
                    t = self.sb(name + "_t", shape, dt)
                    self.dma(q, t[:], src, (), [name])
                    return t
                oh16 = cload("oh16", [30, 256], oh16_d)
                W30 = cload("W30", [30, DA], conv_a_w[0:30, :])
                ohS = cload("ohS", [33, 256], ohS_d)
                ptb = self.sb("ptb_t", [128, 1, 128], I32)
                pt3 = pt_d.rearrange("o (sn two) -> o sn two", two=2)
                self.dma('act', ptb[0:64], pt3[:, :, 0].partition_broadcast(64), (), ['ptb'], slow=True)
                self.dma('act', ptb[64:128], pt3[:, :, 1].partition_broadcast(64), (), ['ptb'], slow=True)
                rb0B = cload("rb0B", [128, 1, H], rel_bias[0:1, :].partition_broadcast(128))
                io = self.sb("io", [128, 1], I32)
                iof = self.sb("iof", [128, 1], F32)
                idx = self.sb("idx", [128, 128], I32)
                S.op('pool', lambda e: e.iota(io[0:64, :], pattern=[[0, 1]], base=0, channel_multiplier=1), (), ['io'])
                S.op('pool', lambda e: e.iota(io[64:128, :], pattern=[[0, 1]], base=0, channel_multiplier=1), (), ['io'])
                self.cp('pool', iof[:], io[:], ['io'], ['iof'])
                self.ts('dve', idx[:], ptb[:, 0, :], 64.0, iof[:, 0:1], ALU.mult, ALU.add, ['ptb', 'iof'], ['idx'])
                self.tt('dve', rb0B[:, 0, :], rb0B[:, 0, :], b31B[:, 0, :], ALU.subtract, ['rb0B', 'b31B'], ['rb0B'])
                biasP = self.sb("biasP", [128, 2 * H], F32)
                pb, pk = self.gbank()
                for slot in range(2):
                    self.mm(pb[:, slot * H:(slot + 1) * H], ohS[:, slot * 128:(slot + 1) * 128], rb[:], True, True, ['ohS', 'rb'], [pk])
                self.cp('dve', biasP[:], pb[:, 0:2 * H], [pk], ['biasP'])
                VS = self.sb("VS", [128, 512], F32)
                AsT = self.sb("AsT", [128, 4, 16], F32)
                small = self.sb("smalls", [16, 64], F32)
                self.dma('sp', xres[0:16, 0, :], xs_d, (), [('xres', 0)])
                norm_T(xres[:, 0, :], ('xres', 0), gE[:], 'gE', 0)
                hk0 = [('hT', 0)]
                kslot, kskey = self.wpiece([w_cols(w_in_e, 1536, 512)])
                pb, pk = tok_mm(kslot, kskey, 0)
                Kf, Kfk = head_norm(pb, pk, gkB, 'gkB', 1.0)
                self.dma('sp', ks_out, Kf[0:16, :], [Kfk], [])
                vslot, vskey = self.wpiece([w_cols(w_in_e, 2048, 512)])
                pbv, pkv = tok_mm(vslot, vskey, 0)
                self.cp('act', VS[:], pbv[:], [pkv], ['VS'])
                self.dma('sp', vs_out, VS[0:16, :], ['VS'], [])
                qslot, qskey = self.wpiece([w_cols(w_in_e, 1024, 512)])
                pb, pk = tok_mm(qslot, qskey, 0)
                Qf, Qfk = head_norm(pb, pk, gqB, 'gqB', HD ** -0.5)
                self.dma('sp', qs_scr, Qf[0:16, :], [Qfk], ['qs_scr'])
                lself = small[:, 0:8]
                tmpq, tmpqk = nxt('f512', f512)
                self.tt('dve', tmpq[0:16, :], Qf[0:16, :], Kf[0:16, :], ALU.mult, [Qfk, Kfk], [tmpqk])
                self.red(lself, tmpq[0:16, :].rearrange("p (h d) -> p h d", h=H), ALU.add, [tmpqk], ['small'])
                valslot, valk = self.wpiece([w_cols(w_in_e, 0, 512)])
                gateslot, gatek = self.wpiece([w_cols(w_in_e, 512, 512)])
                vv = valslot[:].rearrange("p (c n) -> p c n", c=8)
                gv = gateslot[:].rearrange("p (c n) -> p c n", c=8)
                for cc in range(4):
                    pbv, pkv = self.gbank()
                    pbg, pkg = self.gbank()
                    for kc in range(8):
                        self.mm(pbv[:, 0:16], vv[:, kc, cc * 128:(cc + 1) * 128], hT[:, kc, 0:16], kc == 0, kc == 7, hk0 + [valk], [pkv])
                    for kc in range(8):
                        self.mm(pbg[:, 0:16], gv[:, kc, cc * 128:(cc + 1) * 128], hT[:, kc, 0:16], kc == 0, kc == 7, hk0 + [gatek], [pkg])
                    sg, sgk = nxt('f512', f512)
                    self.act(sg[:, 0:16], pbg[:, 0:16], AF.Sigmoid, [pkg], [sgk])
                    self.tt('dve', AsT[:, cc, :], pbv[:, 0:16], sg[:, 0:16], ALU.mult, [pkv, sgk], ['AsT'])
                accb, acck = self.gbank()
                for s_ in range(16):
                    stA, stAk = nxt('f512', f512)
                    self.dma('sp', stA[0:30, :], sta_d[s_], (), [stAk])
                    self.tt('dve', stA[0:30, :], stA[0:30, :], W30[:], ALU.mult, [stAk, 'W30'], [stAk])
                    self.mm(accb[0:16, :], oh16[:, s_ * 16:(s_ + 1) * 16], stA[0:30, :], s_ == 0, s_ == 15, ['oh16', stAk], [acck])
                cst, cstk = nxt('f512', f512)
                self.cp('dve', cst[0:16, :], accb[0:16, :], [acck], [cstk])
                pbt, pkt = self.gbank()
                for cc in range(4):
                    self.tr(pbt[:, cc * 16:(cc + 1) * 16], cst[0:16, cc * 128:(cc + 1) * 128], identf[0:16, 0:16], [cstk, 'identf'], [pkt])
                cvs = []
                for cc in range(4):
                    cv, cvk = nxt('cvA', self.cvA)
                    self.act(cv[:, 0:16], AsT[:, cc, :], AF.Identity, ['AsT', 'caw', 'cab'], [cvk], scale=caw[:, cc, 30:31], bias=cab[:, cc:cc + 1])
                    self.tt('dve', cv[:, 0:16], cv[:, 0:16], pbt[:, cc * 16:(cc + 1) * 16], ALU.add, [cvk, pkt], [cvk])
                    cvs.append((cv, cvk))
                ln_silu(cvs, 16)
                self.dma('act', cas_out[:, 0:29, :], sta_d[:, 1:30, :], (), [])
                pba, pka = self.gbank()
                for cc in range(4):
                    self.tr(pba[0:16, cc * 128:(cc + 1) * 128], AsT[:, cc, :], identf[:], ['AsT', 'identf'], [pka])
                atok, atokk = nxt('f512', f512)
                self.cp('dve', atok[0:16, :], pba[0:16, :], [pka], [atokk])
                self.dma('sp', cas_out[:, 29, :], atok[0:16, :], [atokk], [])
                lself = small[:, 0:8]
                denAll = small[:, 8:16]
                dtot = small[:, 16:24]
                self.tt('dve', lself, lself, rb0B[0:16, 0, :], ALU.add, ['small', 'rb0B'], ['small'])
                self.act(lself, lself, AF.Exp, ['small'], ['small'])
                self.memset('pool', denAll, 0.0, ['small'])
                bufA = big[:, 0:8192].bitcast(F32).rearrange("p (g n) -> p g n", g=8)
                bufB = big[:, 8192:16384].bitcast(F32).rearrange("p (g n) -> p g n", g=8)
                bufs = [(bufA, [('KTb', 0), ('KTb', 1)]), (bufB, [('VAb', 0), ('VAb', 1)])]
                self.bi = 0

                def gather(src, s_, half):
                    bf_, bk_ = bufs[self.bi % 2]
                    self.bi += 1
                    for bl in range(4):
                        col = s_ * 8 + half * 4 + bl
                        S.op('pool', (lambda e, o=bf_[:, 2 * bl:2 * bl + 2, :].rearrange("p a n -> p (a n)"), ix=idx[:, col:col + 1]:
                                      e.indirect_dma_start(out=o, out_offset=None, in_=src,
                                                           in_offset=bass.IndirectOffsetOnAxis(ap=ix, axis=0))),
                             ['idx'], bk_, dma=True)
                    return bf_, bk_
                vbb = [wslot_take() for _ in range(2)]
                pzs, pzk = wslot_take()
                Pz = pzs[:].rearrange("p (pg hg c) -> p pg hg c", pg=16, hg=2)
                Pz4 = pzs[:].rearrange("p (pg h sl) -> p pg h sl", pg=16, h=H)
                self.memset('pool', pzs[:], 0.0, [pzk])
                Lt = self.sb("Lt", [128, 128], F32)
                Pf = self.sb("Pf", [128, 128], F32)
                gsb = self.sb("gsb", [128, 8, 8], F32)
                c1b = self.sb("c1b", [128, 8, 8], F32)
                t8s = self.sb("t8s", [128, H, 8], F32)
                den8 = self.sb("den8", [128, H], F32)
                ob0, ok0 = self.ps[6], ('ps', 6)
                ob1, ok1 = self.ps[7], ('ps', 7)
                obs = [(ob0, ok0), (ob1, ok1)]
                for s_ in range(16):
                    qb, qbk = nxt('f512', f512)
                    self.dma('sp', qb[:].rearrange("p (o n) -> p o n", o=1), qs_scr[s_:s_ + 1, :].partition_broadcast(128), ['qs_scr'], [qbk])
                    for half in range(2):
                        Kb, Kbk = gather(ck_d, s_, half)
                        self.tt('dve', Kb, Kb, qb[:].unsqueeze(1).to_broadcast([128, 8, 512]), ALU.mult, Kbk + [qbk], Kbk)
                        self.red(Lt[:, half * 64:(half + 1) * 64], Kb.rearrange("p g (h d) -> p (g h) d", h=H), ALU.add, Kbk, ['Lt'])
                    pbg, pkg = self.gbank()
                    self.mm(pbg[:, 0:128], onesf[:], Lt[:], True, True, ['onesf', 'Lt'], [pkg])
                    G4 = pbg[:, 0:128].rearrange("p (n two h) -> p n two h", n=8, two=2)
                    self.cp('dve', gsb[:], G4[:, :, 0, :], [pkg], ['gsb'])
                    self.tt('dve', gsb[:], gsb[:], G4[:, :, 1, :], ALU.add, [pkg, 'gsb'], ['gsb'])
                    for h in range(H):
                        S.op('dve', (lambda e, o=t8s[:, h, :], i=gsb[:, :, h]: e.max(out=o, in_=i)), ['gsb'], ['t8s'])
                    self.tt('dve', c1b[:], gsb[:], t8s[:, :, 2].unsqueeze(1).to_broadcast([128, 8, H]), ALU.is_ge, ['gsb', 't8s'], ['c1b'])
                    self.ts('dve', c1b[:], c1b[:], -NEG, NEG, ALU.mult, ALU.add, ['c1b'], ['c1b'])
                    L4 = Lt[:].rearrange("p (n two h) -> p n two h", n=8, two=2)
                    self.tt('dve', L4, L4, c1b[:].unsqueeze(2).to_broadcast([128, 8, 2, H]), ALU.add, ['Lt', 'c1b'], ['Lt'])
                    self.tt('dve', Lt[:, 112:128], Lt[:, 112:128], biasP[:], ALU.add, ['Lt', 'biasP'], ['Lt'])
                    self.act(Pz4[:, :, :, s_], Lt[:].rearrange("p (pg h) -> p pg h", pg=16), AF.Exp, ['Lt'], [pzk])
                    self.cp('dve', Pf[:].rearrange("p (pg h) -> p pg h", pg=16), Pz4[:, :, :, s_], [pzk], ['Pf'])
                    pbd, pkd = self.gbank()
                    self.mm(pbd[:, 0:128], onesf[:], Pf[:], True, True, ['onesf', 'Pf'], [pkd])
                    self.red(den8[:], pbd[:, 0:128].rearrange("p (pg h) -> p h pg", pg=16), ALU.add, [pkd], ['den8'])
                    self.stt(denAll, den8[0:16, :], identf[0:16, s_:s_ + 1], denAll, ALU.mult, ALU.add, ['den8', 'identf', 'small'], ['small'])
                    for half in range(2):
                        vb_, vbk_ = vbb[half]
                        vb3 = vb_[:].rearrange("p (g n) -> p g n", g=8)
                        Vb, Vbk = gather(cv_d, s_, half)
                        self.cp('act', vb3, Vb, Vbk, [vbk_])
                        for pg in range(8):
                            page = half * 8 + pg
                            for hg in range(2):
                                self.mm(obs[hg][0][:], Pz[:, page, hg, :], vb3[:, pg, :], (s_ == 0 and page == 0), (s_ == 15 and page == 15),
                                        [pzk, vbk_], [obs[hg][1]])
                    S.op('dve', (lambda e, o=Pz4[:, :, :, s_]: e.memset(o, 0.0)), (), [pzk])
                otok_t, otokk = nxt('f512', f512)
                otok = otok_t[0:16, :]
                for h in range(H):
                    hg, hl = h // 4, h % 4
                    self.cp('dve', otok[:, h * 64:(h + 1) * 64], obs[hg][0][32 * hl:32 * hl + 16, h * 64:(h + 1) * 64], [obs[hg][1]], [otokk])
                o3 = otok.rearrange("p (h d) -> p h d", h=H)
                tv, tvk = nxt('f512', f512)
                tv3 = tv[0:16, :].rearrange("p (h d) -> p h d", h=H)
                self.tt('dve', tv3, VS[0:16, :].rearrange("p (h d) -> p h d", h=H), lself.unsqueeze(2).to_broadcast([16, H, HD]), ALU.mult,
                        ['VS', 'small'], [tvk])
                self.tt('dve', o3, o3, tv3, ALU.add, [otokk, tvk], [otokk])
                self.tt('dve', dtot, denAll, lself, ALU.add, ['small'], ['small'])
                self.recip(dtot, dtot, ['small'], ['small'])
                self.tt('dve', o3, o3, dtot.unsqueeze(2).to_broadcast([16, H, HD]), ALU.mult, [otokk, 'small'], [otokk])
                pbo, pko = self.gbank()
                for cc in range(4):
                    self.tr(pbo[:, cc * 16:(cc + 1) * 16], otok[:, cc * 128:(cc + 1) * 128], identf[0:16, 0:16], [otokk, 'identf'], [pko])
                self.cp('dve', catT[:, 4:8, 0:16], pbo[:, 0:64].rearrange("p (c t) -> p c t", c=4), [pko],
                        [('catT', 4 + c_, k_) for c_ in range(4) for k_ in range(2)])
                out_proj(w_out_e, 1, lambda kc: [('catT', kc, 0), ('catT', kc, 1)])
                ffn_s(0)
                norm_T(xres[:, 0, :], ('xres', 0), gO[:], 'gO', 0)
                stcT = self.sb("stcT", [128, 8, 2, 16], F32)
                cuT = self.sb("cuT", [128, 8, 16], F32)
                for j in range(2):
                    for hh in range(2):
                        tl, tlk = nxt('f512', f512)
                        self.dma('sp', tl[0:16, :], stc_d[:, j, hh * 512:(hh + 1) * 512], (), [tlk])
                        pbt, pkt = self.gbank()
                        for c4 in range(4):
                            self.tr(pbt[:, c4 * 16:(c4 + 1) * 16], tl[0:16, c4 * 128:(c4 + 1) * 128], identf[0:16, 0:16], [tlk, 'identf'], [pkt])
                        self.cp('dve', stcT[:, hh * 4:(hh + 1) * 4, j, :], pbt[:, 0:64].rearrange("p (c t) -> p c t", c=4), [pkt], ['stcT'])
                for c in range(8):
                    if c % 4 == 0:
                        cslots = [self.wpiece([w_cols(w_in_o, k * 1024 + (c // 4) * 512, 512)]) for k in range(3)]
                    pbs_ = []
                    for k in range(3):
                        pb, pk = self.gbank()
                        sv = cslots[k][0][:].rearrange("p (c n) -> p c n", c=8)
                        for kc in range(8):
                            self.mm(pb[:, 0:16], sv[:, kc, (c % 4) * 128:(c % 4 + 1) * 128], hT[:, kc, 0:16], kc == 0, kc == 7,
                                    hk0 + [cslots[k][1]], [pk])
                        pbs_.append((pb, pk))
                    uu, uuk = nxt('f512', f512)
                    self.cp('act', uu[:, 0:16], pbs_[2][0][:, 0:16], [pbs_[2][1]], [uuk])
                    self.tt('dve', cuT[:, c, :], pbs_[1][0][:, 0:16], uu[:, 0:16], ALU.mult, [pbs_[1][1], uuk], ['cuT'])
                    cv, cvk = nxt('f512', f512)
                    self.act(cv[:, 0:16], cuT[:, c, :], AF.Identity, ['cuT', 'ccw'], [cvk], scale=ccw[:, c, 2:3])
                    self.stt(cv[:, 0:16], stcT[:, c, 1, :], ccw[:, c, 1:2], cv[:, 0:16], ALU.mult, ALU.add, ['stcT', cvk, 'ccw'], [cvk])
                    self.stt(cv[:, 0:16], stcT[:, c, 0, :], ccw[:, c, 0:1], cv[:, 0:16], ALU.mult, ALU.add, ['stcT', cvk, 'ccw'], [cvk])
                    self.tt('dve', catT[:, c, 0:16], pbs_[0][0][:, 0:16], cv[:, 0:16], ALU.mult, [pbs_[0][1], cvk],
                            [('catT', c, 0), ('catT', c, 1)])
                self.dma('act', ccs_out[:, 0, :], stc_d[:, 1, :], (), [])
                for hh in range(2):
                    pbc, pkc = self.gbank()
                    for c4 in range(4):
                        self.tr(pbc[0:16, c4 * 128:(c4 + 1) * 128], cuT[:, hh * 4 + c4, :], identf[:], ['cuT', 'identf'], [pkc])
                    ct, ctk = nxt('f512', f512)
                    self.cp('dve', ct[0:16, :], pbc[0:16, :], [pkc], [ctk])
                    self.dma('sp', ccs_out[:, 1, hh * 512:(hh + 1) * 512], ct[0:16, :], [ctk], [])
                out_proj(w_out_o, 1, lambda kc: [('catT', kc, 0), ('catT', kc, 1)])
                ffn_s(1)
                self.dma('sp', ys_out, xres[0:16, 0, :], [('xres', 0)], [])

            def ffn_s(l):
                norm_T(xres[:, 0, :], ('xres', 0), gF[:, l, :], 'gF', 0)
                hk0 = [('hT', 0)]
                stfT = self.stfT
                upT = self.upT
                for j in range(2):
                    for blk in range(11):
                        tl, tlk = nxt('f512', f512)
                        self.dma('sp', tl[0:16, :], stf_d[l, :, j, blk * 512:(blk + 1) * 512], (), [tlk])
                        pbt, pkt = self.gbank()
                        for c4 in range(4):
                            self.tr(pbt[:, c4 * 16:(c4 + 1) * 16], tl[0:16, c4 * 128:(c4 + 1) * 128], identf[0:16, 0:16], [tlk, 'identf'], [pkt])
                        self.cp('dve', stfT[:, blk * 4:(blk + 1) * 4, j, :], pbt[:, 0:64].rearrange("p (c t) -> p c t", c=4), [pkt], ['stfT'])
                for c4 in range((NPAIR + 3) // 4):
                    ncol = min(4, NPAIR - c4 * 4) * 128
                    uslots = [self.wpiece([w_cols(w_up[l], part * DFF + c4 * 512, ncol)]) for part in range(2)]
                    for cl in range(ncol // 128):
                        c = c4 * 4 + cl
                        res = []
                        for part in range(2):
                            ci = c + part * NPAIR
                            pb, pk = self.gbank()
                            for kc in range(8):
                                self.mm(pb[:, 0:16], uslots[part][0][:, 0:8 * ncol].rearrange("p (c n) -> p c n", c=8)[:, kc, cl * 128:(cl + 1) * 128],
                                        hT[:, kc, 0:16], kc == 0, kc == 7, hk0 + [uslots[part][1]], [pk])
                            self.cp('act', upT[:, ci, :], pb[:, 0:16], [pk], ['upT'])
                            cv, cvk = nxt('f512', f512)
                            self.act(cv[:, 0:16], upT[:, ci, :], AF.Identity, ['upT', 'cfw', 'cfb'], [cvk],
                                     scale=cfw[:, l, ci, 2:3], bias=cfb[:, l, ci:ci + 1])
                            self.stt(cv[:, 0:16], stfT[:, ci, 1, :], cfw[:, l, ci, 1:2], cv[:, 0:16], ALU.mult, ALU.add, ['stfT', cvk, 'cfw'], [cvk])
                            self.stt(cv[:, 0:16], stfT[:, ci, 0, :], cfw[:, l, ci, 0:1], cv[:, 0:16], ALU.mult, ALU.add, ['stfT', cvk, 'cfw'], [cvk])
                            res.append((cv, cvk))
                        (cg, cgk), (cu, cuk) = res
                        self.act(cg[:, 0:16], cg[:, 0:16], AF.Silu, [cgk], [cgk])
                        self.tt('dve', mT[:, c, 0:16], cg[:, 0:16], cu[:, 0:16], ALU.mult, [cgk, cuk], mkeys(c))
                self.dma('act', fs_out[l, :, 0, :], stf_d[l, :, 1, :], (), [])
                for blk in range(11):
                    pbc, pkc = self.gbank()
                    for c4 in range(4):
                        self.tr(pbc[0:16, c4 * 128:(c4 + 1) * 128], upT[:, blk * 4 + c4, :], identf[:], ['upT', 'identf'], [pkc])
                    ct, ctk = nxt('f512', f512)
                    self.cp('dve', ct[0:16, :], pbc[0:16, :], [pkc], [ctk])
                    self.dma('sp', fs_out[l, :, 1, blk * 512:(blk + 1) * 512], ct[0:16, :], [ctk], [])
                ffn_down(l, 1)

            if self.do_sample:
                self.stfT = self.sb("stfT", [128, 44, 2, 16], F32)
                self.upT = self.sb("upT", [128, 44, 16], F32)
                sample_phase()

            def fm_out(src_fn, nchunk, r, dst):
                c = 0
                while c < nchunk:
                    n = min(4, nchunk - c)
                    pb, pk = self.gbank()
                    for i in range(n):
                        ap_, keys = src_fn(c + i)
                        self.tr(pb[0:r, i * 128:(i + 1) * 128], ap_, identf[:], keys + ['identf'], [pk])
                    o, ok = nxt('f512', f512)
                    self.cp('dve', o[0:r, 0:n * 128], pb[0:r, 0:n * 128], [pk], [ok])
                    self.dma('sp', dst[:, c * 128:(c + n) * 128], o[0:r, 0:n * 128], [ok], [])
                    c += n
            fm_out(lambda c: (Ast[:, c, :], ['Ast']), 4, 30, ca_out)
            fm_out(lambda c: (Cst[:, c, :], ['Cst']), 8, 2, cc_out)
            for l in range(2):
                fm_out(lambda c, l=l: (Ust[:, l, c, :], ['Ust']), 44, 2, f_out[l])


            self.nops = len(S.ops)
            if os.environ.get('KSKIPOUT'):
                S.ops = [o for o in S.ops if not (o['dma'] and len(o['w']) == 0)]
            if os.environ.get('KSTOP'):
                S.ops = S.ops[:int(os.environ['KSTOP'])]
            S.emit(st)
            print(self.marks)
            print('nops', self.nops, 'emitted', len(S.ops), 'sem counts', S.max_counts, flush=True)
        return nc


def _bf(x):
    return x


def make_consts(C):
    NB = 4 * C // 256
    oh = np.zeros((33, 384), np.float32)
    for m in range(383):
        d = m - 127
        if d < 0:
            oh[32, m] = 1.0
        else:
            oh[int(t5_bucket_np(np.array([d]))[0]), m] = 1.0
    return oh


def core_inputs(inputs, c, C, SEQ, do_sample=True):
    b, j = c // 4, c % 4
    P = 3 * C
    NK = 4 * C
    NB = NK // 256
    x = inputs['x_prompt'][b]
    xk = np.zeros((NK, D), np.float32)
    lo = C * j - P
    src_lo = max(lo, 0)
    xk[src_lo - lo:, :] = x[src_lo:C * j + C]
    nvalid0 = (src_lo - lo) // 256
    gmask = np.zeros((NB, NB), np.float32)
    for own in range(NB):
        for n in range(NB):
            if n >= own or n < nvalid0:
                gmask[own, n] = -60000.0
    hv = np.array([[0.0 if j == 0 else 1.0]], np.float32)
    m = dict(xk=xk, gmask=gmask, hv=hv, oh=make_consts(C))
    for k in ('w_in_e', 'w_out_e', 'w_in_o', 'w_out_o', 'norm_mix_e', 'norm_mix_o', 'conv_a_w', 'conv_a_b', 'ln_a_g',
              'ln_a_b', 'q_norm_g', 'k_norm_g', 'conv_c_w'):
        m[k] = np.ascontiguousarray(inputs[k][0])
    for k in ('w_up', 'w_down', 'norm_ffn', 'conv_f_w', 'conv_f_b', 'rel_bias'):
        m[k] = np.ascontiguousarray(inputs[k])
    if do_sample:
        s0 = 16 * c
        m['xs'] = np.ascontiguousarray(inputs['x_sample'][s0:s0 + 16, 0, :])
        m['pt'] = np.ascontiguousarray(inputs['page_table'][s0:s0 + 16].reshape(1, 256).astype(np.int32))
        ck = inputs['cache_k'][0]
        m['cache_k'] = ck.reshape(ck.shape[0] * 64, 1024)
        cv = inputs['cache_v'][0]
        m['cache_v'] = cv.reshape(cv.shape[0] * 64, 1024)
        m['state_a'] = np.ascontiguousarray(inputs['state_conv_a'][0, s0:s0 + 16])
        m['state_c'] = np.ascontiguousarray(inputs['state_conv_c'][0, s0:s0 + 16])
        m['state_f'] = np.ascontiguousarray(inputs['state_ffn'][:, s0:s0 + 16])
        ohS = np.zeros((33, 2, 128), np.float32)
        for slot in range(2):
            for p in range(64, 128):
                kk = 2 * (p - 64) + slot
                ohS[int(t5_bucket_np(np.array([128 - kk]))[0]), slot, p] = 1.0
        ohS = ohS.reshape(33, 256)
        m['ohS'] = ohS
        oh16 = np.zeros((30, 16, 16), np.float32)
        for s_ in range(16):
            oh16[:, s_, s_] = 1.0
        m['oh16'] = oh16.reshape(30, 256)
    return m


_NC_CACHE = {}


def run_all(inputs, C, SEQ, do_sample=True):
    npool = inputs['cache_k'].shape[1] if do_sample else 2560
    key = (C, do_sample, npool)
    if key not in _NC_CACHE:
        bld = Builder(C, do_sample=do_sample, npool=npool)
        _NC_CACHE[key] = bld.build()
    nc = _NC_CACHE[key]
    in_maps = [core_inputs(inputs, c, C, SEQ, do_sample) for c in range(8)]
    res = run_bass_kernel_spmd(nc, in_maps, core_ids=list(range(8)))
    return res.results


def run_prompt(inputs, C, SEQ):
    return run_all(inputs, C, SEQ, do_sample=False)


def assemble(res, C, SEQ):
    y = np.zeros((2, SEQ, D), np.float32)
    k = np.zeros((2, SEQ, 512), np.float32)
    v = np.zeros((2, SEQ, 512), np.float32)
    for c in range(8):
        b, j = c // 4, c % 4
        y[b, C * j:C * j + C] = res[c]['y_out']
        k[b, C * j:C * j + C] = res[c]['k_out']
        v[b, C * j:C * j + C] = res[c]['v_out']
    npg = SEQ // 128
    k_prompt = k.reshape(1, 2, npg, 128, H, HD)
    v_prompt = v.reshape(1, 2, npg, 128, H, HD)
    a_p = np.stack([res[3]['ca_out'], res[7]['ca_out']])[None]
    c_p = np.stack([res[3]['cc_out'], res[7]['cc_out']])[None]
    f_p = np.stack([np.stack([res[3]['f_out'][l], res[7]['f_out'][l]]) for l in range(2)])
    ys = np.concatenate([res[c]['ys_out'] for c in range(8)], 0)[:, None, :]
    ks = np.concatenate([res[c]['ks_out'] for c in range(8)], 0).reshape(1, 128, 1, H, HD)
    vs = np.concatenate([res[c]['vs_out'] for c in range(8)], 0).reshape(1, 128, 1, H, HD)
    a_s = np.concatenate([res[c]['cas_out'] for c in range(8)], 0)[None]
    c_s = np.concatenate([res[c]['ccs_out'] for c in range(8)], 0)[None]
    f_s = np.concatenate([res[c]['fs_out'] for c in range(8)], 1)
    outs = (y, ys, k_prompt, v_prompt, a_p, c_p, f_p, ks, vs, a_s, c_s, f_s)
    return tuple(np.ascontiguousarray(o, dtype=np.float32) for o in outs)


def kernel(**inputs):
    inputs = {k: np.asarray(v) for k, v in inputs.items()}
    SEQ = inputs['x_prompt'].shape[1]
    C = SEQ // 4
    res = run_all(inputs, C, SEQ, do_sample=True)
    return assemble(res, C, SEQ)
```

```python
import math
import os
from contextlib import ExitStack
import numpy as np
import concourse.bass as bass
import concourse.mybir as mybir
from concourse.bass_utils import run_bass_kernel_spmd

F32 = mybir.dt.float32
BF16 = mybir.dt.bfloat16
I32 = mybir.dt.int32
AF = mybir.ActivationFunctionType
ALU = mybir.AluOpType
AX = mybir.AxisListType

D = 1024
DA = 512
H = 8
HD = 64
DFF = 2816
NPAIR = 22
EPS = 1e-6
NEG = -30000.0
ENG = ('pe', 'act', 'dve', 'pool', 'sp')


class Sched:
    NDMA = 12

    def __init__(self, nc):
        self.nc = nc
        self.ops = []
        self.eng = {'pe': nc.tensor, 'act': nc.scalar, 'dve': nc.vector, 'pool': nc.gpsimd, 'sp': nc.sync}

    def op(self, engine, fn, reads=(), writes=(), dma=False):
        self.ops.append(dict(e=engine, fn=fn, r=tuple(reads), w=tuple(writes), dma=dma,
                             sig=False, seq=None, waits=[]))

    def emit(self, stack):
        nc = self.nc
        ops = self.ops
        last_w = {}
        readers = {}
        seqc = {e: 0 for e in ENG}
        waited = {e: {y: -1 for y in ENG} for e in ENG}
        waited_dma = {e: set() for e in ENG}
        dma_cnt = {e: 0 for e in ENG}
        for i, o in enumerate(ops):
            e = o['e']
            deps = set()
            for k in o['r']:
                if k in last_w:
                    deps.add(last_w[k])
            for k in o['w']:
                if k in last_w:
                    deps.add(last_w[k])
                for j in readers.get(k, ()):
                    deps.add(j)
            deps.discard(i)
            need = {}
            for j in deps:
                oj = ops[j]
                if oj['dma']:
                    if j not in waited_dma[e]:
                        waited_dma[e].add(j)
                        o['waits'].append(('dma', j))
                else:
                    y = oj['e']
                    if y == 'pe' and e == 'pe':
                        continue
                    if oj['seq'] > waited[e][y]:
                        need[y] = max(need.get(y, -1), oj['seq'])
            for y, s in need.items():
                waited[e][y] = s
                o['waits'].append(('cmp', y, s))
            if o['dma']:
                n = dma_cnt[e]
                dma_cnt[e] += 1
                o['dsem'] = (e, n % self.NDMA)
                o['dval'] = 16 * (n // self.NDMA + 1)
            else:
                o['seq'] = seqc[e]
                seqc[e] += 1
            for k in o['r']:
                readers.setdefault(k, []).append(i)
            for k in o['w']:
                last_w[k] = i
                readers[k] = []
        by_seq = {e: {} for e in ENG}
        for i, o in enumerate(ops):
            if not o['dma']:
                by_seq[o['e']][o['seq']] = i
        for o in ops:
            for w in o['waits']:
                if w[0] == 'cmp':
                    ops[by_seq[w[1]][w[2]]]['sig'] = True
        cnt = {e: 0 for e in ENG}
        for o in ops:
            if not o['dma'] and o['sig']:
                cnt[o['e']] += 1
                o['cnt'] = cnt[o['e']]
        self.max_counts = dict(cnt)
        csem = {e: stack.enter_context(nc.semaphore('c_' + e)) for e in ENG}
        dsem = {}
        for e in ENG:
            for n in range(min(self.NDMA, dma_cnt[e])):
                dsem[(e, n)] = stack.enter_context(nc.semaphore('d_%s_%d' % (e, n)))
        for i, o in enumerate(ops):
            e = o['e']
            eng = self.eng[e]
            for w in o['waits']:
                if w[0] == 'dma':
                    oj = ops[w[1]]
                    eng.wait_ge(dsem[oj['dsem']], oj['dval'])
                else:
                    oj = ops[by_seq[w[1]][w[2]]]
                    eng.wait_ge(csem[w[1]], oj['cnt'])
            if o['dma']:
                if o['dval'] > 16:
                    eng.wait_ge(dsem[o['dsem']], o['dval'] - 16)
                ins = o['fn'](eng)
                ins.then_inc(dsem[o['dsem']], 16)
            else:
                ins = o['fn'](eng)
                if o['sig']:
                    ins.then_inc(csem[e], 1)
        last_dma = {}
        for o in ops:
            if o['dma']:
                last_dma[o['dsem']] = o['dval']
        for k, v in last_dma.items():
            nc.sync.wait_ge(dsem[k], v)


def t5_bucket_np(n):
    n = np.maximum(n, 0)
    nf = np.maximum(n, 1).astype(np.float32)
    large = 16 + (np.log(nf / np.float32(16)) / np.float32(math.log(128 / 16)) * np.float32(16)).astype(np.int32)
    large = np.minimum(large, 31)
    return np.where(n < 16, n, large)


class Builder:
    def __init__(self, C, nsamp=16, do_sample=True, npool=2560):
        self.C = C
        self.P = 3 * C
        self.NK = 4 * C
        self.NT = self.NK // 128
        self.NB = self.NK // 256
        self.GS = min(512, C)
        self.nsamp = nsamp
        self.do_sample = do_sample
        self.npool = npool
        self.nc = bass.Bass("TRN2", target_bir_lowering=False)
        self.ukey = 0

    def din(self, name, shape, dt=F32):
        return self.nc.dram_tensor(name, list(shape), dt, kind="ExternalInput").ap()

    def dout(self, name, shape, dt=F32):
        return self.nc.dram_tensor(name, list(shape), dt, kind="ExternalOutput").ap()

    def dscr(self, name, shape, dt):
        return self.nc.dram_tensor(name, list(shape), dt, kind="Internal").ap()

    def sb(self, name, shape, dt):
        return self.st.enter_context(self.nc.sbuf_tensor(name, list(shape), dt))

    def uk(self, p='t'):
        self.ukey += 1
        return (p, self.ukey)

    def mm(self, out, lhsT, rhs, start, stop, r, w):
        self.S.op('pe', lambda e: e.matmul(out, lhsT=lhsT, rhs=rhs, start=start, stop=stop), r, w)

    def tr(self, out, in_, ident, r, w):
        self.S.op('pe', lambda e: e.transpose(out=out, in_=in_, identity=ident), r, w)

    def act(self, out, in_, func, r, w, scale=None, bias=None, accum=None):
        kw = {}
        if scale is not None:
            kw['scale'] = scale
        if bias is not None:
            kw['bias'] = bias
        if accum is not None:
            kw['accum_out'] = accum
        self.S.op('act', lambda e: e.activation(out=out, in_=in_, func=func, **kw), r, w)

    def tt(self, eng, out, in0, in1, op, r, w):
        self.S.op(eng, lambda e: e.tensor_tensor(out=out, in0=in0, in1=in1, op=op), r, w)

    def ts(self, eng, out, in0, s1, s2, op0, op1, r, w):
        if op1 is None:
            self.S.op(eng, lambda e: e.tensor_scalar(out=out, in0=in0, scalar1=s1, scalar2=None, op0=op0), r, w)
        else:
            self.S.op(eng, lambda e: e.tensor_scalar(out=out, in0=in0, scalar1=s1, scalar2=s2, op0=op0, op1=op1), r, w)

    def stt(self, out, in0, scalar, in1, op0, op1, r, w):
        self.S.op('dve', lambda e: e.scalar_tensor_tensor(out=out, in0=in0, scalar=scalar, in1=in1, op0=op0, op1=op1), r, w)

    def cp(self, eng, out, in_, r, w):
        if eng == 'act':
            self.S.op('act', lambda e: e.activation(out=out, in_=in_, func=AF.Copy), r, w)
        else:
            self.S.op(eng, lambda e: e.tensor_copy(out=out, in_=in_), r, w)

    def red(self, out, in_, op, r, w, axis=AX.X):
        self.S.op('dve', lambda e: e.tensor_reduce(out=out, in_=in_, axis=axis, op=op), r, w)

    def recip(self, out, in_, r, w):
        self.S.op('dve', lambda e: e.reciprocal(out=out, in_=in_), r, w)

    def memset(self, eng, ap, val, w):
        self.S.op(eng, lambda e: e.memset(ap, val), (), w)

    def dma(self, q, out, in_, r, w, slow=False):
        if slow:
            self.S.op(q, lambda e: e.dma_start(out=out, in_=in_, allow_slow_non_contiguous=True), r, w, dma=True)
        else:
            self.S.op(q, lambda e: e.dma_start(out=out, in_=in_), r, w, dma=True)

    def gbank(self):
        i = self.gi % 4
        self.gi += 1
        return self.ps[i], ('ps', i)

    def abank(self):
        i = 4 + self.ai % 4
        self.ai += 1
        return self.ps[i], ('ps', i)

    def sbank(self):
        i = (4, 5, 0, 1)[self.si % 4]
        self.si += 1
        return self.ps[i], ('ps', i)

    def obank(self):
        i = 6 + self.oi % 2
        self.oi += 1
        return self.ps[i], ('ps', i)

    def wpiece(self, loads):
        i = self.wi % self.NSLOT
        self.wi += 1
        slot = self.wsl[i]
        key = ('w', i)
        nws = int(os.environ.get('KWS', '4'))
        if self.wpend.get(i) is not None:
            j = self.wpend.pop(i)
            self.dma('act', self.wscr[j], slot[:, :], [key], [('wscr', j)])
        pid = tuple((src.tensor.name, src.offset, str(src.ap)) for _, src in loads)
        if pid in self.wcache:
            j = self.wcache[pid]
            for i2, j2 in list(self.wpend.items()):
                if j2 == j:
                    self.wpend.pop(i2)
                    self.dma('act', self.wscr[j], self.wsl[i2][:, :], [('w', i2)], [('wscr', j)])
            self.wser += 1
            self.dma('pool', slot[:, :], self.wscr[j], [('wscr', j)], [key, ('wser', self.wser % nws)])
        else:
            j = len(self.wcache)
            self.wcache[pid] = j
            for dstf, src in loads:
                self.wser += 1
                self.dma('pool', dstf(slot), src, (), [key, ('wser', self.wser % nws)])
            self.wpend[i] = j
        return slot, key

    def build(self):
        nc = self.nc
        C, P, NK, NT, NB, GS = self.C, self.P, self.NK, self.NT, self.NB, self.GS
        xk = self.din("xk", [NK, D])
        gmask_d = self.din("gmask", [NB, NB])
        hv_d = self.din("hv", [1, 1])
        oh_d = self.din("oh", [33, 384])
        w_in_e = self.din("w_in_e", [D, 2560])
        w_out_e = self.din("w_out_e", [D, D])
        w_in_o = self.din("w_in_o", [D, 3072])
        w_out_o = self.din("w_out_o", [D, D])
        w_up = self.din("w_up", [2, D, 2 * DFF])
        w_down = self.din("w_down", [2, DFF, D])
        rel_bias = self.din("rel_bias", [32, H])
        norm_mix_e = self.din("norm_mix_e", [D])
        norm_mix_o = self.din("norm_mix_o", [D])
        norm_ffn = self.din("norm_ffn", [2, D])
        conv_a_w = self.din("conv_a_w", [31, DA])
        conv_a_b = self.din("conv_a_b", [DA])
        ln_a_g = self.din("ln_a_g", [DA])
        ln_a_b = self.din("ln_a_b", [DA])
        q_norm_g = self.din("q_norm_g", [HD])
        k_norm_g = self.din("k_norm_g", [HD])
        conv_c_w = self.din("conv_c_w", [3, D])
        conv_f_w = self.din("conv_f_w", [2, 3, 2 * DFF])
        conv_f_b = self.din("conv_f_b", [2, 2 * DFF])
        y_out = self.dout("y_out", [C, D])
        k_out = self.dout("k_out", [C, 512])
        v_out = self.dout("v_out", [C, 512])
        ca_out = self.dout("ca_out", [30, DA])
        cc_out = self.dout("cc_out", [2, D])
        f_out = self.dout("f_out", [2, 2, 2 * DFF])
        if self.do_sample:
            NPOOL = self.npool
            xs_d = self.din("xs", [16, D])
            pt_d = self.din("pt", [1, 256], I32)
            ck_d = self.din("cache_k", [NPOOL * 64, 1024])
            cv_d = self.din("cache_v", [NPOOL * 64, 1024])
            sta_d = self.din("state_a", [16, 30, DA])
            stc_d = self.din("state_c", [16, 2, D])
            stf_d = self.din("state_f", [2, 16, 2, 2 * DFF])
            ohS_d = self.din("ohS", [33, 256])
            oh16_d = self.din("oh16", [30, 256])
            ys_out = self.dout("ys_out", [16, D])
            ks_out = self.dout("ks_out", [16, 512])
            vs_out = self.dout("vs_out", [16, 512])
            cas_out = self.dout("cas_out", [16, 30, DA])
            ccs_out = self.dout("ccs_out", [16, 2, D])
            fs_out = self.dout("fs_out", [2, 16, 2, 2 * DFF])
        qs_scr = self.dscr("qs_scr", [16, 512], F32)
        wscr = self.dscr("wscr", [64, 128, 4096], BF16)
        kt_scr = self.dscr("kt_scr", [H, 96, NK], BF16)
        va_scr = self.dscr("va_scr", [H, 128, NT, 128], BF16)
        fv_scr = self.dscr("fv_scr", [H, 384], F32)
        if os.environ.get("KDBG"):
            k_out = self.dscr("dbg_scr", [C, 512], F32)

        with ExitStack() as st:
            self.st = st
            self.S = Sched(nc)
            S = self.S
            self.gi = 0
            self.ai = 0
            self.si = 0
            self.oi = 0
            self.wi = 0
            self.wser = 0
            self.wcache = {}
            self.wpend = {}
            self.wscr = wscr
            self.NSLOT = 5
            self.ps = [st.enter_context(nc.psum_tensor("ps%d" % i, [128, 512], F32)) for i in range(8)]
            self.wsl = [self.sb("wslot%d" % i, [128, 4096], BF16) for i in range(self.NSLOT)]
            identf = self.sb("identf", [128, 128], F32)
            ident = self.sb("ident", [128, 128], BF16)
            onesf = self.sb("onesf", [128, 128], F32)
            eps_t = self.sb("eps_t", [128, 1], F32)
            self.memset('pool', identf[:], 0.0, ['identf'])
            S.op('pool', lambda e: e.affine_select(out=identf[:], in_=identf[:], pattern=[[-1, 128]], compare_op=ALU.not_equal,
                                                   fill=1.0, base=0, channel_multiplier=1), ['identf'], ['identf'])
            self.cp('dve', ident[:], identf[:], ['identf'], ['ident'])
            self.memset('pool', onesf[:], 1.0, ['onesf'])
            self.memset('pool', eps_t[:], EPS, ['eps_t'])
            self.ident, self.identf, self.onesf, self.eps_t = ident, identf, onesf, eps_t

            def colload(name, src_ap, shape):
                t = self.sb(name, shape, F32)
                self.dma('act', t[:], src_ap, (), [name], slow=True)
                return t
            def colload2(name, shape, parts):
                t = self.sb(name, shape, F32)
                for dst_fn, src in parts:
                    self.dma('act', dst_fn(t), src, (), [name], slow=True)
                return t
            gE = colload("gE", norm_mix_e.rearrange("(c p) -> p c", p=128), [128, 8])
            gO = colload("gO", norm_mix_o.rearrange("(c p) -> p c", p=128), [128, 8])
            gF = colload2("gF", [128, 2, 8], [(lambda t, l=l: t[:, l, :], norm_ffn[l].rearrange("(c p) -> p c", p=128)) for l in range(2)])
            caw = colload2("caw", [128, 4, 31], [(lambda t, c=c: t[:, c, :], conv_a_w[:, c * 128:(c + 1) * 128].rearrange("j p -> p j"))
                                                  for c in range(4)])
            cab = colload("cab", conv_a_b.rearrange("(c p) -> p c", p=128), [128, 4])
            lng = colload("lng", ln_a_g.rearrange("(c p) -> p c", p=128), [128, 4])
            lnb = colload("lnb", ln_a_b.rearrange("(c p) -> p c", p=128), [128, 4])
            ccw = colload2("ccw", [128, 8, 3], [(lambda t, j=j: t[:, :, j], conv_c_w[j].rearrange("(c p) -> p c", p=128)) for j in range(3)])
            cfw = colload2("cfw", [128, 2, 44, 3], [(lambda t, l=l, j=j: t[:, l, :, j], conv_f_w[l, j].rearrange("(c p) -> p c", p=128))
                                                     for l in range(2) for j in range(3)])
            cfb = colload2("cfb", [128, 2, 44], [(lambda t, l=l: t[:, l, :], conv_f_b[l].rearrange("(c p) -> p c", p=128)) for l in range(2)])
            gqB = colload("gqB", q_norm_g.rearrange("(o d) -> o d", o=1).partition_broadcast(128), [128, 1, HD])
            gkB = colload("gkB", k_norm_g.rearrange("(o d) -> o d", o=1).partition_broadcast(128), [128, 1, HD])
            b31B = colload("b31B", rel_bias[31:32, :].partition_broadcast(128), [128, 1, H])
            gmask = self.sb("gmaskt", [128, 1, NB, NB], BF16)
            self.dma('pool', gmask[:], gmask_d.rearrange("(o a) b -> o a b", o=1).partition_broadcast(128), (), ['gmaskt'])
            hv = colload("hvt", hv_d.partition_broadcast(128), [128, 1, 1])
            scB = self.sb("scB", [128, H], F32)
            self.ts('dve', scB[:], b31B[:, 0, :], -NEG, None, ALU.add, None, ['b31B'], ['scB'])

            rb = self.sb("rb", [33, H], F32)
            rb31 = self.sb("rb31", [33, 1, H], F32)
            ohs = self.sb("ohs", [33, 384], F32)
            self.dma('act', rb[0:32, :], rel_bias, (), ['rb'])
            self.dma('act', rb31[:], rel_bias[31:32, :].partition_broadcast(33), (), ['rb31'])
            self.dma('act', ohs[:], oh_d, (), ['ohs'])
            self.tt('dve', rb[0:32, :], rb[0:32, :], rb31[0:32, 0, :], ALU.subtract, ['rb', 'rb31'], ['rb'])
            self.memset('pool', rb[32:33, :], NEG, ['rb'])
            pb, pk = self.gbank()
            self.mm(pb[0:H, 0:384], rb[:], ohs[:], True, True, ['rb', 'ohs'], [pk])
            fv = self.sb("fv", [H, 384], F32)
            self.cp('dve', fv[:], pb[0:H, 0:384], [pk], ['fv'])
            self.dma('sp', fv_scr, fv[:], ['fv'], ['fv_scr'])
            DC = self.sb("DC", [128, H, 2, 128], BF16)

            xres = self.sb("xres", [128, 4, D], F32)
            hT = self.sb("hT", [128, 8, GS], BF16)
            catT = self.sb("catT", [128, 8, GS], BF16)
            big = self.sb("big", [128, 16384], BF16)
            mT = big[:, 0:NPAIR * GS].rearrange("p (c t) -> p c t", c=NPAIR)
            Ast = self.sb("Ast", [128, 4, 30], F32)
            Cst = self.sb("Cst", [128, 8, 2], F32)
            Ust = self.sb("Ust", [128, 2, 44, 2], F32)
            tsum = self.sb("tsum", [64, H, NT], F32)
            kmT = self.sb("kmT", [64, H, NB], F32)
            QTaug = self.sb("QTaug", [96, H, GS], BF16)
            self.memset('pool', Ast[:], 0.0, ['Ast'])
            self.memset('pool', Cst[:], 0.0, ['Cst'])
            self.memset('pool', Ust[:], 0.0, ['Ust'])
            self.memset('pool', tsum[:], 0.0, ['tsum'])
            KTb = [big[0:96, i * 4096:(i + 1) * 4096] for i in range(2)]
            VAb = [big[:, 8192 + i * 4096:8192 + (i + 1) * 4096].rearrange("p (t c) -> p t c", c=128) for i in range(2)]

            def mkeys(c):
                a = ('KTb', 0) if c * GS < 4096 else (('KTb', 1) if c * GS < 8192 else (('VAb', 0) if c * GS < 12288 else ('VAb', 1)))
                b = ('KTb', 0) if (c + 1) * GS - 1 < 4096 else (('KTb', 1) if (c + 1) * GS - 1 < 8192 else (('VAb', 0) if (c + 1) * GS - 1 < 12288 else ('VAb', 1)))
                return list({('mT', c), a, b})
            self.kvi = 0
            def pool(name, n, shape, dt):
                return [self.sb("%s%d" % (name, i), shape, dt) for i in range(n)]
            sqb = pool("sqb", 1, [128, D], BF16)
            hb = pool("hb", 2, [128, D], BF16)
            st1 = pool("st1", 4, [128, 8], F32)
            f512 = pool("f512", 5, [128, 512], F32)
            kaug = pool("kaug", 2, [128, H, 96], BF16)
            nmt = pool("nmt", 2, [128, H, 32], BF16)
            g2p = pool("g2p", 2, [128, H, NB], F32)
            t8p = pool("t8p", 2, [128, H, 8], F32)
            qtf = pool("qtf", 1, [64, H, 128], F32)
            ktsb = pool("ktsb", 2, [96, H, 128], BF16)
            vasb = pool("vasb", 2, [128, H, 128], BF16)
            for i in range(2):
                self.memset('pool', vasb[i][:], 1.0, [('vasb', i)])
            self.cvA = pool("cvA", 4, [128, GS], F32)
            self.ubuf = pool("ubuf", 3, [128, 2 + GS], F32)
            abuf = pool("abuf", 2, [128, 30 + GS], F32)
            ptb = pool("ptb", 4, [128, 512], BF16)
            for i in range(2):
                self.memset('pool', nmt[i][:], 0.0, [('nmt', i)])
            self.rr = {}

            def nxt(name, lst):
                i = self.rr.get(name, 0)
                self.rr[name] = i + 1
                return lst[i % len(lst)], (name, i % len(lst))

            Jf = self.sb("Jf", [128, 128], F32)
            self.memset('pool', Jf[:], 0.0, ['Jf'])
            S.op('pool', lambda e: e.affine_select(out=Jf[:], in_=Jf[:], pattern=[[1, 128]], compare_op=ALU.not_equal,
                                                   fill=1.0, base=-127, channel_multiplier=1), ['Jf'], ['Jf'])
            for h in range(H):
                dct, dctk = nxt('f512', f512)
                for k, off in ((0, 0), (1, 128)):
                    src = bass.AP(tensor=fv_scr.tensor, offset=h * 384 + off, ap=[[1, 128], [1, 128]])
                    self.dma('sp', dct[:, k * 128:(k + 1) * 128], src, ['fv_scr'], [dctk], slow=True)
                pbj, pkj = self.gbank()
                self.mm(pbj[:, 0:256], Jf[:], dct[:, 0:256], True, True, ['Jf', dctk], [pkj])
                self.cp('dve', DC[:, h, :, :], pbj[:, 0:256].rearrange("p (k t) -> p k t", k=2), [pkj], ['DC'])
            def norm_T(xt, xkey, gcol, gkey, ntile_idx):
                sq, sqk = nxt('sqb', sqb)
                s1, s1k = nxt('st1', st1)
                self.act(sq[:], xt, AF.Square, [xkey], [sqk, s1k], accum=s1[:, 0:1])
                self.act(s1[:, 1:2], s1[:, 0:1], AF.Sqrt, [s1k], [s1k], scale=1.0 / D, bias=eps_t[:, 0:1])
                self.recip(s1[:, 2:3], s1[:, 1:2], [s1k], [s1k])
                hh, hk = nxt('hb', hb)
                self.ts('dve', hh[:], xt, s1[:, 2:3], None, ALU.mult, None, [xkey, s1k], [hk])
                pb, pk = self.gbank()
                pbf = pb[:].bitcast(BF16)
                for kc in range(8):
                    self.tr(pbf[:, kc * 128:(kc + 1) * 128], hh[:, kc * 128:(kc + 1) * 128], ident[:], [hk, 'ident'], [pk])
                c0 = ntile_idx * 128
                self.tt('dve', hT[:, :, c0:c0 + 128], pbf.rearrange("p (c t) -> p c t", c=8),
                        gcol.unsqueeze(2).to_broadcast([128, 8, 128]), ALU.mult, [pk, gkey], [('hT', ntile_idx)])

            def head_norm(pb, pk, gB, gBkey, extra_scale):
                sq, sqk = nxt('f512', f512)
                s1, s1k = nxt('st1', st1)
                self.act(sq[:], pb[:], AF.Square, [pk], [sqk])
                self.red(s1[:, 0:8], sq[:].rearrange("p (h d) -> p h d", h=H), ALU.add, [sqk], [s1k])
                s2, s2k = nxt('st1', st1)
                self.act(s2[:], s1[:], AF.Sqrt, [s1k], [s2k], scale=1.0 / HD, bias=eps_t[:, 0:1])
                self.recip(s1[:], s2[:], [s2k], [s1k])
                if extra_scale != 1.0:
                    self.ts('dve', s1[:], s1[:], extra_scale, None, ALU.mult, None, [s1k], [s1k])
                o, ok = nxt('f512', f512)
                o3 = o[:].rearrange("p (h d) -> p h d", h=H)
                self.tt('dve', o3, pb[:].rearrange("p (h d) -> p h d", h=H), s1[:].unsqueeze(2).to_broadcast([128, H, HD]),
                        ALU.mult, [pk, s1k], [ok])
                self.tt('dve', o3, o3, gB[:].to_broadcast([128, H, HD]), ALU.mult, [ok, gBkey], [ok])
                return o, ok

            def tok_mm(slot, skey, ntile_idx):
                pb, pk = self.gbank()
                sv = slot[:].rearrange("p (c n) -> p c n", c=8)
                c0 = ntile_idx * 128
                for kc in range(8):
                    self.mm(pb[:], hT[:, kc, c0:c0 + 128], sv[:, kc, :], kc == 0, kc == 7, [('hT', ntile_idx), skey], [pk])
                return pb, pk

            def w_cols(w2d, c0, n):
                return (lambda s: s[:, 0:8 * n].rearrange("p (c n) -> p c n", c=8),
                        w2d.rearrange("(c p) n -> p c n", p=128)[:, :, c0:c0 + n])

            def kv_a(gt, ntile_idx, kslot, kskey, vslot, vskey, own_out_row):
                pb, pk = tok_mm(kslot, kskey, ntile_idx)
                Kf, Kfk = head_norm(pb, pk, gkB, 'gkB', 1.0)
                if own_out_row is not None:
                    self.dma(os.environ.get('KOQ', 'sp'), k_out[own_out_row:own_out_row + 128, :], Kf[:], [Kfk], [])
                ka, kak = nxt('kaug', kaug)
                self.memset('pool', ka[:, :, 64:96], 0.0, [kak])
                self.memset('pool', ka[:, :, 64 + gt // 2:65 + gt // 2], 1.0, [kak])
                self.cp('pool', ka[:, :, 0:64], Kf[:].rearrange("p (h d) -> p h d", h=H), [Kfk], [kak])
                pbv, pkv = tok_mm(vslot, vskey, ntile_idx)
                vas, vask = nxt('vasb', vasb)
                if own_out_row is not None:
                    Vf, Vfk = nxt('f512', f512)
                    self.cp('act', Vf[:], pbv[:], [pkv], [Vfk])
                    self.dma(os.environ.get('KOQ', 'sp'), v_out[own_out_row:own_out_row + 128, :], Vf[:], [Vfk], [])
                    self.cp('dve', vas[:, :, 0:64], Vf[:].rearrange("p (h d) -> p h d", h=H), [Vfk], [vask])
                else:
                    self.cp('dve', vas[:, :, 0:64], pbv[:].rearrange("p (h d) -> p h d", h=H), [pkv], [vask])
                self.dma('sp', va_scr[:, :, gt, :].rearrange("h p c -> p h c"), vas[:], [vask], ['va_scr'])
                return (gt, Kf, Kfk, ka, kak)

            def kv_b(st_):
                gt, Kf, Kfk, ka, kak = st_
                pb2, pk2 = self.gbank()
                for h in range(H):
                    self.mm(pb2[0:64, h:h + 1], Kf[:, h * 64:(h + 1) * 64], onesf[:, 0:1], True, True, [Kfk, 'onesf'], [pk2])
                self.cp('dve', tsum[:, :, gt], pb2[0:64, 0:H], [pk2], ['tsum'])
                pb3, pk3 = self.gbank()
                p3 = pb3[:].bitcast(BF16)
                for h in range(H):
                    self.tr(p3[0:96, h * 128:(h + 1) * 128], ka[:, h, :], ident[:], [kak, 'ident'], [pk3])
                kts, ktsk = nxt('ktsb', ktsb)
                self.cp('act', kts[:], p3[0:96, :].rearrange("p (h t) -> p h t", h=H), [pk3], [ktsk])
                self.dma('sp', kt_scr[:, :, gt * 128:(gt + 1) * 128].rearrange("h r k -> r h k"), kts[:], [ktsk], ['kt_scr'])

            def kv_tiles(gt0, n, kslot, kskey, vslot, vskey, orow0):
                prev = None
                for ti in range(n):
                    cur = kv_a(gt0 + ti, ti, kslot, kskey, vslot, vskey, None if orow0 is None else orow0 + ti * 128)
                    if prev is not None:
                        kv_b(prev)
                    prev = cur
                kv_b(prev)

            def load_x(row0, ntile):
                for ti in range(ntile):
                    self.dma('sp', xres[:, ti, :], xk[row0 + ti * 128:row0 + (ti + 1) * 128, :], (), [('xres', ti)])

            npre = P // 128 - 1
            kslot, kskey = self.wpiece([w_cols(w_in_e, 1536, 512)])
            vslot, vskey = self.wpiece([w_cols(w_in_e, 2048, 512)])
            gt = 0
            while gt < npre:
                nt_ = min(4, npre - gt)
                load_x(gt * 128, nt_)
                for ti in range(nt_):
                    norm_T(xres[:, ti, :], ('xres', ti), gE[:], 'gE', ti)
                kv_tiles(gt, nt_, kslot, kskey, vslot, vskey, None)
                gt += nt_

            groups = [(P // 128 - 1, 1, None)]
            for g in range(C // GS):
                groups.append((P // 128 + g * (GS // 128), GS // 128, g * GS))

            def attention(gt0, ntile):
                ntok = ntile * 128
                nkt = gt0 + ntile
                for h in range(H):
                    ob, okk = self.obank()
                    nhalf = (nkt + 31) // 32
                    first = True
                    pend = []
                    for hf in range(nhalf):
                        k0 = hf * 32
                        k1 = min(nkt, k0 + 32)
                        i = self.kvi % 2
                        self.kvi += 1
                        ktb, vab = KTb[i], VAb[i]
                        self.dma('sp', ktb[:, 0:(k1 - k0) * 128], kt_scr[h, :, k0 * 128:k1 * 128], ['kt_scr'], [('KTb', i)])
                        self.dma('sp', vab[:, 0:k1 - k0, :], va_scr[h, :, k0:k1, :], ['va_scr'], [('VAb', i)])
                        for kt in range(k0, k1):
                            qlo = max(kt, gt0) - gt0
                            c0 = qlo * 128
                            sb_, sk = self.sbank()
                            has_d0 = kt >= gt0
                            has_c1 = (kt + 1 >= gt0) and (kt + 1 < gt0 + ntile)
                            self.mm(sb_[:, c0:ntok], ktb[:, (kt - k0) * 128:(kt - k0 + 1) * 128], QTaug[:, h, c0:ntok], True,
                                    not (has_d0 or has_c1), [('KTb', i), ('QTaug', h)], [sk])
                            if has_d0:
                                cc = (kt - gt0) * 128
                                self.mm(sb_[:, cc:cc + 128], ident[:], DC[:, h, 0, :], False, not has_c1, ['ident', 'DC'], [sk])
                            if has_c1:
                                cc = (kt + 1 - gt0) * 128
                                self.mm(sb_[:, cc:cc + 128], ident[:], DC[:, h, 1, :], False, True, ['ident', 'DC'], [sk])
                            pt, ptk = nxt('ptb', ptb)
                            self.act(pt[:, c0:ntok], sb_[:, c0:ntok], AF.Exp, [sk], [ptk])
                            pend.append((ob[:, c0:ntok], vab[:, kt - k0, :], pt[:, c0:ntok], first, kt == nkt - 1, [('VAb', i), ptk], [okk]))
                            if len(pend) > 2:
                                self.mm(*pend.pop(0))
                            first = False
                    while pend:
                        self.mm(*pend.pop(0))
                    rec, rk = nxt('f512', f512)
                    self.recip(rec[0:64, 0:ntok], ob[64:128, 0:ntok], [okk], [rk])
                    po = (h % 2) * 64
                    self.tt('dve', catT[po:po + 64, 4 + h // 2, 0:ntok], ob[0:64, 0:ntok], rec[0:64, 0:ntok], ALU.mult,
                            [okk, rk], [('catT', 4 + h // 2, h % 2)])

            def out_proj(wsrc, ntile, catkeys):
                for half in range(2):
                    slot, skey = self.wpiece([w_cols(wsrc, half * 512, 512)])
                    sv = slot[:].rearrange("p (c n) -> p c n", c=8)
                    banks = [self.abank() for _ in range(ntile)]
                    for kc in range(8):
                        for ti in range(ntile):
                            self.mm(banks[ti][0][:], catT[:, kc, ti * 128:(ti + 1) * 128], sv[:, kc, :], kc == 0, kc == 7,
                                    catkeys(kc) + [skey], [banks[ti][1]])
                    for ti in range(ntile):
                        self.tt('dve', xres[:, ti, half * 512:(half + 1) * 512], banks[ti][0][:], xres[:, ti, half * 512:(half + 1) * 512],
                                ALU.add, [banks[ti][1], ('xres', ti)], [('xres', ti)])

            def ffn(l, ntile, first_own):
                ntok = ntile * 128
                for ti in range(ntile):
                    norm_T(xres[:, ti, :], ('xres', ti), gF[:, l, :], 'gF', ti)
                hkeys = [('hT', ti) for ti in range(ntile)]
                if first_own:
                    self.ts('dve', Ust[:, l, :, :], Ust[:, l, :, :], hv[:, 0, 0:1], None, ALU.mult, None, ['Ust', 'hvt'], ['Ust'])
                for c4 in range((NPAIR + 3) // 4):
                    ncol = min(4, NPAIR - c4 * 4) * 128
                    uslots = [self.wpiece([w_cols(w_up[l], part * DFF + c4 * 512, ncol)]) for part in range(2)]
                    for cl in range(ncol // 128):
                        c = c4 * 4 + cl
                        res = []
                        for part in range(2):
                            ci = c + part * NPAIR
                            pb, pk = self.gbank()
                            for kc in range(8):
                                self.mm(pb[:, 0:ntok], uslots[part][0][:, 0:8 * ncol].rearrange("p (c n) -> p c n", c=8)[:, kc, cl * 128:(cl + 1) * 128],
                                        hT[:, kc, 0:ntok], kc == 0, kc == 7, hkeys + [uslots[part][1]], [pk])
                            U, Uk = nxt('ubuf', self.ubuf)
                            self.cp('pool', U[:, 0:2], Ust[:, l, ci, :], ['Ust'], [Uk])
                            self.cp('act', U[:, 2:2 + ntok], pb[:, 0:ntok], [pk], [Uk])
                            self.cp('pool', Ust[:, l, ci, :], U[:, ntok:ntok + 2], [Uk], ['Ust'])
                            cv, cvk = nxt('f512', f512)
                            self.act(cv[:, 0:ntok], U[:, 2:2 + ntok], AF.Identity, [Uk, 'cfw', 'cfb'], [cvk],
                                     scale=cfw[:, l, ci, 2:3], bias=cfb[:, l, ci:ci + 1])
                            self.stt(cv[:, 0:ntok], U[:, 1:1 + ntok], cfw[:, l, ci, 1:2], cv[:, 0:ntok], ALU.mult, ALU.add, [Uk, cvk, 'cfw'], [cvk])
                            self.stt(cv[:, 0:ntok], U[:, 0:ntok], cfw[:, l, ci, 0:1], cv[:, 0:ntok], ALU.mult, ALU.add, [Uk, cvk, 'cfw'], [cvk])
                            res.append((cv, cvk))
                        (cg, cgk), (cu, cuk) = res
                        self.act(cg[:, 0:ntok], cg[:, 0:ntok], AF.Silu, [cgk], [cgk])
                        self.tt('dve', mT[:, c, 0:ntok], cg[:, 0:ntok], cu[:, 0:ntok], ALU.mult, [cgk, cuk], mkeys(c))
                ffn_down(l, ntile)

            def ffn_down(l, ntile):
                for half in range(2):
                    banks = [self.abank() for _ in range(ntile)]
                    for pi, (ca, cb) in enumerate(((0, 8), (8, 16), (16, 22))):
                        slot, skey = self.wpiece([(lambda s, n=cb - ca: s[:, 0:n * 512].rearrange("p (c n) -> p c n", c=n),
                                                   w_down[l, ca * 128:cb * 128, half * 512:(half + 1) * 512].rearrange("(c p) n -> p c n", p=128))])
                        sv = slot[:, 0:(cb - ca) * 512].rearrange("p (c n) -> p c n", c=cb - ca)
                        for c in range(ca, cb):
                            for ti in range(ntile):
                                self.mm(banks[ti][0][:], mT[:, c, ti * 128:(ti + 1) * 128], sv[:, c - ca, :], c == 0, c == NPAIR - 1,
                                        mkeys(c) + [skey], [banks[ti][1]])
                    for ti in range(ntile):
                        self.tt('dve', xres[:, ti, half * 512:(half + 1) * 512], banks[ti][0][:], xres[:, ti, half * 512:(half + 1) * 512],
                                ALU.add, [banks[ti][1], ('xres', ti)], [('xres', ti)])

            def ln_silu(cvs, ntok):
                pbm, pkm = self.gbank()
                pbs, pks = self.gbank()
                sqs = []
                for cc in range(4):
                    sq, sqk = nxt('f512', f512)
                    self.act(sq[:, 0:ntok], cvs[cc][0][:, 0:ntok], AF.Square, [cvs[cc][1]], [sqk])
                    sqs.append((sq, sqk))
                for cc in range(4):
                    self.mm(pbm[:, 0:ntok], onesf[:], cvs[cc][0][:, 0:ntok], cc == 0, cc == 3, ['onesf', cvs[cc][1]], [pkm])
                for cc in range(4):
                    self.mm(pbs[:, 0:ntok], onesf[:], sqs[cc][0][:, 0:ntok], cc == 0, cc == 3, ['onesf', sqs[cc][1]], [pks])
                mean, meank = nxt('f512', f512)
                self.ts('dve', mean[:, 0:ntok], pbm[:, 0:ntok], 1.0 / DA, None, ALU.mult, None, [pkm], [meank])
                var, vark = sqs[0]
                self.tt('dve', var[:, 0:ntok], mean[:, 0:ntok], mean[:, 0:ntok], ALU.mult, [meank], [vark])
                self.stt(var[:, 0:ntok], pbs[:, 0:ntok], 1.0 / DA, var[:, 0:ntok], ALU.mult, ALU.subtract, [pks, vark], [vark])
                self.act(var[:, 0:ntok], var[:, 0:ntok], AF.Sqrt, [vark], [vark], bias=eps_t[:, 0:1], scale=1.0)
                self.recip(var[:, 0:ntok], var[:, 0:ntok], [vark], [vark])
                for cc in range(4):
                    cv, cvk = cvs[cc]
                    self.tt('dve', cv[:, 0:ntok], cv[:, 0:ntok], mean[:, 0:ntok], ALU.subtract, [cvk, meank], [cvk])
                    self.tt('dve', cv[:, 0:ntok], cv[:, 0:ntok], var[:, 0:ntok], ALU.mult, [cvk, vark], [cvk])
                    self.act(catT[:, cc, 0:ntok], cv[:, 0:ntok], AF.Silu, [cvk, 'lng', 'lnb'], [('catT', cc, 0), ('catT', cc, 1)],
                             scale=lng[:, cc:cc + 1], bias=lnb[:, cc:cc + 1])

            self.marks = []
            mark = lambda n: self.marks.append((n, len(S.ops)))
            for (gt0, ntile, orow) in groups:
                ntok = ntile * 128
                mark('group %d start' % gt0)
                is_halo = orow is None
                first_own = (orow == 0)
                load_x(gt0 * 128, ntile)
                for ti in range(ntile):
                    norm_T(xres[:, ti, :], ('xres', ti), gE[:], 'gE', ti)
                hkeys = [('hT', ti) for ti in range(ntile)]
                kslot, kskey = self.wpiece([w_cols(w_in_e, 1536, 512)])
                vslot, vskey = self.wpiece([w_cols(w_in_e, 2048, 512)])
                kv_tiles(gt0, ntile, kslot, kskey, vslot, vskey, None if is_halo else orow)
                mark('kv done')
                self.tt('dve', kmT[:], tsum[:].rearrange("p h (n two) -> p h n two", two=2)[:, :, :, 0],
                        tsum[:].rearrange("p h (n two) -> p h n two", two=2)[:, :, :, 1], ALU.add, ['tsum'], ['kmT'])
                qslot, qskey = self.wpiece([w_cols(w_in_e, 1024, 512)])
                for ti in range(ntile):
                    own = (gt0 + ti) // 2
                    pb, pk = tok_mm(qslot, qskey, ti)
                    Qf, Qfk = head_norm(pb, pk, gqB, 'gqB', HD ** -0.5)
                    qt_, qtk = nxt('qtf', qtf)
                    for hh2 in range(2):
                        pbq, pkq = self.gbank()
                        for h4 in range(4):
                            h = hh2 * 4 + h4
                            self.tr(pbq[0:64, h4 * 128:(h4 + 1) * 128], Qf[:, h * 64:(h + 1) * 64], identf[:], [Qfk, 'identf'], [pkq])
                        self.cp('act', qt_[:, hh2 * 4:(hh2 + 1) * 4, :], pbq[0:64, :].rearrange("p (h t) -> p h t", h=4), [pkq], [qtk])
                    self.cp('pool', QTaug[0:64, :, ti * 128:(ti + 1) * 128], qt_[:], [qtk], [('QTaug', h) for h in range(H)])
                    pbg, pkg = self.gbank()
                    for h in range(H):
                        self.mm(pbg[:, h * NB:(h + 1) * NB], qt_[:, h, :], kmT[:, h, :], True, True, [qtk, 'kmT'], [pkg])
                    g2, g2k = nxt('g2p', g2p)
                    self.tt('dve', g2[:], pbg[:, 0:H * NB].rearrange("p (h n) -> p h n", h=H),
                            gmask[:, 0, own:own + 1, :].to_broadcast([128, H, NB]), ALU.add, [pkg, 'gmaskt'], [g2k])
                    t8, t8k = nxt('t8p', t8p)
                    for h in range(H):
                        S.op('dve', (lambda e, o=t8[:, h, :], i=g2[:, h, :]: e.max(out=o, in_=i)), [g2k], [t8k])
                    c1, c1k = nxt('g2p', g2p)
                    self.tt('dve', c1[:], g2[:], t8[:, :, 2:3].to_broadcast([128, H, NB]), ALU.is_ge, [g2k, t8k], [c1k])
                    self.ts('dve', g2[:], g2[:], NEG, None, ALU.is_gt, None, [g2k], [g2k])
                    self.tt('dve', c1[:], c1[:], g2[:], ALU.mult, [c1k, g2k], [c1k])
                    self.tt('dve', c1[:], c1[:], scB[:].unsqueeze(2).to_broadcast([128, H, NB]), ALU.mult, [c1k, 'scB'], [c1k])
                    nm, nmk = nxt('nmt', nmt)
                    self.ts('dve', nm[:, :, 0:NB], c1[:], NEG, None, ALU.add, None, [c1k], [nmk])
                    self.cp('dve', nm[:, :, own], b31B[:, 0, :], ['b31B', nmk], [nmk])
                    pbn, pkn = self.gbank()
                    pn = pbn[:].bitcast(BF16)
                    for h in range(H):
                        self.tr(pn[0:32, h * 128:(h + 1) * 128], nm[:, h, :], ident[:], [nmk, 'ident'], [pkn])
                    self.cp('act', QTaug[64:96, :, ti * 128:(ti + 1) * 128], pn[0:32, :].rearrange("p (h t) -> p h t", h=H),
                            [pkn], [('QTaug', h) for h in range(H)])
                mark('q done')
                if first_own:
                    self.ts('dve', Ast[:], Ast[:], hv[:, 0, 0:1], None, ALU.mult, None, ['Ast', 'hvt'], ['Ast'])
                valslot, valk = self.wpiece([w_cols(w_in_e, 0, 512)])
                gateslot, gatek = self.wpiece([w_cols(w_in_e, 512, 512)])
                vv = valslot[:].rearrange("p (c n) -> p c n", c=8)
                gv = gateslot[:].rearrange("p (c n) -> p c n", c=8)
                cvs = []
                for cc in range(4):
                    pbv, pkv = self.gbank()
                    pbg, pkg = self.gbank()
                    for kc in range(8):
                        self.mm(pbv[:, 0:ntok], vv[:, kc, cc * 128:(cc + 1) * 128], hT[:, kc, 0:ntok], kc == 0, kc == 7, hkeys + [valk], [pkv])
                    for kc in range(8):
                        self.mm(pbg[:, 0:ntok], gv[:, kc, cc * 128:(cc + 1) * 128], hT[:, kc, 0:ntok], kc == 0, kc == 7, hkeys + [gatek], [pkg])
                    sg, sgk = nxt('f512', f512)
                    self.act(sg[:, 0:ntok], pbg[:, 0:ntok], AF.Sigmoid, [pkg], [sgk])
                    ab, abk = nxt('abuf', abuf)
                    self.cp('pool', ab[:, 0:30], Ast[:, cc, :], ['Ast'], [abk])
                    self.tt('dve', ab[:, 30:30 + ntok], pbv[:, 0:ntok], sg[:, 0:ntok], ALU.mult, [pkv, sgk], [abk])
                    self.cp('pool', Ast[:, cc, :], ab[:, ntok:ntok + 30], [abk], ['Ast'])
                    cv, cvk = nxt('cvA', self.cvA)
                    self.act(cv[:, 0:ntok], ab[:, 30:30 + ntok], AF.Identity, [abk, 'caw', 'cab'], [cvk],
                             scale=caw[:, cc, 30:31], bias=cab[:, cc:cc + 1])
                    for j in range(30):
                        self.stt(cv[:, 0:ntok], ab[:, j:j + ntok], caw[:, cc, j:j + 1], cv[:, 0:ntok], ALU.mult, ALU.add,
                                 [abk, cvk, 'caw'], [cvk])
                    cvs.append((cv, cvk))
                ln_silu(cvs, ntok)
                mark('mixerA done')
                attention(gt0, ntile)
                mark('attn done')
                out_proj(w_out_e, ntile, lambda kc: [('catT', kc, 0), ('catT', kc, 1)])
                mark('outproj done')
                ffn(0, ntile, first_own)
                mark('ffn0 done')
                for ti in range(ntile):
                    norm_T(xres[:, ti, :], ('xres', ti), gO[:], 'gO', ti)
                if first_own:
                    self.ts('dve', Cst[:], Cst[:], hv[:, 0, 0:1], None, ALU.mult, None, ['Cst', 'hvt'], ['Cst'])
                for c in range(8):
                    if c % 4 == 0:
                        cslots = [self.wpiece([w_cols(w_in_o, k * 1024 + (c // 4) * 512, 512)]) for k in range(3)]
                    pbs_ = []
                    for k in range(3):
                        pb, pk = self.gbank()
                        sv = cslots[k][0][:].rearrange("p (c n) -> p c n", c=8)
                        for kc in range(8):
                            self.mm(pb[:, 0:ntok], sv[:, kc, (c % 4) * 128:(c % 4 + 1) * 128], hT[:, kc, 0:ntok], kc == 0, kc == 7,
                                    hkeys + [cslots[k][1]], [pk])
                        pbs_.append((pb, pk))
                    uu, uuk = nxt('f512', f512)
                    self.cp('act', uu[:, 0:ntok], pbs_[2][0][:, 0:ntok], [pbs_[2][1]], [uuk])
                    cb_, cbk = nxt('ubuf', self.ubuf)
                    self.cp('pool', cb_[:, 0:2], Cst[:, c, :], ['Cst'], [cbk])
                    self.tt('dve', cb_[:, 2:2 + ntok], pbs_[1][0][:, 0:ntok], uu[:, 0:ntok], ALU.mult, [pbs_[1][1], uuk], [cbk])
                    self.cp('pool', Cst[:, c, :], cb_[:, ntok:ntok + 2], [cbk], ['Cst'])
                    cv, cvk = nxt('f512', f512)
                    self.act(cv[:, 0:ntok], cb_[:, 2:2 + ntok], AF.Identity, [cbk, 'ccw'], [cvk], scale=ccw[:, c, 2:3])
                    self.stt(cv[:, 0:ntok], cb_[:, 1:1 + ntok], ccw[:, c, 1:2], cv[:, 0:ntok], ALU.mult, ALU.add, [cbk, cvk, 'ccw'], [cvk])
                    self.stt(cv[:, 0:ntok], cb_[:, 0:ntok], ccw[:, c, 0:1], cv[:, 0:ntok], ALU.mult, ALU.add, [cbk, cvk, 'ccw'], [cvk])
                    self.tt('dve', catT[:, c, 0:ntok], pbs_[0][0][:, 0:ntok], cv[:, 0:ntok], ALU.mult, [pbs_[0][1], cvk],
                            [('catT', c, 0), ('catT', c, 1)])
                out_proj(w_out_o, ntile, lambda kc: [('catT', kc, 0), ('catT', kc, 1)])
                ffn(1, ntile, first_own)
                if not is_halo:
                    for ti in range(ntile):
                        self.dma('sp', y_out[orow + ti * 128:orow + (ti + 1) * 128, :], xres[:, ti, :], [('xres', ti)], [])


            def wslot_take():
                i = self.wi % self.NSLOT
                self.wi += 1
                return self.wsl[i], ('w', i)

            def sample_phase():
                def cload(name, shape, src, dt=F32, q='act'):
                    t = self.sb(name + "_t", shape, dt)
                    self.dma(q, t[:], src, (), [name])
                    return t
                oh16 = cload("oh16", [30, 256], oh16_d)
                W30 = cload("W30", [30, DA], conv_a_w[0:30, :])
                ohS = cload("ohS", [33, 256], ohS_d)
                ptb = self.sb("ptb_t", [128, 1, 128], I32)
                pt3 = pt_d.rearrange("o (sn two) -> o sn two", two=2)
                self.dma('act', ptb[0:64], pt3[:, :, 0].partition_broadcast(64), (), ['ptb'], slow=True)
                self.dma('act', ptb[64:128], pt3[:, :, 1].partition_broadcast(64), (), ['ptb'], slow=True)
                rb0B = cload("rb0B", [128, 1, H], rel_bias[0:1, :].partition_broadcast(128))
                io = self.sb("io", [128, 1], I32)
                iof = self.sb("iof", [128, 1], F32)
                idx = self.sb("idx", [128, 128], I32)
                S.op('pool', lambda e: e.iota(io[0:64, :], pattern=[[0, 1]], base=0, channel_multiplier=1), (), ['io'])
                S.op('pool', lambda e: e.iota(io[64:128, :], pattern=[[0, 1]], base=0, channel_multiplier=1), (), ['io'])
                self.cp('pool', iof[:], io[:], ['io'], ['iof'])
                self.ts('dve', idx[:], ptb[:, 0, :], 64.0, iof[:, 0:1], ALU.mult, ALU.add, ['ptb', 'iof'], ['idx'])
                self.tt('dve', rb0B[:, 0, :], rb0B[:, 0, :], b31B[:, 0, :], ALU.subtract, ['rb0B', 'b31B'], ['rb0B'])
                biasP = self.sb("biasP", [128, 2 * H], F32)
                pb, pk = self.gbank()
                for slot in range(2):
                    self.mm(pb[:, slot * H:(slot + 1) * H], ohS[:, slot * 128:(slot + 1) * 128], rb[:], True, True, ['ohS', 'rb'], [pk])
                self.cp('dve', biasP[:], pb[:, 0:2 * H], [pk], ['biasP'])
                VS = self.sb("VS", [128, 512], F32)
                AsT = self.sb("AsT", [128, 4, 16], F32)
                small = self.sb("smalls", [16, 64], F32)
                self.dma('sp', xres[0:16, 0, :], xs_d, (), [('xres', 0)])
                norm_T(xres[:, 0, :], ('xres', 0), gE[:], 'gE', 0)
                hk0 = [('hT', 0)]
                kslot, kskey = self.wpiece([w_cols(w_in_e, 1536, 512)])
                pb, pk = tok_mm(kslot, kskey, 0)
                Kf, Kfk = head_norm(pb, pk, gkB, 'gkB', 1.0)
                self.dma('sp', ks_out, Kf[0:16, :], [Kfk], [])
                vslot, vskey = self.wpiece([w_cols(w_in_e, 2048, 512)])
                pbv, pkv = tok_mm(vslot, vskey, 0)
                self.cp('act', VS[:], pbv[:], [pkv], ['VS'])
                self.dma('sp', vs_out, VS[0:16, :], ['VS'], [])
                qslot, qskey = self.wpiece([w_cols(w_in_e, 1024, 512)])
                pb, pk = tok_mm(qslot, qskey, 0)
                Qf, Qfk = head_norm(pb, pk, gqB, 'gqB', HD ** -0.5)
                self.dma('sp', qs_scr, Qf[0:16, :], [Qfk], ['qs_scr'])
                lself = small[:, 0:8]
                tmpq, tmpqk = nxt('f512', f512)
                self.tt('dve', tmpq[0:16, :], Qf[0:16, :], Kf[0:16, :], ALU.mult, [Qfk, Kfk], [tmpqk])
                self.red(lself, tmpq[0:16, :].rearrange("p (h d) -> p h d", h=H), ALU.add, [tmpqk], ['small'])
                valslot, valk = self.wpiece([w_cols(w_in_e, 0, 512)])
                gateslot, gatek = self.wpiece([w_cols(w_in_e, 512, 512)])
                vv = valslot[:].rearrange("p (c n) -> p c n", c=8)
                gv = gateslot[:].rearrange("p (c n) -> p c n", c=8)
                for cc in range(4):
                    pbv, pkv = self.gbank()
                    pbg, pkg = self.gbank()
                    for kc in range(8):
                        self.mm(pbv[:, 0:16], vv[:, kc, cc * 128:(cc + 1) * 128], hT[:, kc, 0:16], kc == 0, kc == 7, hk0 + [valk], [pkv])
                    for kc in range(8):
                        self.mm(pbg[:, 0:16], gv[:, kc, cc * 128:(cc + 1) * 128], hT[:, kc, 0:16], kc == 0, kc == 7, hk0 + [gatek], [pkg])
                    sg, sgk = nxt('f512', f512)
                    self.act(sg[:, 0:16], pbg[:, 0:16], AF.Sigmoid, [pkg], [sgk])
                    self.tt('dve', AsT[:, cc, :], pbv[:, 0:16], sg[:, 0:16], ALU.mult, [pkv, sgk], ['AsT'])
                accb, acck = self.gbank()
                for s_ in range(16):
                    stA, stAk = nxt('f512', f512)
                    self.dma('sp', stA[0:30, :], sta_d[s_], (), [stAk])
                    self.tt('dve', stA[0:30, :], stA[0:30, :], W30[:], ALU.mult, [stAk, 'W30'], [stAk])
                    self.mm(accb[0:16, :], oh16[:, s_ * 16:(s_ + 1) * 16], stA[0:30, :], s_ == 0, s_ == 15, ['oh16', stAk], [acck])
                cst, cstk = nxt('f512', f512)
                self.cp('dve', cst[0:16, :], accb[0:16, :], [acck], [cstk])
                pbt, pkt = self.gbank()
                for cc in range(4):
                    self.tr(pbt[:, cc * 16:(cc + 1) * 16], cst[0:16, cc * 128:(cc + 1) * 128], identf[0:16, 0:16], [cstk, 'identf'], [pkt])
                cvs = []
                for cc in range(4):
                    cv, cvk = nxt('cvA', self.cvA)
                    self.act(cv[:, 0:16], AsT[:, cc, :], AF.Identity, ['AsT', 'caw', 'cab'], [cvk], scale=caw[:, cc, 30:31], bias=cab[:, cc:cc + 1])
                    self.tt('dve', cv[:, 0:16], cv[:, 0:16], pbt[:, cc * 16:(cc + 1) * 16], ALU.add, [cvk, pkt], [cvk])
                    cvs.append((cv, cvk))
                ln_silu(cvs, 16)
                self.dma('act', cas_out[:, 0:29, :], sta_d[:, 1:30, :], (), [])
                pba, pka = self.gbank()
                for cc in range(4):
                    self.tr(pba[0:16, cc * 128:(cc + 1) * 128], AsT[:, cc, :], identf[:], ['AsT', 'identf'], [pka])
                atok, atokk = nxt('f512', f512)
                self.cp('dve', atok[0:16, :], pba[0:16, :], [pka], [atokk])
                self.dma('sp', cas_out[:, 29, :], atok[0:16, :], [atokk], [])
                lself = small[:, 0:8]
                denAll = small[:, 8:16]
                dtot = small[:, 16:24]
                self.tt('dve', lself, lself, rb0B[0:16, 0, :], ALU.add, ['small', 'rb0B'], ['small'])
                self.act(lself, lself, AF.Exp, ['small'], ['small'])
                self.memset('pool', denAll, 0.0, ['small'])
                bufA = big[:, 0:8192].bitcast(F32).rearrange("p (g n) -> p g n", g=8)
                bufB = big[:, 8192:16384].bitcast(F32).rearrange("p (g n) -> p g n", g=8)
                bufs = [(bufA, [('KTb', 0), ('KTb', 1)]), (bufB, [('VAb', 0), ('VAb', 1)])]
                self.bi = 0

                def gather(src, s_, half):
                    bf_, bk_ = bufs[self.bi % 2]
                    self.bi += 1
                    for bl in range(4):
                        col = s_ * 8 + half * 4 + bl
                        S.op('pool', (lambda e, o=bf_[:, 2 * bl:2 * bl + 2, :].rearrange("p a n -> p (a n)"), ix=idx[:, col:col + 1]:
                                      e.indirect_dma_start(out=o, out_offset=None, in_=src,
                                                           in_offset=bass.IndirectOffsetOnAxis(ap=ix, axis=0))),
                             ['idx'], bk_, dma=True)
                    return bf_, bk_
                vbb = [wslot_take() for _ in range(2)]
                pzs, pzk = wslot_take()
                Pz = pzs[:].rearrange("p (pg hg c) -> p pg hg c", pg=16, hg=2)
                Pz4 = pzs[:].rearrange("p (pg h sl) -> p pg h sl", pg=16, h=H)
                self.memset('pool', pzs[:], 0.0, [pzk])
                Lt = self.sb("Lt", [128, 128], F32)
                Pf = self.sb("Pf", [128, 128], F32)
                gsb = self.sb("gsb", [128, 8, 8], F32)
                c1b = self.sb("c1b", [128, 8, 8], F32)
                t8s = self.sb("t8s", [128, H, 8], F32)
                den8 = self.sb("den8", [128, H], F32)
                ob0, ok0 = self.ps[6], ('ps', 6)
                ob1, ok1 = self.ps[7], ('ps', 7)
                obs = [(ob0, ok0), (ob1, ok1)]
                for s_ in range(16):
                    qb, qbk = nxt('f512', f512)
                    self.dma('sp', qb[:].rearrange("p (o n) -> p o n", o=1), qs_scr[s_:s_ + 1, :].partition_broadcast(128), ['qs_scr'], [qbk])
                    for half in range(2):
                        Kb, Kbk = gather(ck_d, s_, half)
                        self.tt('dve', Kb, Kb, qb[:].unsqueeze(1).to_broadcast([128, 8, 512]), ALU.mult, Kbk + [qbk], Kbk)
                        self.red(Lt[:, half * 64:(half + 1) * 64], Kb.rearrange("p g (h d) -> p (g h) d", h=H), ALU.add, Kbk, ['Lt'])
                    pbg, pkg = self.gbank()
                    self.mm(pbg[:, 0:128], onesf[:], Lt[:], True, True, ['onesf', 'Lt'], [pkg])
                    G4 = pbg[:, 0:128].rearrange("p (n two h) -> p n two h", n=8, two=2)
                    self.cp('dve', gsb[:], G4[:, :, 0, :], [pkg], ['gsb'])
                    self.tt('dve', gsb[:], gsb[:], G4[:, :, 1, :], ALU.add, [pkg, 'gsb'], ['gsb'])
                    for h in range(H):
                        S.op('dve', (lambda e, o=t8s[:, h, :], i=gsb[:, :, h]: e.max(out=o, in_=i)), ['gsb'], ['t8s'])
                    self.tt('dve', c1b[:], gsb[:], t8s[:, :, 2].unsqueeze(1).to_broadcast([128, 8, H]), ALU.is_ge, ['gsb', 't8s'], ['c1b'])
                    self.ts('dve', c1b[:], c1b[:], -NEG, NEG, ALU.mult, ALU.add, ['c1b'], ['c1b'])
                    L4 = Lt[:].rearrange("p (n two h) -> p n two h", n=8, two=2)
                    self.tt('dve', L4, L4, c1b[:].unsqueeze(2).to_broadcast([128, 8, 2, H]), ALU.add, ['Lt', 'c1b'], ['Lt'])
                    self.tt('dve', Lt[:, 112:128], Lt[:, 112:128], biasP[:], ALU.add, ['Lt', 'biasP'], ['Lt'])
                    self.act(Pz4[:, :, :, s_], Lt[:].rearrange("p (pg h) -> p pg h", pg=16), AF.Exp, ['Lt'], [pzk])
                    self.cp('dve', Pf[:].rearrange("p (pg h) -> p pg h", pg=16), Pz4[:, :, :, s_], [pzk], ['Pf'])
                    pbd, pkd = self.gbank()
                    self.mm(pbd[:, 0:128], onesf[:], Pf[:], True, True, ['onesf', 'Pf'], [pkd])
                    self.red(den8[:], pbd[:, 0:128].rearrange("p (pg h) -> p h pg", pg=16), ALU.add, [pkd], ['den8'])
                    self.stt(denAll, den8[0:16, :], identf[0:16, s_:s_ + 1], denAll, ALU.mult, ALU.add, ['den8', 'identf', 'small'], ['small'])
                    for half in range(2):
                        vb_, vbk_ = vbb[half]
                        vb3 = vb_[:].rearrange("p (g n) -> p g n", g=8)
                        Vb, Vbk = gather(cv_d, s_, half)
                        self.cp('act', vb3, Vb, Vbk, [vbk_])
                        for pg in range(8):
                            page = half * 8 + pg
                            for hg in range(2):
                                self.mm(obs[hg][0][:], Pz[:, page, hg, :], vb3[:, pg, :], (s_ == 0 and page == 0), (s_ == 15 and page == 15),
                                        [pzk, vbk_], [obs[hg][1]])
                    S.op('dve', (lambda e, o=Pz4[:, :, :, s_]: e.memset(o, 0.0)), (), [pzk])
                otok_t, otokk = nxt('f512', f512)
                otok = otok_t[0:16, :]
                for h in range(H):
                    hg, hl = h // 4, h % 4
                    self.cp('dve', otok[:, h * 64:(h + 1) * 64], obs[hg][0][32 * hl:32 * hl + 16, h * 64:(h + 1) * 64], [obs[hg][1]], [otokk])
                o3 = otok.rearrange("p (h d) -> p h d", h=H)
                tv, tvk = nxt('f512', f512)
                tv3 = tv[0:16, :].rearrange("p (h d) -> p h d", h=H)
                self.tt('dve', tv3, VS[0:16, :].rearrange("p (h d) -> p h d", h=H), lself.unsqueeze(2).to_broadcast([16, H, HD]), ALU.mult,
                        ['VS', 'small'], [tvk])
                self.tt('dve', o3, o3, tv3, ALU.add, [otokk, tvk], [otokk])
                self.tt('dve', dtot, denAll, lself, ALU.add, ['small'], ['small'])
                self.recip(dtot, dtot, ['small'], ['small'])
                self.tt('dve', o3, o3, dtot.unsqueeze(2).to_broadcast([16, H, HD]), ALU.mult, [otokk, 'small'], [otokk])
                pbo, pko = self.gbank()
                for cc in range(4):
                    self.tr(pbo[:, cc * 16:(cc + 1) * 16], otok[:, cc * 128:(cc + 1) * 128], identf[0:16, 0:16], [otokk, 'identf'], [pko])
                self.cp('dve', catT[:, 4:8, 0:16], pbo[:, 0:64].rearrange("p (c t) -> p c t", c=4), [pko],
                        [('catT', 4 + c_, k_) for c_ in range(4) for k_ in range(2)])
                out_proj(w_out_e, 1, lambda kc: [('catT', kc, 0), ('catT', kc, 1)])
                ffn_s(0)
                norm_T(xres[:, 0, :], ('xres', 0), gO[:], 'gO', 0)
                stcT = self.sb("stcT", [128, 8, 2, 16], F32)
                cuT = self.sb("cuT", [128, 8, 16], F32)
                for j in range(2):
                    for hh in range(2):
                        tl, tlk = nxt('f512', f512)
                        self.dma('sp', tl[0:16, :], stc_d[:, j, hh * 512:(hh + 1) * 512], (), [tlk])
                        pbt, pkt = self.gbank()
                        for c4 in range(4):
                            self.tr(pbt[:, c4 * 16:(c4 + 1) * 16], tl[0:16, c4 * 128:(c4 + 1) * 128], identf[0:16, 0:16], [tlk, 'identf'], [pkt])
                        self.cp('dve', stcT[:, hh * 4:(hh + 1) * 4, j, :], pbt[:, 0:64].rearrange("p (c t) -> p c t", c=4), [pkt], ['stcT'])
                for c in range(8):
                    if c % 4 == 0:
                        cslots = [self.wpiece([w_cols(w_in_o, k * 1024 + (c // 4) * 512, 512)]) for k in range(3)]
                    pbs_ = []
                    for k in range(3):
                        pb, pk = self.gbank()
                        sv = cslots[k][0][:].rearrange("p (c n) -> p c n", c=8)
                        for kc in range(8):
                            self.mm(pb[:, 0:16], sv[:, kc, (c % 4) * 128:(c % 4 + 1) * 128], hT[:, kc, 0:16], kc == 0, kc == 7,
                                    hk0 + [cslots[k][1]], [pk])
                        pbs_.append((pb, pk))
                    uu, uuk = nxt('f512', f512)
                    self.cp('act', uu[:, 0:16], pbs_[2][0][:, 0:16], [pbs_[2][1]], [uuk])
                    self.tt('dve', cuT[:, c, :], pbs_[1][0][:, 0:16], uu[:, 0:16], ALU.mult, [pbs_[1][1], uuk], ['cuT'])
                    cv, cvk = nxt('f512', f512)
                    self.act(cv[:, 0:16], cuT[:, c, :], AF.Identity, ['cuT', 'ccw'], [cvk], scale=ccw[:, c, 2:3])
                    self.stt(cv[:, 0:16], stcT[:, c, 1, :], ccw[:, c, 1:2], cv[:, 0:16], ALU.mult, ALU.add, ['stcT', cvk, 'ccw'], [cvk])
                    self.stt(cv[:, 0:16], stcT[:, c, 0, :], ccw[:, c, 0:1], cv[:, 0:16], ALU.mult, ALU.add, ['stcT', cvk, 'ccw'], [cvk])
                    self.tt('dve', catT[:, c, 0:16], pbs_[0][0][:, 0:16], cv[:, 0:16], ALU.mult, [pbs_[0][1], cvk],
                            [('catT', c, 0), ('catT', c, 1)])
                self.dma('act', ccs_out[:, 0, :], stc_d[:, 1, :], (), [])
                for hh in range(2):
                    pbc, pkc = self.gbank()
                    for c4 in range(4):
                        self.tr(pbc[0:16, c4 * 128:(c4 + 1) * 128], cuT[:, hh * 4 + c4, :], identf[:], ['cuT', 'identf'], [pkc])
                    ct, ctk = nxt('f512', f512)
                    self.cp('dve', ct[0:16, :], pbc[0:16, :], [pkc], [ctk])
                    self.dma('sp', ccs_out[:, 1, hh * 512:(hh + 1) * 512], ct[0:16, :], [ctk], [])
                out_proj(w_out_o, 1, lambda kc: [('catT', kc, 0), ('catT', kc, 1)])
                ffn_s(1)
                self.dma('sp', ys_out, xres[0:16, 0, :], [('xres', 0)], [])

            def ffn_s(l):
                norm_T(xres[:, 0, :], ('xres', 0), gF[:, l, :], 'gF', 0)
                hk0 = [('hT', 0)]
                stfT = self.stfT
                upT = self.upT
                for j in range(2):
                    for blk in range(11):
                        tl, tlk = nxt('f512', f512)
                        self.dma('sp', tl[0:16, :], stf_d[l, :, j, blk * 512:(blk + 1) * 512], (), [tlk])
                        pbt, pkt = self.gbank()
                        for c4 in range(4):
                            self.tr(pbt[:, c4 * 16:(c4 + 1) * 16], tl[0:16, c4 * 128:(c4 + 1) * 128], identf[0:16, 0:16], [tlk, 'identf'], [pkt])
                        self.cp('dve', stfT[:, blk * 4:(blk + 1) * 4, j, :], pbt[:, 0:64].rearrange("p (c t) -> p c t", c=4), [pkt], ['stfT'])
                for c4 in range((NPAIR + 3) // 4):
                    ncol = min(4, NPAIR - c4 * 4) * 128
                    uslots = [self.wpiece([w_cols(w_up[l], part * DFF + c4 * 512, ncol)]) for part in range(2)]
                    for cl in range(ncol // 128):
                        c = c4 * 4 + cl
                        res = []
                        for part in range(2):
                            ci = c + part * NPAIR
                            pb, pk = self.gbank()
                            for kc in range(8):
                                self.mm(pb[:, 0:16], uslots[part][0][:, 0:8 * ncol].rearrange("p (c n) -> p c n", c=8)[:, kc, cl * 128:(cl + 1) * 128],
                                        hT[:, kc, 0:16], kc == 0, kc == 7, hk0 + [uslots[part][1]], [pk])
                            self.cp('act', upT[:, ci, :], pb[:, 0:16], [pk], ['upT'])
                            cv, cvk = nxt('f512', f512)
                            self.act(cv[:, 0:16], upT[:, ci, :], AF.Identity, ['upT', 'cfw', 'cfb'], [cvk],
                                     scale=cfw[:, l, ci, 2:3], bias=cfb[:, l, ci:ci + 1])
                            self.stt(cv[:, 0:16], stfT[:, ci, 1, :], cfw[:, l, ci, 1:2], cv[:, 0:16], ALU.mult, ALU.add, ['stfT', cvk, 'cfw'], [cvk])
                            self.stt(cv[:, 0:16], stfT[:, ci, 0, :], cfw[:, l, ci, 0:1], cv[:, 0:16], ALU.mult, ALU.add, ['stfT', cvk, 'cfw'], [cvk])
                            res.append((cv, cvk))
                        (cg, cgk), (cu, cuk) = res
                        self.act(cg[:, 0:16], cg[:, 0:16], AF.Silu, [cgk], [cgk])
                        self.tt('dve', mT[:, c, 0:16], cg[:, 0:16], cu[:, 0:16], ALU.mult, [cgk, cuk], mkeys(c))
                self.dma('act', fs_out[l, :, 0, :], stf_d[l, :, 1, :], (), [])
                for blk in range(11):
                    pbc, pkc = self.gbank()
                    for c4 in range(4):
                        self.tr(pbc[0:16, c4 * 128:(c4 + 1) * 128], upT[:, blk * 4 + c4, :], identf[:], ['upT', 'identf'], [pkc])
                    ct, ctk = nxt('f512', f512)
                    self.cp('dve', ct[0:16, :], pbc[0:16, :], [pkc], [ctk])
                    self.dma('sp', fs_out[l, :, 1, blk * 512:(blk + 1) * 512], ct[0:16, :], [ctk], [])
                ffn_down(l, 1)

            if self.do_sample:
                self.stfT = self.sb("stfT", [128, 44, 2, 16], F32)
                self.upT = self.sb("upT", [128, 44, 16], F32)
                sample_phase()

            def fm_out(src_fn, nchunk, r, dst):
                c = 0
                while c < nchunk:
                    n = min(4, nchunk - c)
                    pb, pk = self.gbank()
                    for i in range(n):
                        ap_, keys = src_fn(c + i)
                        self.tr(pb[0:r, i * 128:(i + 1) * 128], ap_, identf[:], keys + ['identf'], [pk])
                    o, ok = nxt('f512', f512)
                    self.cp('dve', o[0:r, 0:n * 128], pb[0:r, 0:n * 128], [pk], [ok])
                    self.dma('sp', dst[:, c * 128:(c + n) * 128], o[0:r, 0:n * 128], [ok], [])
                    c += n
            fm_out(lambda c: (Ast[:, c, :], ['Ast']), 4, 30, ca_out)
            fm_out(lambda c: (Cst[:, c, :], ['Cst']), 8, 2, cc_out)
            for l in range(2):
                fm_out(lambda c, l=l: (Ust[:, l, c, :], ['Ust']), 44, 2, f_out[l])


            self.nops = len(S.ops)
            if os.environ.get('KSKIPOUT'):
                S.ops = [o for o in S.ops if not (o['dma'] and len(o['w']) == 0)]
            if os.environ.get('KSTOP'):
                S.ops = S.ops[:int(os.environ['KSTOP'])]
            S.emit(st)
            print(self.marks)
            print('nops', self.nops, 'emitted', len(S.ops), 'sem counts', S.max_counts, flush=True)
        return nc


def _bf(x):
    return x


def make_consts(C):
    NB = 4 * C // 256
    oh = np.zeros((33, 384), np.float32)
    for m in range(383):
        d = m - 127
        if d < 0:
            oh[32, m] = 1.0
        else:
            oh[int(t5_bucket_np(np.array([d]))[0]), m] = 1.0
    return oh


def core_inputs(inputs, c, C, SEQ, do_sample=True):
    b, j = c // 4, c % 4
    P = 3 * C
    NK = 4 * C
    NB = NK // 256
    x = inputs['x_prompt'][b]
    xk = np.zeros((NK, D), np.float32)
    lo = C * j - P
    src_lo = max(lo, 0)
    xk[src_lo - lo:, :] = x[src_lo:C * j + C]
    nvalid0 = (src_lo - lo) // 256
    gmask = np.zeros((NB, NB), np.float32)
    for own in range(NB):
        for n in range(NB):
            if n >= own or n < nvalid0:
                gmask[own, n] = -60000.0
    hv = np.array([[0.0 if j == 0 else 1.0]], np.float32)
    m = dict(xk=xk, gmask=gmask, hv=hv, oh=make_consts(C))
    for k in ('w_in_e', 'w_out_e', 'w_in_o', 'w_out_o', 'norm_mix_e', 'norm_mix_o', 'conv_a_w', 'conv_a_b', 'ln_a_g',
              'ln_a_b', 'q_norm_g', 'k_norm_g', 'conv_c_w'):
        m[k] = np.ascontiguousarray(inputs[k][0])
    for k in ('w_up', 'w_down', 'norm_ffn', 'conv_f_w', 'conv_f_b', 'rel_bias'):
        m[k] = np.ascontiguousarray(inputs[k])
    if do_sample:
        s0 = 16 * c
        m['xs'] = np.ascontiguousarray(inputs['x_sample'][s0:s0 + 16, 0, :])
        m['pt'] = np.ascontiguousarray(inputs['page_table'][s0:s0 + 16].reshape(1, 256).astype(np.int32))
        ck = inputs['cache_k'][0]
        m['cache_k'] = ck.reshape(ck.shape[0] * 64, 1024)
        cv = inputs['cache_v'][0]
        m['cache_v'] = cv.reshape(cv.shape[0] * 64, 1024)
        m['state_a'] = np.ascontiguousarray(inputs['state_conv_a'][0, s0:s0 + 16])
        m['state_c'] = np.ascontiguousarray(inputs['state_conv_c'][0, s0:s0 + 16])
        m['state_f'] = np.ascontiguousarray(inputs['state_ffn'][:, s0:s0 + 16])
        ohS = np.zeros((33, 2, 128), np.float32)
        for slot in range(2):
            for p in range(64, 128):
                kk = 2 * (p - 64) + slot
                ohS[int(t5_bucket_np(np.array([128 - kk]))[0]), slot, p] = 1.0
        ohS = ohS.reshape(33, 256)
        m['ohS'] = ohS
        oh16 = np.zeros((30, 16, 16), np.float32)
        for s_ in range(16):
            oh16[:, s_, s_] = 1.0
        m['oh16'] = oh16.reshape(30, 256)
    return m


_NC_CACHE = {}


def run_all(inputs, C, SEQ, do_sample=True):
    npool = inputs['cache_k'].shape[1] if do_sample else 2560
    key = (C, do_sample, npool)
    if key not in _NC_CACHE:
        bld = Builder(C, do_sample=do_sample, npool=npool)
        _NC_CACHE[key] = bld.build()
    nc = _NC_CACHE[key]
    in_maps = [core_inputs(inputs, c, C, SEQ, do_sample) for c in range(8)]
    res = run_bass_kernel_spmd(nc, in_maps, core_ids=list(range(8)))
    return res.results


def run_prompt(inputs, C, SEQ):
    return run_all(inputs, C, SEQ, do_sample=False)


def assemble(res, C, SEQ):
    y = np.zeros((2, SEQ, D), np.float32)
    k = np.zeros((2, SEQ, 512), np.float32)
    v = np.zeros((2, SEQ, 512), np.float32)
    for c in range(8):
        b, j = c // 4, c % 4
        y[b, C * j:C * j + C] = res[c]['y_out']
        k[b, C * j:C * j + C] = res[c]['k_out']
        v[b, C * j:C * j + C] = res[c]['v_out']
    npg = SEQ // 128
    k_prompt = k.reshape(1, 2, npg, 128, H, HD)
    v_prompt = v.reshape(1, 2, npg, 128, H, HD)
    a_p = np.stack([res[3]['ca_out'], res[7]['ca_out']])[None]
    c_p = np.stack([res[3]['cc_out'], res[7]['cc_out']])[None]
    f_p = np.stack([np.stack([res[3]['f_out'][l], res[7]['f_out'][l]]) for l in range(2)])
    ys = np.concatenate([res[c]['ys_out'] for c in range(8)], 0)[:, None, :]
    ks = np.concatenate([res[c]['ks_out'] for c in range(8)], 0).reshape(1, 128, 1, H, HD)
    vs = np.concatenate([res[c]['vs_out'] for c in range(8)], 0).reshape(1, 128, 1, H, HD)
    a_s = np.concatenate([res[c]['cas_out'] for c in range(8)], 0)[None]
    c_s = np.concatenate([res[c]['ccs_out'] for c in range(8)], 0)[None]
    f_s = np.concatenate([res[c]['fs_out'] for c in range(8)], 1)
    outs = (y, ys, k_prompt, v_prompt, a_p, c_p, f_p, ks, vs, a_s, c_s, f_s)
    return tuple(np.ascontiguousarray(o, dtype=np.float32) for o in outs)


def kernel(**inputs):
    inputs = {k: np.asarray(v) for k, v in inputs.items()}
    SEQ = inputs['x_prompt'].shape[1]
    C = SEQ // 4
    res = run_all(inputs, C, SEQ, do_sample=True)
    return assemble(res, C, SEQ)
```
